# Optimizing a Trainium2 kernel written in Bass

```python
import jax, jax.numpy as jnp
from jax import lax
import numpy as np

D_MODEL = 2048
BATCH = 4
SEQ = 2048
DEPTH = 1

GRID_W = 64
CTX_LEN = 256
HEAD_DIM = 128
N_Q_HEADS = 12
N_KV_HEADS = 4
Q_PER_KV = N_Q_HEADS // N_KV_HEADS
N_FOURIER_GROUPS = 4
FOURIER_GROUP_DIM = 128
ATTN_WIDTH = N_Q_HEADS * HEAD_DIM
KV_WIDTH = N_KV_HEADS * HEAD_DIM
FOURIER_WIDTH = N_FOURIER_GROUPS * FOURIER_GROUP_DIM
IN_WIDTH = ATTN_WIDTH + 2 * KV_WIDTH + FOURIER_WIDTH
MIX_WIDTH = ATTN_WIDTH + FOURIER_WIDTH
D_FF = 4 * D_MODEL
Q_BLOCK = 128
ROPE_THETA = 10000.0
ROPE_AXIS_DIM = HEAD_DIM // 2
N_MOD = 6
EPS = 1e-6

kernel_name = "hybrid_gqa_fourier_dit_prefix_layer"


def rmsnorm(x, g):
    xf = x.astype(jnp.float32)
    y = xf * lax.rsqrt(jnp.mean(xf * xf, axis=-1, keepdims=True) + EPS)
    return (y * g.astype(jnp.float32)).astype(x.dtype)


def modulate(h, shift, scale):
    return h * (1.0 + scale) + shift


def axial_rope_tables(rows):
    t = jnp.arange(rows * GRID_W)
    row = (t // GRID_W).astype(jnp.float32)
    col = (t % GRID_W).astype(jnp.float32)
    inv = ROPE_THETA ** (-jnp.arange(0, ROPE_AXIS_DIM, 2, dtype=jnp.float32) / ROPE_AXIS_DIM)
    ang_r = row[:, None, None] * inv
    ang_c = col[:, None, None] * inv
    return (jnp.cos(ang_r), jnp.sin(ang_r), jnp.cos(ang_c), jnp.sin(ang_c))


def rotate(x, cos, sin):
    half = x.shape[-1] // 2
    x1, x2 = x[..., :half], x[..., half:]
    return jnp.concatenate([x1 * cos - x2 * sin, x2 * cos + x1 * sin], axis=-1)


def apply_axial_rope(x, tabs):
    cr, sr, cc, sc = tabs
    xf = x.astype(jnp.float32)
    y = jnp.concatenate([rotate(xf[..., :ROPE_AXIS_DIM], cr, sr),
                         rotate(xf[..., ROPE_AXIS_DIM:], cc, sc)], axis=-1)
    return y.astype(x.dtype)


def heads(t, n):
    return t.reshape(t.shape[:-1] + (n, HEAD_DIM))


def gqa_scores_out(qblk, k_f, v_f):
    s = jnp.einsum('bqkgd,bnkd->bkgqn', qblk.astype(jnp.float32), k_f) * (HEAD_DIM ** -0.5)
    p = jax.nn.softmax(s, axis=-1)
    return jnp.einsum('bkgqn,bnkd->bqkgd', p, v_f)


def latent_attention(q, k, v, kc, vc):
    B, S = q.shape[0], q.shape[1]
    nblk = S // Q_BLOCK
    k_f = jnp.concatenate([kc, k], axis=1).astype(jnp.float32)
    v_f = jnp.concatenate([vc, v], axis=1).astype(jnp.float32)
    qb = q.reshape(B, nblk, Q_BLOCK, N_KV_HEADS, Q_PER_KV, HEAD_DIM)
    qb = jnp.moveaxis(qb, 1, 0)
    out = lax.map(lambda qblk: gqa_scores_out(qblk, k_f, v_f), qb)
    out = jnp.moveaxis(out, 0, 1).reshape(B, S, ATTN_WIDTH)
    return out.astype(q.dtype)


def context_attention(qc, kc, vc):
    B, L = qc.shape[0], qc.shape[1]
    qb = qc.reshape(B, L, N_KV_HEADS, Q_PER_KV, HEAD_DIM)
    o = gqa_scores_out(qb, kc.astype(jnp.float32), vc.astype(jnp.float32))
    return o.reshape(B, L, ATTN_WIDTH).astype(qc.dtype)


def fourier_mix(u, w_f):
    B, N = u.shape[0], u.shape[1]
    ug = u.reshape(B, N, N_FOURIER_GROUPS, FOURIER_GROUP_DIM).astype(jnp.float32)
    f = jnp.fft.fft2(ug, axes=(1, 3), norm='ortho').real
    y = jnp.einsum('bngc,gcd->bngd', f, w_f.astype(jnp.float32))
    return y.reshape(B, N, FOURIER_WIDTH).astype(u.dtype)


def split_proj(p):
    q = p[..., :ATTN_WIDTH]
    k = p[..., ATTN_WIDTH:ATTN_WIDTH + KV_WIDTH]
    v = p[..., ATTN_WIDTH + KV_WIDTH:ATTN_WIDTH + 2 * KV_WIDTH]
    u = p[..., ATTN_WIDTH + 2 * KV_WIDTH:]
    return q, k, v, u


def sq_relu_mlp(h, w1, w2):
    a = jax.nn.relu(h @ w1)
    return (a * a) @ w2


def hybrid_layer(x, ctx, mod_lat, mod_ctx, g1, w_in, q_g, k_g, w_f, w_out, g2, w1, w2, tabs, update_ctx):
    sh1, sc1, gt1, sh2, sc2, gt2 = jnp.split(mod_lat, N_MOD, axis=-1)
    csh1, csc1, cgt1, csh2, csc2, cgt2 = jnp.split(mod_ctx, N_MOD, axis=-1)

    hc = modulate(rmsnorm(ctx, g1), csh1, csc1)
    if update_ctx:
        qc, kc, vc, uc = split_proj(hc @ w_in)
    else:
        kv = hc @ w_in[:, ATTN_WIDTH:ATTN_WIDTH + 2 * KV_WIDTH]
        kc, vc = kv[..., :KV_WIDTH], kv[..., KV_WIDTH:]
    kc = rmsnorm(heads(kc, N_KV_HEADS), k_g)
    vc = heads(vc, N_KV_HEADS)

    h = modulate(rmsnorm(x, g1), sh1, sc1)
    q, k, v, u = split_proj(h @ w_in)
    q = apply_axial_rope(rmsnorm(heads(q, N_Q_HEADS), q_g), tabs)
    k = apply_axial_rope(rmsnorm(heads(k, N_KV_HEADS), k_g), tabs)
    v = heads(v, N_KV_HEADS)
    attn = latent_attention(q, k, v, kc, vc)
    four = fourier_mix(u, w_f)
    x = x + gt1 * (jnp.concatenate([attn, four], axis=-1) @ w_out)

    h2 = modulate(rmsnorm(x, g2), sh2, sc2)
    x = x + gt2 * sq_relu_mlp(h2, w1, w2)

    if update_ctx:
        qc = rmsnorm(heads(qc, N_Q_HEADS), q_g)
        attn_c = context_attention(qc, kc, vc)
        four_c = fourier_mix(uc, w_f)
        ctx = ctx + cgt1 * (jnp.concatenate([attn_c, four_c], axis=-1) @ w_out)
        hc2 = modulate(rmsnorm(ctx, g2), csh2, csc2)
        ctx = ctx + cgt2 * sq_relu_mlp(hc2, w1, w2)
    return x, ctx


def setup_inputs(seed: int = 0) -> dict:
    key = jax.random.key(seed)
    ks = jax.random.split(key, 16)
    f32 = jnp.float32
    n = lambda k, shape, s: jax.random.normal(k, shape, f32) * s
    return {
        "x": n(ks[0], (BATCH, SEQ, D_MODEL), 1.0),
        "c": n(ks[1], (BATCH, D_MODEL), 1.0),
        "ctx": n(ks[2], (BATCH, CTX_LEN, D_MODEL), 1.0),
        "c_ctx": n(ks[3], (D_MODEL,), 1.0),
        "w_ada": n(ks[4], (DEPTH, D_MODEL, N_MOD * D_MODEL), D_MODEL ** -0.5),
        "b_ada": n(ks[5], (DEPTH, N_MOD * D_MODEL), 0.02),
        "norm1_g": 1.0 + n(ks[6], (DEPTH, D_MODEL), 0.02),
        "w_in": n(ks[7], (DEPTH, D_MODEL, IN_WIDTH), D_MODEL ** -0.5),
        "q_norm_g": 1.0 + n(ks[8], (DEPTH, HEAD_DIM), 0.02),
        "k_norm_g": 1.0 + n(ks[9], (DEPTH, HEAD_DIM), 0.02),
        "w_fourier": n(ks[10], (DEPTH, N_FOURIER_GROUPS, FOURIER_GROUP_DIM, FOURIER_GROUP_DIM), FOURIER_GROUP_DIM ** -0.5),
        "w_out": n(ks[11], (DEPTH, MIX_WIDTH, D_MODEL), MIX_WIDTH ** -0.5),
        "norm2_g": 1.0 + n(ks[12], (DEPTH, D_MODEL), 0.02),
        "w_mlp1": n(ks[13], (DEPTH, D_MODEL, D_FF), D_MODEL ** -0.5),
        "w_mlp2": n(ks[14], (DEPTH, D_FF, D_MODEL), D_FF ** -0.5),
        "final_norm_g": 1.0 + n(ks[15], (D_MODEL,), 0.02),
    }


def reference(x, c, ctx, c_ctx, w_ada, b_ada, norm1_g, w_in, q_norm_g, k_norm_g, w_fourier,
              w_out, norm2_g, w_mlp1, w_mlp2, final_norm_g):
    ROWS = x.shape[1] // GRID_W
    tabs = axial_rope_tables(ROWS)
    silu_c = jax.nn.silu(c)
    silu_cc = jax.nn.silu(c_ctx)
    for layer in range(DEPTH):
        mod_lat = (silu_c @ w_ada[layer] + b_ada[layer])[:, None, :]
        mod_ctx = (silu_cc @ w_ada[layer] + b_ada[layer])[None, None, :]
        x, ctx = hybrid_layer(x, ctx, mod_lat, mod_ctx, norm1_g[layer], w_in[layer],
                              q_norm_g[layer], k_norm_g[layer], w_fourier[layer], w_out[layer],
                              norm2_g[layer], w_mlp1[layer], w_mlp2[layer], tabs,
                              update_ctx=(layer < DEPTH - 1))
    return rmsnorm(x, final_norm_g)
```

```python
import numpy as np
from contextlib import ExitStack
import concourse.bass as bass
import concourse.mybir as mybir
from concourse.bass_utils import run_bass_kernel_spmd

F32 = mybir.dt.float32
BF16 = mybir.dt.bfloat16
ALU = mybir.AluOpType
AF = mybir.ActivationFunctionType
AX = mybir.AxisListType

D = 2048
SEQ = 2048
BATCH = 4
CTX = 256
NOWN = 1024
HD = 128
NQH = 12
NKV = 4
DFF = 8192
EPS = 1e-6
NKEY = 2304
ENGS = ('pe', 'act', 'dve', 'pool', 'sp')


class Sched:
    def __init__(self):
        self.ops = []
        self.neng = {e: 0 for e in ENGS}
        self.lastw = {}
        self.readers = {}
        self.dma_cnt = {}

    def add(self, eng, fn, reads=(), writes=(), dma=None, ndma=1):
        idx = self.neng[eng]
        self.neng[eng] += 1
        deps = set()
        for k in reads:
            if k in self.lastw:
                deps.add(self.lastw[k])
        for k in writes:
            if k in self.lastw:
                deps.add(self.lastw[k])
            for ev in self.readers.get(k, ()):
                deps.add(ev)
        if dma is not None:
            self.dma_cnt[dma] = self.dma_cnt.get(dma, 0) + 16 * ndma
            ev = ('d', dma, self.dma_cnt[dma])
        else:
            ev = ('c', eng, idx)
        deps.discard(ev)
        for k in writes:
            self.lastw[k] = ev
            self.readers[k] = []
        for k in reads:
            if k not in writes:
                self.readers.setdefault(k, []).append(ev)
        self.ops.append(dict(eng=eng, idx=idx, fn=fn, deps=deps, ev=ev, dma=dma))
        return ev

    def plan(self):
        known = {e: {} for e in ENGS}
        snap = {}
        signal = set()
        for op in self.ops:
            F = op['eng']
            kn = known[F]
            waits = []
            for d in sorted(op['deps'], key=lambda t: (t[0], str(t[1]), t[2])):
                if d[0] == 'c':
                    E, i = d[1], d[2]
                    if E == 'pe' and F == 'pe':
                        continue
                    if kn.get(('c', E), -1) >= i:
                        continue
                    waits.append(d)
                else:
                    if kn.get(('d', d[1]), -1) >= d[2]:
                        continue
                    waits.append(d)
            for d in waits:
                if d[0] == 'c':
                    signal.add(d)
                    kn[('c', d[1])] = max(kn.get(('c', d[1]), -1), d[2])
                else:
                    kn[('d', d[1])] = max(kn.get(('d', d[1]), -1), d[2])
                for k, v in snap.get(d, {}).items():
                    if kn.get(k, -1) < v:
                        kn[k] = v
            op['waits'] = waits
            snap[op['ev']] = dict(kn)
        cnt = {e: 0 for e in ENGS}
        self.sigval = {}
        for op in self.ops:
            if op['dma'] is None and op['ev'] in signal:
                cnt[op['eng']] += 1
                self.sigval[op['ev']] = cnt[op['eng']]
                op['signal'] = True
            else:
                op['signal'] = False
        return cnt

    def emit(self, nc, stack):
        self.plan()
        esem = {e: stack.enter_context(nc.semaphore('s_' + e)) for e in ENGS}
        dsem = {}
        for k in self.dma_cnt:
            dsem[k] = stack.enter_context(nc.semaphore('d_' + str(k)))
        per = {e: [o for o in self.ops if o['eng'] == e] for e in ENGS}
        sigval = self.sigval

        def run(engname, eng):
            for op in per[engname]:
                for d in op['waits']:
                    if d[0] == 'c':
                        eng.wait_ge(esem[d[1]], sigval[d])
                    else:
                        eng.wait_ge(dsem[d[1]], d[2])
                if op['fn'] is None:
                    continue
                ins = op['fn'](eng)
                if op['dma'] is not None:
                    if not isinstance(ins, (list, tuple)):
                        ins = [ins]
                    for i_ in ins:
                        i_.then_inc(dsem[op['dma']], 16)
                elif op['signal']:
                    ins.then_inc(esem[engname], 1)

        with nc.Block() as block:
            @block.tensor
            def _(e):
                run('pe', e)

            @block.scalar
            def _(e):
                run('act', e)

            @block.vector
            def _(e):
                run('dve', e)

            @block.gpsimd
            def _(e):
                run('pool', e)

            @block.sync
            def _(e):
                run('sp', e)


KB = 1024
ARENA_BYTES = 206 * KB


def build_program(dbg=None, stop=None):
    nc = bass.Bass("TRN2", target_bir_lowering=False)
    S = Sched()

    def din(name, shape, dt=F32):
        return nc.dram_tensor(name, list(shape), dt, kind="ExternalInput").ap()

    x_own = din("x_own", [NOWN, D])
    x_oth = din("x_oth", [NOWN, D])
    ctx_in = din("ctx_b", [CTX, D])
    ccol = din("ccol", [128, 32])
    w_ada = din("w_ada", [D, 6 * D])
    g1col_d = din("g1col", [128, 16])
    g2col_d = din("g2col", [128, 16])
    w_in = din("w_in", [D, 3072])
    qg_d = din("qg_bc", [128, 128])
    kg_d = din("kg_bc", [128, 128])
    wf_d = din("w_f", [4, 128, 128])
    w_out = din("w_out", [D, D])
    w1 = din("w1", [D, DFF])
    w2 = din("w2", [DFF, D])
    fg_d = din("fg_bc", [128, D])
    ident_d = din("ident", [128, 128])
    sel_d = din("sel3", [3, 128])
    r3_d = din("r3", [3, 2])
    b_ada_d = din("b_ada", [1, 6 * D])
    cos_d = din("rope_cos", [SEQ, 128])
    sin_d = din("rope_sin", [SEQ, 128])
    cn_d = din("dft_cn", [SEQ, NOWN])
    sn_d = din("dft_sn", [SEQ, NOWN])
    cc_d = din("dft_cc", [128, 128])
    sc_d = din("dft_sc", [128, 128])
    out_d = nc.dram_tensor("out", [NOWN, D], F32, kind="ExternalOutput").ap()
    dbg_out = {}
    if dbg:
        for name, shape in dbg.items():
            dbg_out[name] = nc.dram_tensor("dbg_" + name, list(shape), F32, kind="ExternalOutput").ap()

    with ExitStack() as st:
        arena = st.enter_context(nc.sbuf_tensor("arena", [128, ARENA_BYTES // 4], F32))
        psum = st.enter_context(nc.psum_tensor("psum", [128, 4096], F32))

        def AV(off, nbytes, dt=F32):
            assert off % 4 == 0 and nbytes % 4 == 0 and off + nbytes <= ARENA_BYTES
            a = arena[:, off // 4:(off + nbytes) // 4]
            if dt == BF16:
                a = a.bitcast(BF16)
            return a

        def PS(bank):
            return psum[:, bank * 512:(bank + 1) * 512]

        def PSB(bank):
            return psum[:, bank * 512:(bank + 1) * 512].bitcast(BF16)

        o = 0
        identf = AV(o, 512); o += 512
        ident = AV(o, 256, BF16); o += 256
        ones_bf = AV(o, 256, BF16); o += 256
        qg_bc = AV(o, 512); o += 512
        kg_bc = AV(o, 512); o += 512
        g1col = AV(o, 64); o += 64
        g2col = AV(o, 64); o += 64
        ccol_t = AV(o, 128); o += 128
        sT = AV(o, 64, BF16); o += 64
        modcol = AV(o, 768).rearrange("p (m j r) -> p m j r", m=6, j=16); o += 768
        gs = AV(o, 192).rearrange("p (k j) -> p k j", k=3); o += 192
        stat = AV(o, 1408); o += 1408
        assert o <= 6 * KB
        sel3 = AV(o, 512)[0:3, :]; o += 512
        r3 = AV(o, 8)[0:3, :]; o += 8
        ones_f = AV(o, 512); o += 512
        modrow = [AV(184 * KB, 2 * KB)[0:3, :], AV(186 * KB, 2 * KB)[0:3, :]]

        WB = [AV(6 * KB, 16 * KB, BF16), AV(22 * KB, 16 * KB, BF16)]
        qT = AV(38 * KB, 24 * KB, BF16).rearrange("p (h n) -> p h n", h=NQH)
        kT = AV(62 * KB, 18 * KB, BF16).rearrange("p (h n) -> p h n", h=NKV)
        Vt = AV(80 * KB, 18 * KB, BF16).rearrange("p (c n) -> p c n", c=18)
        Ut = AV(98 * KB, 16 * KB, BF16).rearrange("p (c n) -> p c n", c=16)
        ACC = [AV(38 * KB + t * 8 * KB, 8 * KB) for t in range(8)]
        hT = AV(114 * KB, 32 * KB, BF16).rearrange("p (k n) -> p k n", k=16)
        mixT = hT
        h2T = hT
        XT = [AV(146 * KB, 8 * KB), AV(154 * KB, 8 * KB)]
        XN = [AV(162 * KB, 4 * KB, BF16), AV(166 * KB, 4 * KB, BF16)]
        ROPE = [AV(170 * KB + i * KB, KB).rearrange("p (a n) -> p a n", a=2) for i in range(4)]
        SCR = [AV(174 * KB + i * 2 * KB, 2 * KB) for i in range(8)]
        GT1 = AV(190 * KB, 8 * KB)
        GT2 = AV(198 * KB, 8 * KB)
        JUNK = AV(190 * KB, 4 * KB, BF16)
        FTAB = [AV(146 * KB + i * 2 * KB, 2 * KB, BF16).rearrange("p (a n) -> p a n", a=2) for i in range(6)]
        RDEN = [AV(162 * KB, 2 * KB), AV(164 * KB, 2 * KB)]
        ABT = AV(166 * KB, 8 * KB, BF16).rearrange("p (g n) -> p g n", g=8)
        PT = [AV(174 * KB + i * KB, KB, BF16) for i in range(4)]
        W1S = WB
        A2T = [AV(146 * KB, 16 * KB, BF16).rearrange("p (j n) -> p j n", j=8),
               AV(162 * KB, 16 * KB, BF16).rearrange("p (j n) -> p j n", j=8)]
        W2S = [AV(178 * KB, 8 * KB, BF16).rearrange("p (a n) -> p a n", a=2),
               AV(186 * KB, 8 * KB, BF16).rearrange("p (a n) -> p a n", a=2),
               AV(102 * KB, 8 * KB, BF16).rearrange("p (a n) -> p a n", a=2)]
        RSC = [AV(110 * KB, 2 * KB), AV(112 * KB, 2 * KB)]
        TMPG = [AV(194 * KB, 2 * KB), AV(196 * KB, 2 * KB)]
        FGB = AV(6 * KB, 8 * KB)

        statc = {3: [0, 0, 32], 12: [0, 96, 13], 1: [0, 252, 100]}

        def newstat(n=1):
            ent = statc[n]
            c = ent[1] + (ent[0] % ent[2]) * n
            ent[0] += 1
            assert c + n <= 352
            return c

        ACCK = lambda t: ['acc%d_%d' % (t, cb) for cb in range(4)]
        fence_id = [0]

        def fence(old_keys, eng='dve'):
            fence_id[0] += 1
            name = 'fence%d' % fence_id[0]
            c = newstat()
            S.add(eng, lambda e, c=c: e.memset(stat[:, c:c + 1], 0.0), writes=list(old_keys) + [name])
            return name

        def dma(eng, out, in_, key, reads=(), writes=(), noncontig=False):
            def fn(e, out=out, in_=in_):
                if noncontig:
                    with nc.allow_non_contiguous_dma(reason="layout"):
                        return e.dma_start(out=out, in_=in_)
                return e.dma_start(out=out, in_=in_)
            return S.add(eng, fn, reads=reads, writes=writes, dma=key)

        def mm(out, lhsT, rhs, start, stop, reads, writes):
            S.add('pe', lambda e, o_=out, l=lhsT, r=rhs, a=start, b=stop: e.matmul(o_, lhsT=l, rhs=r, start=a, stop=b),
                  reads=reads, writes=writes)

        def tr(out, in_, idt, reads, writes):
            S.add('pe', lambda e, o_=out, i_=in_, d_=idt: e.transpose(out=o_, in_=i_, identity=d_), reads=reads, writes=writes)

        def dbg_dump(name, ap, reads, eng='sp'):
            if name in dbg_out:
                dma(eng, dbg_out[name], ap, 'dbg_' + name, reads=reads, writes=['dbgo_' + name])

        def finish():
            S.add('sp', None, reads=['dbgo_' + n for n in dbg_out])
            S.emit(nc, st)
            return nc

        def dump(name, ap, reads):
            if name in dbg_out:
                dma('pool', dbg_out[name], ap, 'dbg_' + name, reads=reads, writes=['dbgo_' + name])

        dma('sp', identf, ident_d[:, :], 'c_identf', writes=['identf'])
        dma('pool', ident, ident_d[:, :], 'c_ident', writes=['ident'])
        dma('sp', sel3, sel_d[:, :], 'c_sel', writes=['sel'])
        dma('sp', r3, r3_d[:, :], 'c_r3', writes=['r3'])
        dma('sp', qg_bc, qg_d[:, :], 'c_qg', writes=['qg'])
        dma('sp', kg_bc, kg_d[:, :], 'c_kg', writes=['kg'])
        dma('sp', g1col, g1col_d[:, :], 'c_g1', writes=['g1col'])
        dma('sp', g2col, g2col_d[:, :], 'c_g2', writes=['g2col'])
        dma('sp', ccol_t, ccol[:, :], 'c_cc', writes=['ccol'])
        S.add('dve', lambda e: e.memset(ones_bf, 1.0), writes=['ones'])
        S.add('dve', lambda e: e.memset(ones_f, 1.0), writes=['onesf'])
        S.add('act', lambda e: e.activation(out=sT, in_=ccol_t, func=AF.Silu), reads=['ccol'], writes=['sT'])
        sT3 = sT.rearrange("p (k r) -> p k r", r=2)

        wb_i = [0]

        def wb_next():
            s = wb_i[0] % 2
            wb_i[0] += 1
            return s

        ada_i = [0]

        def ada_gen(j, psA, psB, extra=(), padded=False):
            m, jb = j // 4, j % 4
            s = wb_next()
            r = ada_i[0] % 2
            ada_i[0] += 1
            wv = WB[s].rearrange("p (k n) -> p k n", k=16)
            dma('pool', wv, w_ada[:, j * 512:(j + 1) * 512].rearrange("(k p) n -> p k n", p=128), 'wb%d' % s, writes=['wb%d' % s])
            dma('sp', modrow[r][2:3, :], b_ada_d[:, j * 512:(j + 1) * 512], 'brow%d' % r, reads=list(extra), writes=['brow%d' % r])
            yield
            for kc in range(16):
                if padded:
                    mm(PS(psA), STPAD[:, kc, :], wv[:, kc, :], kc == 0, kc == 15,
                       reads=['stpad', 'wb%d' % s], writes=['ps%d' % psA])
                else:
                    mm(PS(psA)[0:2, :], sT3[:, kc, :], wv[:, kc, :], kc == 0, kc == 15,
                       reads=['sT', 'wb%d' % s], writes=['ps%d' % psA])
                yield
            S.add('dve', lambda e, r=r: e.tensor_copy(out=modrow[r][0:2, :], in_=PS(psA)[0:2, :]),
                  reads=['ps%d' % psA] + list(extra), writes=['modrow%d' % r])
            if m in (0, 1, 3, 4):
                for q in range(4):
                    mm(PS(psB)[:, q * 2:(q + 1) * 2], modrow[r][:, q * 128:(q + 1) * 128], r3, True, True,
                       reads=['modrow%d' % r, 'brow%d' % r, 'r3'], writes=['ps%d' % psB])
                S.add('dve', lambda e, m=m, jb=jb: e.tensor_copy(
                    out=modcol[:, m, jb * 4:(jb + 1) * 4, :], in_=PS(psB)[:, 0:8].rearrange("p (q r) -> p q r", r=2)),
                    reads=['ps%d' % psB], writes=['modcol%d' % m])
            else:
                gt = GT1 if m == 2 else GT2
                mm(PS(psB), sel3, modrow[r], True, True, reads=['modrow%d' % r, 'brow%d' % r, 'sel'], writes=['ps%d' % psB])
                S.add('dve', lambda e, gt=gt, jb=jb: e.tensor_copy(out=gt[:, jb * 512:(jb + 1) * 512], in_=PS(psB)),
                      reads=['ps%d' % psB] + list(extra), writes=['gt%d_%d' % (m, jb)])
            yield

        def ada_block(j, psA, psB, extra=()):
            for _ in ada_gen(j, psA, psB, extra):
                pass

        for j in range(8):
            ada_block(j, 0, 1)
        S.add('dve', lambda e: e.scalar_tensor_tensor(out=gs[:, 0, :], in0=modcol[:, 1, :, 0], scalar=1.0, in1=g1col,
                                                      op0=ALU.add, op1=ALU.mult),
              reads=['modcol1', 'g1col'], writes=['gs0'])
        S.add('dve', lambda e: e.scalar_tensor_tensor(out=gs[:, 1, :], in0=modcol[:, 1, :, 1], scalar=1.0, in1=g1col,
                                                      op0=ALU.add, op1=ALU.mult),
              reads=['modcol1', 'g1col'], writes=['gs1'])

        if stop == 'A':
            dump('modcol', modcol.rearrange("p m j r -> p (m j r)"), ['modcol0', 'modcol1'])
            dump('gs', gs.rearrange("p k j -> p (k j)"), ['gs0', 'gs1'])
            return finish()

        xt_i = [0]
        tb_i = [0]
        rope_i = [0]
        scr_i = [0]
        pc_i = [0]
        pq_i = [0]
        ev_i = [0]

        def scr_next():
            s = scr_i[0] % 8
            scr_i[0] += 1
            return s

        def norm_gen(src, ntiles, gsk, gskey, shm, shr, dst, dstkey, cfg, srckeys=None, extra_reads=(), evac_extra_writes=None, tiles=None):
            for t in (tiles if tiles is not None else range(ntiles)):
                if srckeys is None:
                    xs = xt_i[0] % 2
                    xt_i[0] += 1
                    xt = XT[xs]
                    xkey = 'xt%d' % xs
                    dma('sp', xt, src[t * 128:(t + 1) * 128, :], xkey, writes=[xkey])
                    xkeys = [xkey]
                else:
                    xt = src[t]
                    xkeys = list(srckeys[t])
                c = newstat(3)
                S.add('act', lambda e, xt=xt, c=c: e.activation(out=cfg['junk'], in_=xt, func=AF.Square, accum_out=stat[:, c:c + 1]),
                      reads=xkeys + list(extra_reads), writes=['junk', 'st%d' % c])
                S.add('act', lambda e, c=c: e.activation(out=stat[:, c + 1:c + 2], in_=stat[:, c:c + 1], func=AF.Ln,
                                                         scale=1.0 / D, bias=EPS),
                      reads=['st%d' % c], writes=['st%d' % (c + 1)])
                S.add('act', lambda e, c=c: e.activation(out=stat[:, c + 2:c + 3], in_=stat[:, c + 1:c + 2], func=AF.Exp, scale=-0.5),
                      reads=['st%d' % (c + 1)], writes=['st%d' % (c + 2)])
                xs2 = t % 2
                XNc = cfg['xn']
                S.add('dve', lambda e, xt=xt, c=c, xs2=xs2, XNc=XNc: e.tensor_scalar(out=XNc[xs2], in0=xt, scalar1=stat[:, c + 2:c + 3],
                                                                           scalar2=None, op0=ALU.mult),
                      reads=xkeys + ['st%d' % (c + 2)] + list(extra_reads), writes=['xn%d' % xs2])
                yield
                for half in range(2):
                    bank = cfg['banks'][tb_i[0] % len(cfg['banks'])]
                    tb_i[0] += 1
                    for k8 in range(8):
                        kc = half * 8 + k8
                        tr(PSB(bank)[:, k8 * 128:(k8 + 1) * 128], XNc[xs2][:, kc * 128:(kc + 1) * 128], ident,
                           reads=['xn%d' % xs2, 'ident'], writes=['ps%d' % bank])
                    for k8 in range(8):
                        kc = half * 8 + k8
                        eng = 'dve' if half == 0 else 'act'
                        o_ = dst[:, kc, t * 128:(t + 1) * 128]
                        i_ = PSB(bank)[:, k8 * 128:(k8 + 1) * 128]
                        sc_ = gs[:, gsk, kc:kc + 1]
                        sh_ = modcol[:, shm, kc:kc + 1, shr]
                        if eng == 'dve':
                            S.add('dve', lambda e, o_=o_, i_=i_, sc_=sc_, sh_=sh_: e.tensor_scalar(
                                out=o_, in0=i_, scalar1=sc_, scalar2=sh_, op0=ALU.mult, op1=ALU.add),
                                reads=['ps%d' % bank, gskey, 'modcol%d' % shm] + list(extra_reads), writes=['%s_%d_%d' % (dstkey, t, kc)] + (evac_extra_writes(t) if evac_extra_writes else []))
                        else:
                            S.add('act', lambda e, o_=o_, i_=i_, sc_=sc_, sh_=sh_: e.activation(
                                out=o_, in_=i_, func=AF.Identity, scale=sc_, bias=sh_),
                                reads=['ps%d' % bank, gskey, 'modcol%d' % shm] + list(extra_reads), writes=['%s_%d_%d' % (dstkey, t, kc)] + (evac_extra_writes(t) if evac_extra_writes else []))
                yield

        def norm_set(*a, **kw):
            for _ in norm_gen(*a, **kw):
                pass

        cfgB = dict(junk=JUNK, xn=XN, banks=[2, 3, 4, 5])
        CBANKS = [6, 7]
        QBANKS = [0, 1]

        def evac_copy(out, in_, reads, writes):
            eng = 'act' if ev_i[0] % 2 == 0 else 'dve'
            ev_i[0] += 1
            if eng == 'act':
                S.add('act', lambda e, o_=out, i_=in_: e.copy(out=o_, in_=i_), reads=reads, writes=writes)
            else:
                S.add('dve', lambda e, o_=out, i_=in_: e.tensor_copy(out=o_, in_=i_), reads=reads, writes=writes)

        def qk_process(bank, is_q, use_rope, rope_src, tok0, dst, dstkey):
            pk = 'ps%d' % bank
            gbc = qg_bc if is_q else kg_bc
            gk = 'qg' if is_q else 'kg'
            s_sq, s_xn, s_a, s_b = scr_next(), scr_next(), scr_next(), scr_next()
            c = newstat(12)
            S.add('act', lambda e: e.activation(out=SCR[s_sq], in_=PS(bank), func=AF.Square), reads=[pk], writes=['scr%d' % s_sq])
            S.add('dve', lambda e: e.tensor_reduce(out=stat[:, c:c + 4], in_=SCR[s_sq].rearrange("p (h d) -> p h d", h=4),
                                                   axis=AX.X, op=ALU.add),
                  reads=['scr%d' % s_sq], writes=['st%d' % c])
            S.add('act', lambda e: e.activation(out=stat[:, c + 4:c + 8], in_=stat[:, c:c + 4], func=AF.Ln, scale=1.0 / HD, bias=EPS),
                  reads=['st%d' % c], writes=['st%d' % (c + 4)])
            S.add('act', lambda e: e.activation(out=stat[:, c + 8:c + 12], in_=stat[:, c + 4:c + 8], func=AF.Exp, scale=-0.5),
                  reads=['st%d' % (c + 4)], writes=['st%d' % (c + 8)])
            x3 = lambda ap: ap.rearrange("p (h d) -> p h d", h=4)
            S.add('dve', lambda e: e.tensor_tensor(out=x3(SCR[s_xn]), in0=x3(PS(bank)),
                                                   in1=stat[:, c + 8:c + 12].unsqueeze(2).to_broadcast([128, 4, 128]), op=ALU.mult),
                  reads=[pk, 'st%d' % (c + 8)], writes=['scr%d' % s_xn])
            if use_rope:
                S.add('dve', lambda e: e.tensor_tensor(out=x3(SCR[s_a]), in0=x3(SCR[s_xn]),
                                                       in1=gbc.unsqueeze(1).to_broadcast([128, 4, 128]), op=ALU.mult),
                      reads=['scr%d' % s_xn, gk], writes=['scr%d' % s_a])
                rs = rope_i[0] % 4
                rope_i[0] += 1
                rk = 'rope%d' % rs
                dma('sp', ROPE[rs][:, 0, :], cos_d[rope_src:rope_src + 128, :], rk, writes=[rk])
                dma('sp', ROPE[rs][:, 1, :], sin_d[rope_src:rope_src + 128, :], rk + 's', writes=[rk + 's'])
                x5 = lambda ap: ap.rearrange("p (h a f d) -> p h a f d", h=4, a=2, f=2)
                sin4 = ROPE[rs][:, 1, :].rearrange("p (a f d) -> p a f d", a=2, f=2)
                for f in range(2):
                    S.add('dve', lambda e, f=f: e.tensor_tensor(
                        out=x5(SCR[s_b])[:, :, :, f, :], in0=x5(SCR[s_a])[:, :, :, 1 - f, :],
                        in1=sin4[:, :, f, :].unsqueeze(1).to_broadcast([128, 4, 2, 32]), op=ALU.mult),
                        reads=['scr%d' % s_a, rk + 's'], writes=['scr%d' % s_b] if f == 0 else ['scr%d' % s_b, 'dummy_b'])
                S.add('dve', lambda e: e.tensor_tensor(out=x3(SCR[s_xn]), in0=x3(SCR[s_a]),
                                                       in1=ROPE[rs][:, 0, :].unsqueeze(1).to_broadcast([128, 4, 128]), op=ALU.mult),
                      reads=['scr%d' % s_a, rk], writes=['scr%d' % s_xn])
                xb = SCR[s_a].bitcast(BF16)[:, 0:512]
                S.add('dve', lambda e: e.tensor_tensor(out=xb, in0=SCR[s_xn], in1=SCR[s_b], op=ALU.add),
                      reads=['scr%d' % s_xn, 'scr%d' % s_b], writes=['scr%d' % s_a])
            else:
                xb = SCR[s_a].bitcast(BF16)[:, 0:512]
                S.add('dve', lambda e: e.tensor_tensor(out=xb.rearrange("p (h d) -> p h d", h=4), in0=x3(SCR[s_xn]),
                                                       in1=gbc.unsqueeze(1).to_broadcast([128, 4, 128]), op=ALU.mult),
                      reads=['scr%d' % s_xn, gk], writes=['scr%d' % s_a])
            def back():
                qb = QBANKS[pq_i[0] % 2]
                pq_i[0] += 1
                for h in range(4):
                    tr(PSB(qb)[:, h * 128:(h + 1) * 128], xb[:, h * 128:(h + 1) * 128], ident,
                       reads=['scr%d' % s_a, 'ident'], writes=['ps%d' % qb])
                evac_copy(dst, PSB(qb)[:, 0:512].rearrange("p (h n) -> p h n", h=4), reads=['ps%d' % qb], writes=[dstkey])
            return back

        PEND_DEPTH = 2
        HTB = [AV(114 * KB, 16 * KB, BF16).rearrange("p (k n) -> p k n", k=16),
               AV(130 * KB, 16 * KB, BF16).rearrange("p (k n) -> p k n", k=16)]

        def project_gen(ntiles, hb, blocks, key_base, rope_base):
            hbuf = HTB[hb]
            hkey = 'hT%d' % hb
            pending = []
            for cb in blocks:
                s = wb_next()
                wv = WB[s].rearrange("p (k n) -> p k n", k=16)
                dma('pool', wv, w_in[:, cb * 512:(cb + 1) * 512].rearrange("(k p) n -> p k n", p=128), 'wb%d' % s, writes=['wb%d' % s])
                for t in range(ntiles):
                    bank = CBANKS[pc_i[0] % 2]
                    pc_i[0] += 1
                    for kc in range(16):
                        mm(PS(bank), hbuf[:, kc, t * 128:(t + 1) * 128], wv[:, kc, :], kc == 0, kc == 15,
                           reads=['%s_%d_%d' % (hkey, t, kc), 'wb%d' % s], writes=['ps%d' % bank])
                    kchunk = key_base // 128 + t
                    while len(pending) > PEND_DEPTH - 1:
                        pending.pop(0)()
                    if cb < 3:
                        pending.append(qk_process(bank, True, True, rope_base + t * 128, None,
                                                  qT[:, cb * 4:(cb + 1) * 4, key_base + t * 128:key_base + (t + 1) * 128], 'qT'))
                    elif cb == 3:
                        pending.append(qk_process(bank, False, rope_base is not None, (rope_base + t * 128) if rope_base is not None else 0, None,
                                                  kT[:, :, key_base + t * 128:key_base + (t + 1) * 128], 'kT'))
                    elif cb == 4:
                        evac_copy(Vt[:, kchunk, :], PS(bank), reads=['ps%d' % bank], writes=['V'])
                    else:
                        evac_copy(Ut[:, kchunk, :], PS(bank), reads=['ps%d' % bank], writes=['U'])
                    yield
            while pending:
                pending.pop(0)()

        groups = [
            (x_oth[0:512, :], 4, 0, 'gs0', 0, [3, 4, 5], 1024, 1024),
            (x_oth[512:1024, :], 4, 0, 'gs0', 0, [3, 4, 5], 1536, 1536),
            (ctx_in, 2, 1, 'gs1', 1, [3, 4], 2048, None),
            (x_own[0:512, :], 4, 0, 'gs0', 0, [3, 4, 5, 0, 1, 2], 0, 0),
            (x_own[512:1024, :], 4, 0, 'gs0', 0, [3, 4, 5, 0, 1, 2], 512, 512),
        ]

        def mk_norm(gi):
            src, nt, gsk, gskey, shr, blocks, kb, rb = groups[gi]
            return norm_gen(src, nt, gsk, gskey, 0, shr, HTB[gi % 2], 'hT%d' % (gi % 2), cfgB)

        for _ in mk_norm(0):
            pass
        for gi, g in enumerate(groups):
            src, nt, gsk, gskey, shr, blocks, kb, rb = g
            pg = project_gen(nt, gi % 2, blocks, kb, rb)
            ng = mk_norm(gi + 1) if gi + 1 < len(groups) else None
            nunits = nt * len(blocks)
            nsteps = 2 * groups[gi + 1][1] if ng is not None else 0
            stride = max(1, nunits // (nsteps + 1)) if nsteps else 0
            for ui, _ in enumerate(pg):
                if ng is not None and ui >= 1 and (ui - 1) % stride == 0:
                    try:
                        next(ng)
                    except StopIteration:
                        ng = None
            if ng is not None:
                for _ in ng:
                    pass

        if stop == 'C':
            dump('qT', qT.rearrange("p h n -> p (h n)"), ['qT'])
            dump('kT', kT.rearrange("p h n -> p (h n)"), ['kT'])
            dump('V', Vt.rearrange("p c n -> p (c n)"), ['V'])
            dump('U', Ut.rearrange("p c n -> p (c n)"), ['U'])
            return finish()

        f0 = fence(['hT%d_%d_%d' % (hb, t, kc) for hb in range(2) for t in range(4) for kc in range(16)] + ['junk', 'xn0', 'xn1', 'xt0', 'xt1']
                   + ['scr%d' % i for i in range(8)] + ['rope%d' % i for i in range(4)] + ['rope%ds' % i for i in range(4)])
        wcs = AV(178 * KB, 2 * KB, BF16).rearrange("p (g d) -> p g d", g=8)
        cc_t = SCR[3][:, 0:128]
        sc_t = SCR[3][:, 128:256]
        dma('sp', cc_t, cc_d[:, :], 'c_ccm', reads=[f0], writes=['ccm'])
        dma('sp', sc_t, sc_d[:, :], 'c_scm', reads=[f0], writes=['scm'])
        wf_t = SCR[4].rearrange("p (g d) -> p g d", g=4)
        dma('sp', wf_t, wf_d.rearrange("g c d -> c g d"), 'c_wf', reads=[f0], writes=['wfm'])
        for g in range(4):
            mm(PS(6)[:, g * 128:(g + 1) * 128], cc_t, wf_t[:, g, :], True, True, reads=['ccm', 'wfm'], writes=['ps6'])
        for g in range(4):
            mm(PS(7)[:, g * 128:(g + 1) * 128], sc_t, wf_t[:, g, :], True, True, reads=['scm', 'wfm'], writes=['ps7'])
        S.add('dve', lambda e: e.tensor_copy(out=wcs[:, 0:4, :], in_=PS(6).rearrange("p (g d) -> p g d", g=4)), reads=['ps6', f0], writes=['wcs'])
        S.add('dve', lambda e: e.tensor_copy(out=wcs[:, 4:8, :], in_=PS(7).rearrange("p (g d) -> p g d", g=4)), reads=['ps7', f0], writes=['wcsb'])

        ada_todo = list(range(8, 24))
        SCALE = float(HD) ** -0.5
        units = [(h, qg) for h in range(NQH) for qg in range(2)]
        steps = [(ui, ch) for ui in range(len(units)) for ch in range(18)]
        NPT = 4
        cur_ada = [None]
        SBANKS = [0, 1, 6]
        DACC = [AV(188 * KB, 2 * KB), AV(182 * KB, 2 * KB)]
        STPAD = AV(158 * KB, 4 * KB, BF16).rearrange("p (k m) -> p k m", k=16)
        S.add('dve', lambda e: e.memset(STPAD, 0.0), reads=[f0], writes=['stpad'])
        S.add('dve', lambda e: e.tensor_copy(out=STPAD[:, :, 0:2], in_=sT3), reads=['sT'], writes=['stpad'])

        def ada_tick():
            if cur_ada[0] is None:
                if not ada_todo:
                    return
                cur_ada[0] = ada_gen(ada_todo.pop(0), 7, 7, extra=[f0], padded=True)
            try:
                next(cur_ada[0])
            except StopIteration:
                cur_ada[0] = None

        def emit_S(k):
            ui, ch = steps[k]
            h, qg = units[ui]
            kh = h // 3
            sbank = SBANKS[k % 3]
            mm(PS(sbank), kT[:, kh, ch * 128:(ch + 1) * 128], qT[:, h, qg * 512:(qg + 1) * 512], True, True,
               reads=['kT', 'qT'], writes=['ps%d' % sbank])
            p = k % NPT
            S.add('act', lambda e, p=p, sbank=sbank: e.activation(out=PT[p], in_=PS(sbank), func=AF.Exp, scale=SCALE),
                  reads=['ps%d' % sbank, f0], writes=['pt%d' % p])

        emit_S(0)
        emit_S(1)
        emit_S(2)
        for k, (ui, ch) in enumerate(steps):
            h, qg = units[ui]
            kh = h // 3
            p = k % NPT
            ob = 2 + (ui % 2) * 2
            mm(PS(ob), Vt[:, ch, kh * 128:(kh + 1) * 128], PT[p], ch == 0, ch == 17, reads=['V', 'pt%d' % p], writes=['ps%d' % ob])
            da = ui % 2
            if ch % 2 == 0:
                mm(PS(ob + 1), ones_bf, PT[p], ch == 0, False, reads=['ones', 'pt%d' % p], writes=['ps%d' % (ob + 1)])
            elif ch == 1:
                wr_ = ['dacc%d' % da] + (['wfm', 'scm', 'ccm'] if ui < 2 else [])
                S.add('dve', lambda e, da=da, p=p: e.tensor_copy(out=DACC[da], in_=PT[p]), reads=['pt%d' % p, f0], writes=wr_)
            else:
                S.add('dve', lambda e, da=da, p=p: e.tensor_tensor(out=DACC[da], in0=DACC[da], in1=PT[p], op=ALU.add),
                      reads=['pt%d' % p, 'dacc%d' % da], writes=['dacc%d' % da])
            if ch == 17:
                mm(PS(ob + 1), ones_f, DACC[da], False, True, reads=['onesf', 'dacc%d' % da], writes=['ps%d' % (ob + 1)])
            if k + 3 < len(steps):
                emit_S(k + 3)
            ada_tick()
            if ch == 17:
                rd = ui % 2
                S.add('dve', lambda e, rd=rd, ob=ob: e.reciprocal(out=RDEN[rd], in_=PS(ob + 1)), reads=['ps%d' % (ob + 1), f0], writes=['rden%d' % rd])
                S.add('dve', lambda e, rd=rd, h=h, qg=qg, ob=ob: e.tensor_tensor(out=mixT[:, h, qg * 512:(qg + 1) * 512], in0=PS(ob), in1=RDEN[rd],
                                                                               op=ALU.mult),
                      reads=['ps%d' % ob, 'rden%d' % rd, f0], writes=['mixT_%d' % t_ for t_ in range(qg * 4, qg * 4 + 4)])
        while ada_todo or cur_ada[0] is not None:
            ada_tick()
        S.add('dve', lambda e: e.scalar_tensor_tensor(out=gs[:, 2, :], in0=modcol[:, 4, :, 0], scalar=1.0, in1=g2col,
                                                      op0=ALU.add, op1=ALU.mult),
              reads=['modcol4', 'g2col'], writes=['gs2'])

        ft_i = [0]
        for hq in range(2):
            for ch in range(16):
                fs = ft_i[0] % 6
                ft_i[0] += 1
                fk = 'ftab%d' % fs
                dma('pool', FTAB[fs][:, 0, :], cn_d[ch * 128:(ch + 1) * 128, hq * 512:(hq + 1) * 512], fk, reads=[f0], writes=[fk])
                dma('pool', FTAB[fs][:, 1, :], sn_d[ch * 128:(ch + 1) * 128, hq * 512:(hq + 1) * 512], fk + 's', reads=[f0], writes=[fk + 's'])
                for g in range(4):
                    mm(PS(g), Ut[:, ch, g * 128:(g + 1) * 128], FTAB[fs][:, 0, :], ch == 0, ch == 15,
                       reads=['U', fk], writes=['ps%d' % g])
                    mm(PS(4 + g), Ut[:, ch, g * 128:(g + 1) * 128], FTAB[fs][:, 1, :], ch == 0, ch == 15,
                       reads=['U', fk + 's'], writes=['ps%d' % (4 + g)])
            for i in range(8):
                evac_copy(ABT[:, i, :], PS(i), reads=['ps%d' % i, f0], writes=['abt%d' % i])
            for g in range(4):
                mm(PS(g), wcs[:, g, :], ABT[:, g, :], True, False, reads=['wcs', 'wcsb', 'abt%d' % g], writes=['ps%d' % g])
                mm(PS(g), wcs[:, 4 + g, :], ABT[:, 4 + g, :], False, True, reads=['wcs', 'wcsb', 'abt%d' % (4 + g)], writes=['ps%d' % g])
                evac_copy(mixT[:, 12 + g, hq * 512:(hq + 1) * 512], PS(g), reads=['ps%d' % g], writes=['mixT_%d' % t_ for t_ in range(hq * 4, hq * 4 + 4)])

        if stop == 'D':
            dump('mixT', mixT.rearrange("p k n -> p (k n)"), ['mixT_%d' % t_ for t_ in range(8)])
            dump('gt1', GT1, ['gt2_%d' % i for i in range(4)])
            return finish()

        w2_slot = {}
        W2C = [[AV(178 * KB + j * 4 * KB, 4 * KB, BF16) for j in range(4)],
               [AV(102 * KB + j * 4 * KB, 4 * KB, BF16) for j in range(3)] + [AV(162 * KB, 4 * KB, BF16)]]
        def load_w2(b, deps):
            s = b % 2
            w2_slot[b] = s
            r0 = b * 512
            if s == 0:
                o_ = AV(178 * KB, 16 * KB, BF16).rearrange("p (a n) -> p a n", a=4)
                dma('pool', o_, w2[r0:r0 + 512, :].rearrange("(a p) n -> p a n", p=128), 'w2s0', reads=list(deps), writes=['w2s0'] + ['w2g0_%d' % j for j in range(4)])
            else:
                o_ = AV(102 * KB, 12 * KB, BF16).rearrange("p (a n) -> p a n", a=3)

                def fn(e, o_=o_, r0=r0):
                    i1 = e.dma_start(out=o_, in_=w2[r0:r0 + 384, :].rearrange("(a p) n -> p a n", p=128))
                    i2 = e.dma_start(out=W2C[1][3], in_=w2[r0 + 384:r0 + 512, :])
                    return [i1, i2]
                S.add('pool', fn, reads=list(deps), writes=['w2s1'] + ['w2g1_%d' % j for j in range(4)], dma='w2s1', ndma=2)
            for j in range(4):
                S.add('pool', lambda e, s=s, j=j: e.tensor_tensor(out=W2C[s][j], in0=W2C[s][j], in1=GT2, op=ALU.mult),
                      reads=['w2s%d' % s] + ['gt5_%d' % i for i in range(4)], writes=['w2g%d_%d' % (s, j)])

        f1 = fence(['qT', 'kT', 'V', 'U', 'wcs', 'wcsb', 'ccm', 'scm', 'wfm', 'rden0', 'rden1', 'modrow0', 'modrow1', 'brow0', 'brow1']
                   + ['pt%d' % i for i in range(4)] + ['abt%d' % i for i in range(8)]
                   + ['ftab%d' % i for i in range(6)] + ['ftab%ds' % i for i in range(6)] + ['dacc0', 'dacc1', 'stpad'])
        for t in range(8):
            dma('sp', ACC[t], x_own[t * 128:(t + 1) * 128, :], 'acc%d' % t, reads=[f1], writes=ACCK(t))
        pe_i = [0]
        tm_i = [0]
        ng2 = {}
        TMPE = [AV(102 * KB + i * 2 * KB, 2 * KB) for i in range(4)]
        cfgF = dict(junk=AV(170 * KB, 4 * KB, BF16), xn=[AV(162 * KB, 4 * KB, BF16), AV(166 * KB, 4 * KB, BF16)], banks=[4, 5, 6, 7])
        for cb in range(4):
            s = wb_next()
            wv = WB[s].rearrange("p (k n) -> p k n", k=16)
            dma('pool', wv, w_out[:, cb * 512:(cb + 1) * 512].rearrange("(k p) n -> p k n", p=128), 'wb%d' % s, writes=['wb%d' % s])
            for t in range(8):
                bank = pe_i[0] % 3
                pe_i[0] += 1
                for mc in range(16):
                    mm(PS(bank), mixT[:, mc, t * 128:(t + 1) * 128], wv[:, mc, :], mc == 0, mc == 15,
                       reads=['mixT_%d' % t, 'wb%d' % s], writes=['ps%d' % bank])
                tm = tm_i[0] % 4
                tm_i[0] += 1
                S.add('dve', lambda e, tm=tm, bank=bank, cb=cb: e.tensor_tensor(out=TMPE[tm], in0=PS(bank), in1=GT1[:, cb * 512:(cb + 1) * 512],
                                                                                 op=ALU.mult),
                      reads=['ps%d' % bank, 'gt2_%d' % cb, f1], writes=['tmpE%d' % tm])
                S.add('dve', lambda e, tm=tm, t=t, cb=cb: e.tensor_tensor(out=ACC[t][:, cb * 512:(cb + 1) * 512],
                                                                          in0=ACC[t][:, cb * 512:(cb + 1) * 512], in1=TMPE[tm], op=ALU.add),
                      reads=['tmpE%d' % tm, 'acc%d_%d' % (t, cb)], writes=['acc%d_%d' % (t, cb)])
                if cb == 3:
                    if t >= 2:
                        next(ng2[t - 2])
                    ng2[t] = norm_gen(ACC, 8, 2, 'gs2', 3, 0, h2T, 'hT', cfgF, srckeys=[ACCK(t_) for t_ in range(8)],
                                      extra_reads=[f1], evac_extra_writes=lambda t_: ['mixT_%d' % t_], tiles=[t])
                    next(ng2[t])
            if cb == 1:
                fmid = fence(['gt2_0', 'gt2_1'])
                load_w2(0, [f1, fmid])
        for t in (6, 7):
            next(ng2[t])

        if stop == 'E':
            for t in range(8):
                dump('acc%d' % t, ACC[t], ACCK(t))
            return finish()

        f2 = fence(['mixT_%d' % t_ for t_ in range(8)] + ['tmpE%d' % i for i in range(4)] + ['gt2_%d' % i for i in range(4)])
        if stop == 'F':
            dump('h2T', h2T.rearrange("p k n -> p (k n)"), ['hT_%d_%d' % (t, kc) for t in range(8) for kc in range(16)])
            return finish()

        f3 = fence(['junk', 'xn0', 'xn1'])
        NB = 16
        w1_slot = {}
        A2T = [AV(146 * KB, 8 * KB, BF16).rearrange("p (j n) -> p j n", j=4),
               AV(154 * KB, 8 * KB, BF16).rearrange("p (j n) -> p j n", j=4)]
        RSC = [AV(170 * KB, 2 * KB), AV(172 * KB, 2 * KB), AV(174 * KB, 2 * KB), AV(176 * KB, 2 * KB)]
        TMPG = [AV(194 * KB, 2 * KB), AV(196 * KB, 2 * KB)]

        def load_w1(b):
            s = wb_next()
            w1_slot[b] = s
            wv = WB[s].rearrange("p (k n) -> p k n", k=16)
            dma('pool', wv, w1[:, b * 512:(b + 1) * 512].rearrange("(k p) n -> p k n", p=128), 'wb%d' % s, writes=['wb%d' % s])

        am_i = [0]
        rs_i = [0]

        def stageA(b, j):
            s = w1_slot[b]
            wv = WB[s].rearrange("p (k n) -> p k n", k=16)
            bb = b % 2
            banks = [(am_i[0] % 2) * 2, (am_i[0] % 2) * 2 + 1]
            am_i[0] += 1
            for kc in range(16):
                for tg in range(2):
                    mm(PS(banks[tg]), wv[:, kc, j * 128:(j + 1) * 128], h2T[:, kc, tg * 512:(tg + 1) * 512], kc == 0, kc == 15,
                       reads=['wb%d' % s] + ['hT_%d_%d' % (tg * 4 + i, kc) for i in range(4)], writes=['ps%d' % banks[tg]])
            for tg in range(2):
                r = rs_i[0] % 4
                rs_i[0] += 1
                S.add('act', lambda e, r=r, bk=banks[tg]: e.activation(out=RSC[r], in_=PS(bk), func=AF.Relu),
                      reads=['ps%d' % banks[tg], f3], writes=['rsc%d' % r])
                S.add('act', lambda e, r=r, bb=bb, j=j, tg=tg: e.activation(out=A2T[bb][:, j, tg * 512:(tg + 1) * 512], in_=RSC[r], func=AF.Square),
                      reads=['rsc%d' % r, f3], writes=['a2t%d_%d' % (bb, j)])

        bn_i = [0]
        tg_i = [0]

        def stageB(b, t, cp):
            s = w2_slot[b]
            bb = b % 2
            banks = [4 + (bn_i[0] % 2) * 2, 5 + (bn_i[0] % 2) * 2]
            bn_i[0] += 1
            for j in range(4):
                for ci in range(2):
                    cb = cp * 2 + ci
                    mm(PS(banks[ci]), A2T[bb][:, j, t * 128:(t + 1) * 128], W2C[s][j][:, cb * 512:(cb + 1) * 512], j == 0, j == 3,
                       reads=['a2t%d_%d' % (bb, j), 'w2g%d_%d' % (s, j)], writes=['ps%d' % banks[ci]])
            for ci in range(2):
                cb = cp * 2 + ci
                S.add('dve', lambda e, bk=banks[ci], t=t, cb=cb: e.tensor_tensor(out=ACC[t][:, cb * 512:(cb + 1) * 512], in0=PS(bk),
                                                                                in1=ACC[t][:, cb * 512:(cb + 1) * 512], op=ALU.add),
                      reads=['ps%d' % banks[ci], 'acc%d_%d' % (t, cb), f2], writes=['acc%d_%d' % (t, cb)])

        ADD_ENG = 'pool'
        load_w1(0)
        load_w1(1)
        load_w2(1, [f1, f2, f3])
        for j in range(4):
            stageA(0, j)
        for b in range(NB):
            bunits = [(t, cp) for t in range(8) for cp in range(2)]
            aunits = list(range(4)) if b + 1 < NB else []
            for i, (t, cp) in enumerate(bunits):
                if i % 4 == 0 and aunits:
                    stageA(b + 1, aunits.pop(0))
                stageB(b, t, cp)
            if b + 2 < NB:
                load_w1(b + 2)
                load_w2(b + 2, [f1, f2, f3])

        if stop == 'G':
            for t in range(8):
                dump('acc%d' % t, ACC[t], ACCK(t))
            return finish()

        f4 = fence(['hT_%d_%d' % (t, kc) for t in range(8) for kc in range(16)] + ['wb0', 'wb1'])
        dma('sp', FGB, fg_d[:, :], 'c_fg', reads=[f4], writes=['fgb'])
        JH = AV(114 * KB, 4 * KB, BF16)
        for t in range(8):
            c = newstat(3)
            S.add('act', lambda e, t=t, c=c: e.activation(out=JH, in_=ACC[t], func=AF.Square, accum_out=stat[:, c:c + 1]),
                  reads=ACCK(t) + [f4], writes=['junkH', 'st%d' % c])
            S.add('act', lambda e, c=c: e.activation(out=stat[:, c + 1:c + 2], in_=stat[:, c:c + 1], func=AF.Ln, scale=1.0 / D, bias=EPS),
                  reads=['st%d' % c], writes=['st%d' % (c + 1)])
            S.add('act', lambda e, c=c: e.activation(out=stat[:, c + 2:c + 3], in_=stat[:, c + 1:c + 2], func=AF.Exp, scale=-0.5),
                  reads=['st%d' % (c + 1)], writes=['st%d' % (c + 2)])
            S.add('dve', lambda e, t=t, c=c: e.scalar_tensor_tensor(out=ACC[t], in0=ACC[t], scalar=stat[:, c + 2:c + 3], in1=FGB,
                                                                   op0=ALU.mult, op1=ALU.mult),
                  reads=ACCK(t) + ['st%d' % (c + 2), 'fgb'], writes=ACCK(t))
            dma('sp', out_d[t * 128:(t + 1) * 128, :], ACC[t], 'out%d' % t, reads=ACCK(t), writes=['outd%d' % t])
        S.add('sp', None, reads=['outd%d' % t for t in range(8)] + ['dbgo_' + n for n in dbg_out])
        S.emit(nc, st)
    return nc


_CONST_CACHE = {}


def _consts():
    if _CONST_CACHE:
        return _CONST_CACHE
    f = np.float32
    t = np.arange(SEQ)
    row = (t // 64).astype(np.float64)
    col = (t % 64).astype(np.float64)
    inv = 10000.0 ** (-np.arange(0, 64, 2, dtype=np.float64) / 64.0)
    ar = row[:, None] * inv[None, :]
    ac = col[:, None] * inv[None, :]
    cos = np.concatenate([np.cos(ar), np.cos(ar), np.cos(ac), np.cos(ac)], axis=1).astype(f)
    sin = np.concatenate([-np.sin(ar), np.sin(ar), -np.sin(ac), np.sin(ac)], axis=1).astype(f)
    n = np.arange(SEQ, dtype=np.int64)
    ang = 2.0 * np.pi * ((n[:, None] * n[None, :]) % SEQ).astype(np.float64) / SEQ
    cn = np.cos(ang).astype(f)
    sn = np.sin(ang).astype(f)
    c = np.arange(128, dtype=np.int64)
    angc = 2.0 * np.pi * ((c[:, None] * c[None, :]) % 128).astype(np.float64) / 128
    cc = (np.cos(angc) / 512.0).astype(f)
    sc = (-np.sin(angc) / 512.0).astype(f)
    sel3 = np.zeros((3, 128), f)
    sel3[0] = 1.0
    sel3[2] = 1.0
    r3 = np.array([[1, 0], [0, 1], [1, 1]], f)
    _CONST_CACHE.update(cos=cos, sin=sin, cn=cn, sn=sn, cc=cc, sc=sc, sel3=sel3, r3=r3, ident=np.eye(128, dtype=f))
    return _CONST_CACHE


_PROG = {}


def _in_maps(x, c, ctx, c_ctx, w_ada, b_ada, norm1_g, w_in, q_norm_g, k_norm_g, w_fourier,
             w_out, norm2_g, w_mlp1, w_mlp2, final_norm_g):
    K = _consts()
    f = np.float32
    A = lambda a: np.ascontiguousarray(np.asarray(a, dtype=f))
    x, c, ctx, c_ctx = A(x), A(c), A(ctx), A(c_ctx)
    shared = dict(
        w_ada=A(w_ada[0]), b_ada=A(b_ada[0]).reshape(1, -1),
        g1col=A(np.asarray(norm1_g[0]).reshape(16, 128).T), g2col=A(np.asarray(norm2_g[0]).reshape(16, 128).T),
        w_in=A(w_in[0]), qg_bc=A(np.broadcast_to(np.asarray(q_norm_g[0])[None, :], (128, 128))),
        kg_bc=A(np.broadcast_to(np.asarray(k_norm_g[0])[None, :], (128, 128))),
        w_f=A(w_fourier[0]), w_out=A(w_out[0]), w1=A(w_mlp1[0]), w2=A(w_mlp2[0]),
        fg_bc=A(np.broadcast_to(np.asarray(final_norm_g)[None, :], (128, D))),
        ident=K['ident'], sel3=K['sel3'], r3=K['r3'], dft_cc=K['cc'], dft_sc=K['sc'],
    )
    maps = []
    for core in range(8):
        b, half = core // 2, core % 2
        own = slice(half * NOWN, (half + 1) * NOWN)
        oth = slice((1 - half) * NOWN, (2 - half) * NOWN)
        order = np.concatenate([np.arange(own.start, own.stop), np.arange(oth.start, oth.stop)])
        cpair = np.stack([c[b], c_ctx], axis=0)
        ccol = A(cpair.reshape(2, 16, 128).transpose(2, 1, 0).reshape(128, 32))
        m = dict(shared)
        m.update(
            x_own=A(x[b, own]), x_oth=A(x[b, oth]), ctx_b=A(ctx[b]), ccol=ccol,
            rope_cos=A(K['cos'][order]), rope_sin=A(K['sin'][order]),
            dft_cn=A(K['cn'][order][:, own]), dft_sn=A(K['sn'][order][:, own]),
        )
        maps.append(m)
    return maps


def kernel(**inputs):
    if 'nc' not in _PROG:
        _PROG['nc'] = build_program()
    nc = _PROG['nc']
    maps = _in_maps(**inputs)
    res = run_bass_kernel_spmd(nc, maps, core_ids=list(range(8)))
    out = np.zeros((BATCH, SEQ, D), np.float32)
    for core in range(8):
        b, half = core // 2, core % 2
        out[b, half * NOWN:(half + 1) * NOWN] = res.results[core]["out"]
    return out
```

```python
import numpy as np
from contextlib import ExitStack
import concourse.bass as bass
import concourse.mybir as mybir
from concourse.bass_utils import run_bass_kernel_spmd

F32 = mybir.dt.float32
BF16 = mybir.dt.bfloat16
ALU = mybir.AluOpType
AF = mybir.ActivationFunctionType
AX = mybir.AxisListType

D = 2048
SEQ = 2048
BATCH = 4
CTX = 256
NOWN = 1024
HD = 128
NQH = 12
NKV = 4
DFF = 8192
EPS = 1e-6
NKEY = 2304
ENGS = ('pe', 'act', 'dve', 'pool', 'sp')


class Sched:
    def __init__(self):
        self.ops = []
        self.neng = {e: 0 for e in ENGS}
        self.lastw = {}
        self.readers = {}
        self.dma_cnt = {}

    def add(self, eng, fn, reads=(), writes=(), dma=None, ndma=1):
        idx = self.neng[eng]
        self.neng[eng] += 1
        deps = set()
        for k in reads:
            if k in self.lastw:
                deps.add(self.lastw[k])
        for k in writes:
            if k in self.lastw:
                deps.add(self.lastw[k])
            for ev in self.readers.get(k, ()):
                deps.add(ev)
        if dma is not None:
            self.dma_cnt[dma] = self.dma_cnt.get(dma, 0) + 16 * ndma
            ev = ('d', dma, self.dma_cnt[dma])
        else:
            ev = ('c', eng, idx)
        deps.discard(ev)
        for k in writes:
            self.lastw[k] = ev
            self.readers[k] = []
        for k in reads:
            if k not in writes:
                self.readers.setdefault(k, []).append(ev)
        self.ops.append(dict(eng=eng, idx=idx, fn=fn, deps=deps, ev=ev, dma=dma))
        return ev

    def plan(self):
        known = {e: {} for e in ENGS}
        snap = {}
        signal = set()
        for op in self.ops:
            F = op['eng']
            kn = known[F]
            waits = []
            for d in sorted(op['deps'], key=lambda t: (t[0], str(t[1]), t[2])):
                if d[0] == 'c':
                    E, i = d[1], d[2]
                    if E == 'pe' and F == 'pe':
                        continue
                    if kn.get(('c', E), -1) >= i:
                        continue
                    waits.append(d)
                else:
                    if kn.get(('d', d[1]), -1) >= d[2]:
                        continue
                    waits.append(d)
            for d in waits:
                if d[0] == 'c':
                    signal.add(d)
                    kn[('c', d[1])] = max(kn.get(('c', d[1]), -1), d[2])
                else:
                    kn[('d', d[1])] = max(kn.get(('d', d[1]), -1), d[2])
                for k, v in snap.get(d, {}).items():
                    if kn.get(k, -1) < v:
                        kn[k] = v
            op['waits'] = waits
            snap[op['ev']] = dict(kn)
        cnt = {e: 0 for e in ENGS}
        self.sigval = {}
        for op in self.ops:
            if op['dma'] is None and op['ev'] in signal:
                cnt[op['eng']] += 1
                self.sigval[op['ev']] = cnt[op['eng']]
                op['signal'] = True
            else:
                op['signal'] = False
        return cnt

    def emit(self, nc, stack):
        self.plan()
        esem = {e: stack.enter_context(nc.semaphore('s_' + e)) for e in ENGS}
        dsem = {}
        for k in self.dma_cnt:
            dsem[k] = stack.enter_context(nc.semaphore('d_' + str(k)))
        per = {e: [o for o in self.ops if o['eng'] == e] for e in ENGS}
        sigval = self.sigval

        def run(engname, eng):
            for op in per[engname]:
                for d in op['waits']:
                    if d[0] == 'c':
                        eng.wait_ge(esem[d[1]], sigval[d])
                    else:
                        eng.wait_ge(dsem[d[1]], d[2])
                if op['fn'] is None:
                    continue
                ins = op['fn'](eng)
                if op['dma'] is not None:
                    if not isinstance(ins, (list, tuple)):
                        ins = [ins]
                    for i_ in ins:
                        i_.then_inc(dsem[op['dma']], 16)
                elif op['signal']:
                    ins.then_inc(esem[engname], 1)

        with nc.Block() as block:
            @block.tensor
            def _(e):
                run('pe', e)

            @block.scalar
            def _(e):
                run('act', e)

            @block.vector
            def _(e):
                run('dve', e)

            @block.gpsimd
            def _(e):
                run('pool', e)

            @block.sync
            def _(e):
                run('sp', e)


KB = 1024
ARENA_BYTES = 206 * KB


def build_program(dbg=None, stop=None):
    nc = bass.Bass("TRN2", target_bir_lowering=False)
    S = Sched()

    def din(name, shape, dt=F32):
        return nc.dram_tensor(name, list(shape), dt, kind="ExternalInput").ap()

    x_own = din("x_own", [NOWN, D])
    x_oth = din("x_oth", [NOWN, D])
    ctx_in = din("ctx_b", [CTX, D])
    ccol = din("ccol", [128, 32])
    w_ada = din("w_ada", [D, 6 * D])
    g1col_d = din("g1col", [128, 16])
    g2col_d = din("g2col", [128, 16])
    w_in = din("w_in", [D, 3072])
    qg_d = din("qg_bc", [128, 128])
    kg_d = din("kg_bc", [128, 128])
    wf_d = din("w_f", [4, 128, 128])
    w_out = din("w_out", [D, D])
    w1 = din("w1", [D, DFF])
    w2 = din("w2", [DFF, D])
    fg_d = din("fg_bc", [128, D])
    ident_d = din("ident", [128, 128])
    sel_d = din("sel3", [3, 128])
    r3_d = din("r3", [3, 2])
    b_ada_d = din("b_ada", [1, 6 * D])
    cos_d = din("rope_cos", [SEQ, 128])
    sin_d = din("rope_sin", [SEQ, 128])
    cn_d = din("dft_cn", [SEQ, NOWN])
    sn_d = din("dft_sn", [SEQ, NOWN])
    cc_d = din("dft_cc", [128, 128])
    sc_d = din("dft_sc", [128, 128])
    out_d = nc.dram_tensor("out", [NOWN, D], F32, kind="ExternalOutput").ap()
    dbg_out = {}
    if dbg:
        for name, shape in dbg.items():
            dbg_out[name] = nc.dram_tensor("dbg_" + name, list(shape), F32, kind="ExternalOutput").ap()

    with ExitStack() as st:
        arena = st.enter_context(nc.sbuf_tensor("arena", [128, ARENA_BYTES // 4], F32))
        psum = st.enter_context(nc.psum_tensor("psum", [128, 4096], F32))

        def AV(off, nbytes, dt=F32):
            assert off % 4 == 0 and nbytes % 4 == 0 and off + nbytes <= ARENA_BYTES
            a = arena[:, off // 4:(off + nbytes) // 4]
            if dt == BF16:
                a = a.bitcast(BF16)
            return a

        def PS(bank):
            return psum[:, bank * 512:(bank + 1) * 512]

        def PSB(bank):
            return psum[:, bank * 512:(bank + 1) * 512].bitcast(BF16)

        o = 0
        identf = AV(o, 512); o += 512
        ident = AV(o, 256, BF16); o += 256
        ones_bf = AV(o, 256, BF16); o += 256
        qg_bc = AV(o, 512); o += 512
        kg_bc = AV(o, 512); o += 512
        g1col = AV(o, 64); o += 64
        g2col = AV(o, 64); o += 64
        ccol_t = AV(o, 128); o += 128
        sT = AV(o, 64, BF16); o += 64
        modcol = AV(o, 768).rearrange("p (m j r) -> p m j r", m=6, j=16); o += 768
        gs = AV(o, 192).rearrange("p (k j) -> p k j", k=3); o += 192
        stat = AV(o, 1408); o += 1408
        assert o <= 6 * KB
        sel3 = AV(o, 512)[0:3, :]; o += 512
        r3 = AV(o, 8)[0:3, :]; o += 8
        ones_f = AV(o, 512); o += 512
        modrow = [AV(184 * KB, 2 * KB)[0:3, :], AV(186 * KB, 2 * KB)[0:3, :]]

        WB = [AV(6 * KB, 16 * KB, BF16), AV(22 * KB, 16 * KB, BF16)]
        qT = AV(38 * KB, 24 * KB, BF16).rearrange("p (h n) -> p h n", h=NQH)
        kT = AV(62 * KB, 18 * KB, BF16).rearrange("p (h n) -> p h n", h=NKV)
        Vt = AV(80 * KB, 18 * KB, BF16).rearrange("p (c n) -> p c n", c=18)
        Ut = AV(98 * KB, 16 * KB, BF16).rearrange("p (c n) -> p c n", c=16)
        ACC = [AV(38 * KB + t * 8 * KB, 8 * KB) for t in range(8)]
        hT = AV(114 * KB, 32 * KB, BF16).rearrange("p (k n) -> p k n", k=16)
        mixT = hT
        h2T = hT
        XT = [AV(146 * KB, 8 * KB), AV(154 * KB, 8 * KB)]
        XN = [AV(162 * KB, 4 * KB, BF16), AV(166 * KB, 4 * KB, BF16)]
        ROPE = [AV(170 * KB + i * KB, KB).rearrange("p (a n) -> p a n", a=2) for i in range(4)]
        SCR = [AV(174 * KB + i * 2 * KB, 2 * KB) for i in range(8)]
        GT1 = AV(190 * KB, 8 * KB)
        GT2 = AV(198 * KB, 8 * KB)
        JUNK = AV(190 * KB, 4 * KB, BF16)
        FTAB = [AV(146 * KB + i * 2 * KB, 2 * KB, BF16).rearrange("p (a n) -> p a n", a=2) for i in range(6)]
        RDEN = [AV(162 * KB, 2 * KB), AV(164 * KB, 2 * KB)]
        ABT = AV(166 * KB, 8 * KB, BF16).rearrange("p (g n) -> p g n", g=8)
        PT = [AV(174 * KB + i * KB, KB, BF16) for i in range(4)]
        W1S = WB
        A2T = [AV(146 * KB, 16 * KB, BF16).rearrange("p (j n) -> p j n", j=8),
               AV(162 * KB, 16 * KB, BF16).rearrange("p (j n) -> p j n", j=8)]
        W2S = [AV(178 * KB, 8 * KB, BF16).rearrange("p (a n) -> p a n", a=2),
               AV(186 * KB, 8 * KB, BF16).rearrange("p (a n) -> p a n", a=2),
               AV(102 * KB, 8 * KB, BF16).rearrange("p (a n) -> p a n", a=2)]
        RSC = [AV(110 * KB, 2 * KB), AV(112 * KB, 2 * KB)]
        TMPG = [AV(194 * KB, 2 * KB), AV(196 * KB, 2 * KB)]
        FGB = AV(6 * KB, 8 * KB)

        statc = {3: [0, 0, 32], 12: [0, 96, 13], 1: [0, 252, 100]}

        def newstat(n=1):
            ent = statc[n]
            c = ent[1] + (ent[0] % ent[2]) * n
            ent[0] += 1
            assert c + n <= 352
            return c

        ACCK = lambda t: ['acc%d_%d' % (t, cb) for cb in range(4)]
        fence_id = [0]

        def fence(old_keys, eng='dve'):
            fence_id[0] += 1
            name = 'fence%d' % fence_id[0]
            c = newstat()
            S.add(eng, lambda e, c=c: e.memset(stat[:, c:c + 1], 0.0), writes=list(old_keys) + [name])
            return name

        def dma(eng, out, in_, key, reads=(), writes=(), noncontig=False):
            def fn(e, out=out, in_=in_):
                if noncontig:
                    with nc.allow_non_contiguous_dma(reason="layout"):
                        return e.dma_start(out=out, in_=in_)
                return e.dma_start(out=out, in_=in_)
            return S.add(eng, fn, reads=reads, writes=writes, dma=key)

        def mm(out, lhsT, rhs, start, stop, reads, writes):
            S.add('pe', lambda e, o_=out, l=lhsT, r=rhs, a=start, b=stop: e.matmul(o_, lhsT=l, rhs=r, start=a, stop=b),
                  reads=reads, writes=writes)

        def tr(out, in_, idt, reads, writes):
            S.add('pe', lambda e, o_=out, i_=in_, d_=idt: e.transpose(out=o_, in_=i_, identity=d_), reads=reads, writes=writes)

        def dbg_dump(name, ap, reads, eng='sp'):
            if name in dbg_out:
                dma(eng, dbg_out[name], ap, 'dbg_' + name, reads=reads, writes=['dbgo_' + name])

        def finish():
            S.add('sp', None, reads=['dbgo_' + n for n in dbg_out])
            S.emit(nc, st)
            return nc

        def dump(name, ap, reads):
            if name in dbg_out:
                dma('pool', dbg_out[name], ap, 'dbg_' + name, reads=reads, writes=['dbgo_' + name])

        dma('sp', identf, ident_d[:, :], 'c_identf', writes=['identf'])
        dma('pool', ident, ident_d[:, :], 'c_ident', writes=['ident'])
        dma('sp', sel3, sel_d[:, :], 'c_sel', writes=['sel'])
        dma('sp', r3, r3_d[:, :], 'c_r3', writes=['r3'])
        dma('sp', qg_bc, qg_d[:, :], 'c_qg', writes=['qg'])
        dma('sp', kg_bc, kg_d[:, :], 'c_kg', writes=['kg'])
        dma('sp', g1col, g1col_d[:, :], 'c_g1', writes=['g1col'])
        dma('sp', g2col, g2col_d[:, :], 'c_g2', writes=['g2col'])
        dma('sp', ccol_t, ccol[:, :], 'c_cc', writes=['ccol'])
        S.add('dve', lambda e: e.memset(ones_bf, 1.0), writes=['ones'])
        S.add('dve', lambda e: e.memset(ones_f, 1.0), writes=['onesf'])
        S.add('act', lambda e: e.activation(out=sT, in_=ccol_t, func=AF.Silu), reads=['ccol'], writes=['sT'])
        sT3 = sT.rearrange("p (k r) -> p k r", r=2)

        wb_i = [0]

        def wb_next():
            s = wb_i[0] % 2
            wb_i[0] += 1
            return s

        ada_i = [0]

        def ada_gen(j, psA, psB, extra=(), padded=False):
            m, jb = j // 4, j % 4
            s = wb_next()
            r = ada_i[0] % 2
            ada_i[0] += 1
            wv = WB[s].rearrange("p (k n) -> p k n", k=16)
            dma('pool', wv, w_ada[:, j * 512:(j + 1) * 512].rearrange("(k p) n -> p k n", p=128), 'wb%d' % s, writes=['wb%d' % s])
            dma('sp', modrow[r][2:3, :], b_ada_d[:, j * 512:(j + 1) * 512], 'brow%d' % r, reads=list(extra), writes=['brow%d' % r])
            yield
            for kc in range(16):
                if padded:
                    mm(PS(psA), STPAD[:, kc, :], wv[:, kc, :], kc == 0, kc == 15,
                       reads=['stpad', 'wb%d' % s], writes=['ps%d' % psA])
                else:
                    mm(PS(psA)[0:2, :], sT3[:, kc, :], wv[:, kc, :], kc == 0, kc == 15,
                       reads=['sT', 'wb%d' % s], writes=['ps%d' % psA])
                yield
            S.add('dve', lambda e, r=r: e.tensor_copy(out=modrow[r][0:2, :], in_=PS(psA)[0:2, :]),
                  reads=['ps%d' % psA] + list(extra), writes=['modrow%d' % r])
            if m in (0, 1, 3, 4):
                for q in range(4):
                    mm(PS(psB)[:, q * 2:(q + 1) * 2], modrow[r][:, q * 128:(q + 1) * 128], r3, True, True,
                       reads=['modrow%d' % r, 'brow%d' % r, 'r3'], writes=['ps%d' % psB])
                S.add('dve', lambda e, m=m, jb=jb: e.tensor_copy(
                    out=modcol[:, m, jb * 4:(jb + 1) * 4, :], in_=PS(psB)[:, 0:8].rearrange("p (q r) -> p q r", r=2)),
                    reads=['ps%d' % psB], writes=['modcol%d' % m])
            else:
                gt = GT1 if m == 2 else GT2
                mm(PS(psB), sel3, modrow[r], True, True, reads=['modrow%d' % r, 'brow%d' % r, 'sel'], writes=['ps%d' % psB])
                S.add('dve', lambda e, gt=gt, jb=jb: e.tensor_copy(out=gt[:, jb * 512:(jb + 1) * 512], in_=PS(psB)),
                      reads=['ps%d' % psB] + list(extra), writes=['gt%d_%d' % (m, jb)])
            yield

        def ada_block(j, psA, psB, extra=()):
            for _ in ada_gen(j, psA, psB, extra):
                pass

        for j in range(8):
            ada_block(j, 0, 1)
        S.add('dve', lambda e: e.scalar_tensor_tensor(out=gs[:, 0, :], in0=modcol[:, 1, :, 0], scalar=1.0, in1=g1col,
                                                      op0=ALU.add, op1=ALU.mult),
              reads=['modcol1', 'g1col'], writes=['gs0'])
        S.add('dve', lambda e: e.scalar_tensor_tensor(out=gs[:, 1, :], in0=modcol[:, 1, :, 1], scalar=1.0, in1=g1col,
                                                      op0=ALU.add, op1=ALU.mult),
              reads=['modcol1', 'g1col'], writes=['gs1'])

        if stop == 'A':
            dump('modcol', modcol.rearrange("p m j r -> p (m j r)"), ['modcol0', 'modcol1'])
            dump('gs', gs.rearrange("p k j -> p (k j)"), ['gs0', 'gs1'])
            return finish()

        xt_i = [0]
        tb_i = [0]
        rope_i = [0]
        scr_i = [0]
        pc_i = [0]
        pq_i = [0]
        ev_i = [0]

        def scr_next():
            s = scr_i[0] % 8
            scr_i[0] += 1
            return s

        def norm_gen(src, ntiles, gsk, gskey, shm, shr, dst, dstkey, cfg, srckeys=None, extra_reads=(), evac_extra_writes=None, tiles=None):
            for t in (tiles if tiles is not None else range(ntiles)):
                if srckeys is None:
                    xs = xt_i[0] % 2
                    xt_i[0] += 1
                    xt = XT[xs]
                    xkey = 'xt%d' % xs
                    dma('sp', xt, src[t * 128:(t + 1) * 128, :], xkey, writes=[xkey])
                    xkeys = [xkey]
                else:
                    xt = src[t]
                    xkeys = list(srckeys[t])
                c = newstat(3)
                S.add('act', lambda e, xt=xt, c=c: e.activation(out=cfg['junk'], in_=xt, func=AF.Square, accum_out=stat[:, c:c + 1]),
                      reads=xkeys + list(extra_reads), writes=['junk', 'st%d' % c])
                S.add('act', lambda e, c=c: e.activation(out=stat[:, c + 1:c + 2], in_=stat[:, c:c + 1], func=AF.Ln,
                                                         scale=1.0 / D, bias=EPS),
                      reads=['st%d' % c], writes=['st%d' % (c + 1)])
                S.add('act', lambda e, c=c: e.activation(out=stat[:, c + 2:c + 3], in_=stat[:, c + 1:c + 2], func=AF.Exp, scale=-0.5),
                      reads=['st%d' % (c + 1)], writes=['st%d' % (c + 2)])
                xs2 = t % 2
                XNc = cfg['xn']
                S.add('dve', lambda e, xt=xt, c=c, xs2=xs2, XNc=XNc: e.tensor_scalar(out=XNc[xs2], in0=xt, scalar1=stat[:, c + 2:c + 3],
                                                                           scalar2=None, op0=ALU.mult),
                      reads=xkeys + ['st%d' % (c + 2)] + list(extra_reads), writes=['xn%d' % xs2])
                yield
                for half in range(2):
                    bank = cfg['banks'][tb_i[0] % len(cfg['banks'])]
                    tb_i[0] += 1
                    for k8 in range(8):
                        kc = half * 8 + k8
                        tr(PSB(bank)[:, k8 * 128:(k8 + 1) * 128], XNc[xs2][:, kc * 128:(kc + 1) * 128], ident,
                           reads=['xn%d' % xs2, 'ident'], writes=['ps%d' % bank])
                    for k8 in range(8):
                        kc = half * 8 + k8
                        eng = 'dve' if half == 0 else 'act'
                        o_ = dst[:, kc, t * 128:(t + 1) * 128]
                        i_ = PSB(bank)[:, k8 * 128:(k8 + 1) * 128]
                        sc_ = gs[:, gsk, kc:kc + 1]
                        sh_ = modcol[:, shm, kc:kc + 1, shr]
                        if eng == 'dve':
                            S.add('dve', lambda e, o_=o_, i_=i_, sc_=sc_, sh_=sh_: e.tensor_scalar(
                                out=o_, in0=i_, scalar1=sc_, scalar2=sh_, op0=ALU.mult, op1=ALU.add),
                                reads=['ps%d' % bank, gskey, 'modcol%d' % shm] + list(extra_reads), writes=['%s_%d_%d' % (dstkey, t, kc)] + (evac_extra_writes(t) if evac_extra_writes else []))
                        else:
                            S.add('act', lambda e, o_=o_, i_=i_, sc_=sc_, sh_=sh_: e.activation(
                                out=o_, in_=i_, func=AF.Identity, scale=sc_, bias=sh_),
                                reads=['ps%d' % bank, gskey, 'modcol%d' % shm] + list(extra_reads), writes=['%s_%d_%d' % (dstkey, t, kc)] + (evac_extra_writes(t) if evac_extra_writes else []))
                yield

        def norm_set(*a, **kw):
            for _ in norm_gen(*a, **kw):
                pass

        cfgB = dict(junk=JUNK, xn=XN, banks=[2, 3, 4, 5])
        CBANKS = [6, 7]
        QBANKS = [0, 1]

        def evac_copy(out, in_, reads, writes):
            eng = 'act' if ev_i[0] % 2 == 0 else 'dve'
            ev_i[0] += 1
            if eng == 'act':
                S.add('act', lambda e, o_=out, i_=in_: e.copy(out=o_, in_=i_), reads=reads, writes=writes)
            else:
                S.add('dve', lambda e, o_=out, i_=in_: e.tensor_copy(out=o_, in_=i_), reads=reads, writes=writes)

        def qk_process(bank, is_q, use_rope, rope_src, tok0, dst, dstkey):
            pk = 'ps%d' % bank
            gbc = qg_bc if is_q else kg_bc
            gk = 'qg' if is_q else 'kg'
            s_sq, s_xn, s_a, s_b = scr_next(), scr_next(), scr_next(), scr_next()
            c = newstat(12)
            S.add('act', lambda e: e.activation(out=SCR[s_sq], in_=PS(bank), func=AF.Square), reads=[pk], writes=['scr%d' % s_sq])
            S.add('dve', lambda e: e.tensor_reduce(out=stat[:, c:c + 4], in_=SCR[s_sq].rearrange("p (h d) -> p h d", h=4),
                                                   axis=AX.X, op=ALU.add),
                  reads=['scr%d' % s_sq], writes=['st%d' % c])
            S.add('act', lambda e: e.activation(out=stat[:, c + 4:c + 8], in_=stat[:, c:c + 4], func=AF.Ln, scale=1.0 / HD, bias=EPS),
                  reads=['st%d' % c], writes=['st%d' % (c + 4)])
            S.add('act', lambda e: e.activation(out=stat[:, c + 8:c + 12], in_=stat[:, c + 4:c + 8], func=AF.Exp, scale=-0.5),
                  reads=['st%d' % (c + 4)], writes=['st%d' % (c + 8)])
            x3 = lambda ap: ap.rearrange("p (h d) -> p h d", h=4)
            S.add('dve', lambda e: e.tensor_tensor(out=x3(SCR[s_xn]), in0=x3(PS(bank)),
                                                   in1=stat[:, c + 8:c + 12].unsqueeze(2).to_broadcast([128, 4, 128]), op=ALU.mult),
                  reads=[pk, 'st%d' % (c + 8)], writes=['scr%d' % s_xn])
            if use_rope:
                S.add('dve', lambda e: e.tensor_tensor(out=x3(SCR[s_a]), in0=x3(SCR[s_xn]),
                                                       in1=gbc.unsqueeze(1).to_broadcast([128, 4, 128]), op=ALU.mult),
                      reads=['scr%d' % s_xn, gk], writes=['scr%d' % s_a])
                rs = rope_i[0] % 4
                rope_i[0] += 1
                rk = 'rope%d' % rs
                dma('sp', ROPE[rs][:, 0, :], cos_d[rope_src:rope_src + 128, :], rk, writes=[rk])
                dma('sp', ROPE[rs][:, 1, :], sin_d[rope_src:rope_src + 128, :], rk + 's', writes=[rk + 's'])
                x5 = lambda ap: ap.rearrange("p (h a f d) -> p h a f d", h=4, a=2, f=2)
                sin4 = ROPE[rs][:, 1, :].rearrange("p (a f d) -> p a f d", a=2, f=2)
                for f in range(2):
                    S.add('dve', lambda e, f=f: e.tensor_tensor(
                        out=x5(SCR[s_b])[:, :, :, f, :], in0=x5(SCR[s_a])[:, :, :, 1 - f, :],
                        in1=sin4[:, :, f, :].unsqueeze(1).to_broadcast([128, 4, 2, 32]), op=ALU.mult),
                        reads=['scr%d' % s_a, rk + 's'], writes=['scr%d' % s_b] if f == 0 else ['scr%d' % s_b, 'dummy_b'])
                S.add('dve', lambda e: e.tensor_tensor(out=x3(SCR[s_xn]), in0=x3(SCR[s_a]),
                                                       in1=ROPE[rs][:, 0, :].unsqueeze(1).to_broadcast([128, 4, 128]), op=ALU.mult),
                      reads=['scr%d' % s_a, rk], writes=['scr%d' % s_xn])
                xb = SCR[s_a].bitcast(BF16)[:, 0:512]
                S.add('dve', lambda e: e.tensor_tensor(out=xb, in0=SCR[s_xn], in1=SCR[s_b], op=ALU.add),
                      reads=['scr%d' % s_xn, 'scr%d' % s_b], writes=['scr%d' % s_a])
            else:
                xb = SCR[s_a].bitcast(BF16)[:, 0:512]
                S.add('dve', lambda e: e.tensor_tensor(out=xb.rearrange("p (h d) -> p h d", h=4), in0=x3(SCR[s_xn]),
                                                       in1=gbc.unsqueeze(1).to_broadcast([128, 4, 128]), op=ALU.mult),
                      reads=['scr%d' % s_xn, gk], writes=['scr%d' % s_a])
            def back():
                qb = QBANKS[pq_i[0] % 2]
                pq_i[0] += 1
                for h in range(4):
                    tr(PSB(qb)[:, h * 128:(h + 1) * 128], xb[:, h * 128:(h + 1) * 128], ident,
                       reads=['scr%d' % s_a, 'ident'], writes=['ps%d' % qb])
                evac_copy(dst, PSB(qb)[:, 0:512].rearrange("p (h n) -> p h n", h=4), reads=['ps%d' % qb], writes=[dstkey])
            return back

        PEND_DEPTH = 2
        HTB = [AV(114 * KB, 16 * KB, BF16).rearrange("p (k n) -> p k n", k=16),
               AV(130 * KB, 16 * KB, BF16).rearrange("p (k n) -> p k n", k=16)]

        def project_gen(ntiles, hb, blocks, key_base, rope_base):
            hbuf = HTB[hb]
            hkey = 'hT%d' % hb
            pending = []
            for cb in blocks:
                s = wb_next()
                wv = WB[s].rearrange("p (k n) -> p k n", k=16)
                dma('pool', wv, w_in[:, cb * 512:(cb + 1) * 512].rearrange("(k p) n -> p k n", p=128), 'wb%d' % s, writes=['wb%d' % s])
                for t in range(ntiles):
                    bank = CBANKS[pc_i[0] % 2]
                    pc_i[0] += 1
                    for kc in range(16):
                        mm(PS(bank), hbuf[:, kc, t * 128:(t + 1) * 128], wv[:, kc, :], kc == 0, kc == 15,
                           reads=['%s_%d_%d' % (hkey, t, kc), 'wb%d' % s], writes=['ps%d' % bank])
                    kchunk = key_base // 128 + t
                    while len(pending) > PEND_DEPTH - 1:
                        pending.pop(0)()
                    if cb < 3:
                        pending.append(qk_process(bank, True, True, rope_base + t * 128, None,
                                                  qT[:, cb * 4:(cb + 1) * 4, key_base + t * 128:key_base + (t + 1) * 128], 'qT'))
                    elif cb == 3:
                        pending.append(qk_process(bank, False, rope_base is not None, (rope_base + t * 128) if rope_base is not None else 0, None,
                                                  kT[:, :, key_base + t * 128:key_base + (t + 1) * 128], 'kT'))
                    elif cb == 4:
                        evac_copy(Vt[:, kchunk, :], PS(bank), reads=['ps%d' % bank], writes=['V'])
                    else:
                        evac_copy(Ut[:, kchunk, :], PS(bank), reads=['ps%d' % bank], writes=['U'])
                    yield
            while pending:
                pending.pop(0)()

        groups = [
            (x_oth[0:512, :], 4, 0, 'gs0', 0, [3, 4, 5], 1024, 1024),
            (x_oth[512:1024, :], 4, 0, 'gs0', 0, [3, 4, 5], 1536, 1536),
            (ctx_in, 2, 1, 'gs1', 1, [3, 4], 2048, None),
            (x_own[0:512, :], 4, 0, 'gs0', 0, [3, 4, 5, 0, 1, 2], 0, 0),
            (x_own[512:1024, :], 4, 0, 'gs0', 0, [3, 4, 5, 0, 1, 2], 512, 512),
        ]

        def mk_norm(gi):
            src, nt, gsk, gskey, shr, blocks, kb, rb = groups[gi]
            return norm_gen(src, nt, gsk, gskey, 0, shr, HTB[gi % 2], 'hT%d' % (gi % 2), cfgB)

        for _ in mk_norm(0):
            pass
        for gi, g in enumerate(groups):
            src, nt, gsk, gskey, shr, blocks, kb, rb = g
            pg = project_gen(nt, gi % 2, blocks, kb, rb)
            ng = mk_norm(gi + 1) if gi + 1 < len(groups) else None
            nunits = nt * len(blocks)
            nsteps = 2 * groups[gi + 1][1] if ng is not None else 0
            stride = max(1, nunits // (nsteps + 1)) if nsteps else 0
            for ui, _ in enumerate(pg):
                if ng is not None and ui >= 1 and (ui - 1) % stride == 0:
                    try:
                        next(ng)
                    except StopIteration:
                        ng = None
            if ng is not None:
                for _ in ng:
                    pass

        if stop == 'C':
            dump('qT', qT.rearrange("p h n -> p (h n)"), ['qT'])
            dump('kT', kT.rearrange("p h n -> p (h n)"), ['kT'])
            dump('V', Vt.rearrange("p c n -> p (c n)"), ['V'])
            dump('U', Ut.rearrange("p c n -> p (c n)"), ['U'])
            return finish()

        f0 = fence(['hT%d_%d_%d' % (hb, t, kc) for hb in range(2) for t in range(4) for kc in range(16)] + ['junk', 'xn0', 'xn1', 'xt0', 'xt1']
                   + ['scr%d' % i for i in range(8)] + ['rope%d' % i for i in range(4)] + ['rope%ds' % i for i in range(4)])
        wcs = AV(178 * KB, 2 * KB, BF16).rearrange("p (g d) -> p g d", g=8)
        cc_t = SCR[3][:, 0:128]
        sc_t = SCR[3][:, 128:256]
        dma('sp', cc_t, cc_d[:, :], 'c_ccm', reads=[f0], writes=['ccm'])
        dma('sp', sc_t, sc_d[:, :], 'c_scm', reads=[f0], writes=['scm'])
        wf_t = SCR[4].rearrange("p (g d) -> p g d", g=4)
        dma('sp', wf_t, wf_d.rearrange("g c d -> c g d"), 'c_wf', reads=[f0], writes=['wfm'])
        for g in range(4):
            mm(PS(6)[:, g * 128:(g + 1) * 128], cc_t, wf_t[:, g, :], True, True, reads=['ccm', 'wfm'], writes=['ps6'])
        for g in range(4):
            mm(PS(7)[:, g * 128:(g + 1) * 128], sc_t, wf_t[:, g, :], True, True, reads=['scm', 'wfm'], writes=['ps7'])
        S.add('dve', lambda e: e.tensor_copy(out=wcs[:, 0:4, :], in_=PS(6).rearrange("p (g d) -> p g d", g=4)), reads=['ps6', f0], writes=['wcs'])
        S.add('dve', lambda e: e.tensor_copy(out=wcs[:, 4:8, :], in_=PS(7).rearrange("p (g d) -> p g d", g=4)), reads=['ps7', f0], writes=['wcsb'])

        ada_todo = list(range(8, 24))
        SCALE = float(HD) ** -0.5
        units = [(h, qg) for h in range(NQH) for qg in range(2)]
        steps = [(ui, ch) for ui in range(len(units)) for ch in range(18)]
        NPT = 4
        cur_ada = [None]
        SBANKS = [0, 1, 6]
        DACC = [AV(188 * KB, 2 * KB), AV(182 * KB, 2 * KB)]
        STPAD = AV(158 * KB, 4 * KB, BF16).rearrange("p (k m) -> p k m", k=16)
        S.add('dve', lambda e: e.memset(STPAD, 0.0), reads=[f0], writes=['stpad'])
        S.add('dve', lambda e: e.tensor_copy(out=STPAD[:, :, 0:2], in_=sT3), reads=['sT'], writes=['stpad'])

        def ada_tick():
            if cur_ada[0] is None:
                if not ada_todo:
                    return
                cur_ada[0] = ada_gen(ada_todo.pop(0), 7, 7, extra=[f0], padded=True)
            try:
                next(cur_ada[0])
            except StopIteration:
                cur_ada[0] = None

        def emit_S(k):
            ui, ch = steps[k]
            h, qg = units[ui]
            kh = h // 3
            sbank = SBANKS[k % 3]
            mm(PS(sbank), kT[:, kh, ch * 128:(ch + 1) * 128], qT[:, h, qg * 512:(qg + 1) * 512], True, True,
               reads=['kT', 'qT'], writes=['ps%d' % sbank])
            p = k % NPT
            S.add('act', lambda e, p=p, sbank=sbank: e.activation(out=PT[p], in_=PS(sbank), func=AF.Exp, scale=SCALE),
                  reads=['ps%d' % sbank, f0], writes=['pt%d' % p])

        emit_S(0)
        emit_S(1)
        emit_S(2)
        for k, (ui, ch) in enumerate(steps):
            h, qg = units[ui]
            kh = h // 3
            p = k % NPT
            ob = 2 + (ui % 2) * 2
            mm(PS(ob), Vt[:, ch, kh * 128:(kh + 1) * 128], PT[p], ch == 0, ch == 17, reads=['V', 'pt%d' % p], writes=['ps%d' % ob])
            da = ui % 2
            if ch % 2 == 0:
                mm(PS(ob + 1), ones_bf, PT[p], ch == 0, False, reads=['ones', 'pt%d' % p], writes=['ps%d' % (ob + 1)])
            elif ch == 1:
                wr_ = ['dacc%d' % da] + (['wfm', 'scm', 'ccm'] if ui < 2 else [])
                S.add('dve', lambda e, da=da, p=p: e.tensor_copy(out=DACC[da], in_=PT[p]), reads=['pt%d' % p, f0], writes=wr_)
            else:
                S.add('dve', lambda e, da=da, p=p: e.tensor_tensor(out=DACC[da], in0=DACC[da], in1=PT[p], op=ALU.add),
                      reads=['pt%d' % p, 'dacc%d' % da], writes=['dacc%d' % da])
            if ch == 17:
                mm(PS(ob + 1), ones_f, DACC[da], False, True, reads=['onesf', 'dacc%d' % da], writes=['ps%d' % (ob + 1)])
            if k + 3 < len(steps):
                emit_S(k + 3)
            ada_tick()
            if ch == 17:
                rd = ui % 2
                S.add('dve', lambda e, rd=rd, ob=ob: e.reciprocal(out=RDEN[rd], in_=PS(ob + 1)), reads=['ps%d' % (ob + 1), f0], writes=['rden%d' % rd])
                S.add('dve', lambda e, rd=rd, h=h, qg=qg, ob=ob: e.tensor_tensor(out=mixT[:, h, qg * 512:(qg + 1) * 512], in0=PS(ob), in1=RDEN[rd],
                                                                               op=ALU.mult),
                      reads=['ps%d' % ob, 'rden%d' % rd, f0], writes=['mixT_%d' % t_ for t_ in range(qg * 4, qg * 4 + 4)])
        while ada_todo or cur_ada[0] is not None:
            ada_tick()
        S.add('dve', lambda e: e.scalar_tensor_tensor(out=gs[:, 2, :], in0=modcol[:, 4, :, 0], scalar=1.0, in1=g2col,
                                                      op0=ALU.add, op1=ALU.mult),
              reads=['modcol4', 'g2col'], writes=['gs2'])

        ft_i = [0]
        for hq in range(2):
            for ch in range(16):
                fs = ft_i[0] % 6
                ft_i[0] += 1
                fk = 'ftab%d' % fs
                dma('pool', FTAB[fs][:, 0, :], cn_d[ch * 128:(ch + 1) * 128, hq * 512:(hq + 1) * 512], fk, reads=[f0], writes=[fk])
                dma('pool', FTAB[fs][:, 1, :], sn_d[ch * 128:(ch + 1) * 128, hq * 512:(hq + 1) * 512], fk + 's', reads=[f0], writes=[fk + 's'])
                for g in range(4):
                    mm(PS(g), Ut[:, ch, g * 128:(g + 1) * 128], FTAB[fs][:, 0, :], ch == 0, ch == 15,
                       reads=['U', fk], writes=['ps%d' % g])
                    mm(PS(4 + g), Ut[:, ch, g * 128:(g + 1) * 128], FTAB[fs][:, 1, :], ch == 0, ch == 15,
                       reads=['U', fk + 's'], writes=['ps%d' % (4 + g)])
            for i in range(8):
                evac_copy(ABT[:, i, :], PS(i), reads=['ps%d' % i, f0], writes=['abt%d' % i])
            for g in range(4):
                mm(PS(g), wcs[:, g, :], ABT[:, g, :], True, False, reads=['wcs', 'wcsb', 'abt%d' % g], writes=['ps%d' % g])
                mm(PS(g), wcs[:, 4 + g, :], ABT[:, 4 + g, :], False, True, reads=['wcs', 'wcsb', 'abt%d' % (4 + g)], writes=['ps%d' % g])
                evac_copy(mixT[:, 12 + g, hq * 512:(hq + 1) * 512], PS(g), reads=['ps%d' % g], writes=['mixT_%d' % t_ for t_ in range(hq * 4, hq * 4 + 4)])

        if stop == 'D':
            dump('mixT', mixT.rearrange("p k n -> p (k n)"), ['mixT_%d' % t_ for t_ in range(8)])
            dump('gt1', GT1, ['gt2_%d' % i for i in range(4)])
            return finish()

        w2_slot = {}
        W2C = [[AV(178 * KB + j * 4 * KB, 4 * KB, BF16) for j in range(4)],
               [AV(102 * KB + j * 4 * KB, 4 * KB, BF16) for j in range(3)] + [AV(162 * KB, 4 * KB, BF16)]]
        def load_w2(b, deps):
            s = b % 2
            w2_slot[b] = s
            r0 = b * 512
            if s == 0:
                o_ = AV(178 * KB, 16 * KB, BF16).rearrange("p (a n) -> p a n", a=4)
                dma('pool', o_, w2[r0:r0 + 512, :].rearrange("(a p) n -> p a n", p=128), 'w2s0', reads=list(deps), writes=['w2s0'] + ['w2g0_%d' % j for j in range(4)])
            else:
                o_ = AV(102 * KB, 12 * KB, BF16).rearrange("p (a n) -> p a n", a=3)

                def fn(e, o_=o_, r0=r0):
                    i1 = e.dma_start(out=o_, in_=w2[r0:r0 + 384, :].rearrange("(a p) n -> p a n", p=128))
                    i2 = e.dma_start(out=W2C[1][3], in_=w2[r0 + 384:r0 + 512, :])
                    return [i1, i2]
                S.add('pool', fn, reads=list(deps), writes=['w2s1'] + ['w2g1_%d' % j for j in range(4)], dma='w2s1', ndma=2)
            for j in range(4):
                S.add('pool', lambda e, s=s, j=j: e.tensor_tensor(out=W2C[s][j], in0=W2C[s][j], in1=GT2, op=ALU.mult),
                      reads=['w2s%d' % s] + ['gt5_%d' % i for i in range(4)], writes=['w2g%d_%d' % (s, j)])

        f1 = fence(['qT', 'kT', 'V', 'U', 'wcs', 'wcsb', 'ccm', 'scm', 'wfm', 'rden0', 'rden1', 'modrow0', 'modrow1', 'brow0', 'brow1']
                   + ['pt%d' % i for i in range(4)] + ['abt%d' % i for i in range(8)]
                   + ['ftab%d' % i for i in range(6)] + ['ftab%ds' % i for i in range(6)] + ['dacc0', 'dacc1', 'stpad'])
        for t in range(8):
            dma('sp', ACC[t], x_own[t * 128:(t + 1) * 128, :], 'acc%d' % t, reads=[f1], writes=ACCK(t))
        pe_i = [0]
        tm_i = [0]
        ng2 = {}
        TMPE = [AV(102 * KB + i * 2 * KB, 2 * KB) for i in range(4)]
        cfgF = dict(junk=AV(170 * KB, 4 * KB, BF16), xn=[AV(162 * KB, 4 * KB, BF16), AV(166 * KB, 4 * KB, BF16)], banks=[4, 5, 6, 7])
        wo_slot = {}

        def load_wo(cb):
            s_ = wb_next()
            wo_slot[cb] = s_
            dma('pool', WB[s_].rearrange("p (k n) -> p k n", k=16), w_out[:, cb * 512:(cb + 1) * 512].rearrange("(k p) n -> p k n", p=128),
                'wb%d' % s_, writes=['wb%d' % s_])

        load_wo(0)
        load_wo(1)
        for cb in range(4):
            s = wo_slot[cb]
            wv = WB[s].rearrange("p (k n) -> p k n", k=16)
            for t in range(8):
                bank = pe_i[0] % 3
                pe_i[0] += 1
                for mc in range(16):
                    mm(PS(bank), mixT[:, mc, t * 128:(t + 1) * 128], wv[:, mc, :], mc == 0, mc == 15,
                       reads=['mixT_%d' % t, 'wb%d' % s], writes=['ps%d' % bank])
                tm = tm_i[0] % 4
                tm_i[0] += 1
                S.add('dve', lambda e, tm=tm, bank=bank, cb=cb: e.tensor_tensor(out=TMPE[tm], in0=PS(bank), in1=GT1[:, cb * 512:(cb + 1) * 512],
                                                                                 op=ALU.mult),
                      reads=['ps%d' % bank, 'gt2_%d' % cb, f1], writes=['tmpE%d' % tm])
                S.add('dve', lambda e, tm=tm, t=t, cb=cb: e.tensor_tensor(out=ACC[t][:, cb * 512:(cb + 1) * 512],
                                                                          in0=ACC[t][:, cb * 512:(cb + 1) * 512], in1=TMPE[tm], op=ALU.add),
                      reads=['tmpE%d' % tm, 'acc%d_%d' % (t, cb)], writes=['acc%d_%d' % (t, cb)])
                if cb == 3:
                    if t >= 2:
                        next(ng2[t - 2])
                    ng2[t] = norm_gen(ACC, 8, 2, 'gs2', 3, 0, h2T, 'hT', cfgF, srckeys=[ACCK(t_) for t_ in range(8)],
                                      extra_reads=[f1], evac_extra_writes=lambda t_: ['mixT_%d' % t_], tiles=[t])
                    next(ng2[t])
            if cb + 2 < 4:
                load_wo(cb + 2)
            if cb == 1:
                fmid = fence(['gt2_0', 'gt2_1'])
                load_w2(0, [f1, fmid])
        for t in (6, 7):
            next(ng2[t])

        if stop == 'E':
            for t in range(8):
                dump('acc%d' % t, ACC[t], ACCK(t))
            return finish()

        f2 = fence(['mixT_%d' % t_ for t_ in range(8)] + ['tmpE%d' % i for i in range(4)] + ['gt2_%d' % i for i in range(4)])
        if stop == 'F':
            dump('h2T', h2T.rearrange("p k n -> p (k n)"), ['hT_%d_%d' % (t, kc) for t in range(8) for kc in range(16)])
            return finish()

        f3 = fence(['junk', 'xn0', 'xn1'])
        NB = 16
        w1_slot = {}
        A2T = [AV(146 * KB, 8 * KB, BF16).rearrange("p (j n) -> p j n", j=4),
               AV(154 * KB, 8 * KB, BF16).rearrange("p (j n) -> p j n", j=4)]
        RSC = [AV(170 * KB, 2 * KB), AV(172 * KB, 2 * KB), AV(174 * KB, 2 * KB), AV(176 * KB, 2 * KB)]
        TMPG = [AV(194 * KB, 2 * KB), AV(196 * KB, 2 * KB)]

        def load_w1(b):
            s = wb_next()
            w1_slot[b] = s
            wv = WB[s].rearrange("p (k n) -> p k n", k=16)
            dma('pool', wv, w1[:, b * 512:(b + 1) * 512].rearrange("(k p) n -> p k n", p=128), 'wb%d' % s, writes=['wb%d' % s])

        am_i = [0]
        rs_i = [0]

        def stageA(b, j):
            s = w1_slot[b]
            wv = WB[s].rearrange("p (k n) -> p k n", k=16)
            bb = b % 2
            banks = [(am_i[0] % 2) * 2, (am_i[0] % 2) * 2 + 1]
            am_i[0] += 1
            for kc in range(16):
                for tg in range(2):
                    mm(PS(banks[tg]), wv[:, kc, j * 128:(j + 1) * 128], h2T[:, kc, tg * 512:(tg + 1) * 512], kc == 0, kc == 15,
                       reads=['wb%d' % s] + ['hT_%d_%d' % (tg * 4 + i, kc) for i in range(4)], writes=['ps%d' % banks[tg]])
            for tg in range(2):
                r = rs_i[0] % 4
                rs_i[0] += 1
                S.add('act', lambda e, r=r, bk=banks[tg]: e.activation(out=RSC[r], in_=PS(bk), func=AF.Relu),
                      reads=['ps%d' % banks[tg], f3], writes=['rsc%d' % r])
                S.add('act', lambda e, r=r, bb=bb, j=j, tg=tg: e.activation(out=A2T[bb][:, j, tg * 512:(tg + 1) * 512], in_=RSC[r], func=AF.Square),
                      reads=['rsc%d' % r, f3], writes=['a2t%d_%d' % (bb, j)])

        bn_i = [0]
        tg_i = [0]

        def stageB(b, t, cp):
            s = w2_slot[b]
            bb = b % 2
            banks = [4 + (bn_i[0] % 2) * 2, 5 + (bn_i[0] % 2) * 2]
            bn_i[0] += 1
            for j in range(4):
                for ci in range(2):
                    cb = cp * 2 + ci
                    mm(PS(banks[ci]), A2T[bb][:, j, t * 128:(t + 1) * 128], W2C[s][j][:, cb * 512:(cb + 1) * 512], j == 0, j == 3,
                       reads=['a2t%d_%d' % (bb, j), 'w2g%d_%d' % (s, j)], writes=['ps%d' % banks[ci]])
            for ci in range(2):
                cb = cp * 2 + ci
                S.add('dve', lambda e, bk=banks[ci], t=t, cb=cb: e.tensor_tensor(out=ACC[t][:, cb * 512:(cb + 1) * 512], in0=PS(bk),
                                                                                in1=ACC[t][:, cb * 512:(cb + 1) * 512], op=ALU.add),
                      reads=['ps%d' % banks[ci], 'acc%d_%d' % (t, cb), f2], writes=['acc%d_%d' % (t, cb)])

        ADD_ENG = 'pool'
        load_w1(0)
        load_w1(1)
        load_w2(1, [f1, f2, f3])
        for j in range(4):
            stageA(0, j)
        for b in range(NB):
            bunits = [(t, cp) for t in range(8) for cp in range(2)]
            aunits = list(range(4)) if b + 1 < NB else []
            for i, (t, cp) in enumerate(bunits):
                if i % 4 == 0 and aunits:
                    stageA(b + 1, aunits.pop(0))
                stageB(b, t, cp)
            if b + 2 < NB:
                load_w1(b + 2)
                load_w2(b + 2, [f1, f2, f3])

        if stop == 'G':
            for t in range(8):
                dump('acc%d' % t, ACC[t], ACCK(t))
            return finish()

        f4 = fence(['hT_%d_%d' % (t, kc) for t in range(8) for kc in range(16)] + ['wb0', 'wb1'])
        dma('sp', FGB, fg_d[:, :], 'c_fg', reads=[f4], writes=['fgb'])
        JH = AV(114 * KB, 4 * KB, BF16)
        for t in range(8):
            c = newstat(3)
            S.add('act', lambda e, t=t, c=c: e.activation(out=JH, in_=ACC[t], func=AF.Square, accum_out=stat[:, c:c + 1]),
                  reads=ACCK(t) + [f4], writes=['junkH', 'st%d' % c])
            S.add('act', lambda e, c=c: e.activation(out=stat[:, c + 1:c + 2], in_=stat[:, c:c + 1], func=AF.Ln, scale=1.0 / D, bias=EPS),
                  reads=['st%d' % c], writes=['st%d' % (c + 1)])
            S.add('act', lambda e, c=c: e.activation(out=stat[:, c + 2:c + 3], in_=stat[:, c + 1:c + 2], func=AF.Exp, scale=-0.5),
                  reads=['st%d' % (c + 1)], writes=['st%d' % (c + 2)])
            S.add('dve', lambda e, t=t, c=c: e.scalar_tensor_tensor(out=ACC[t], in0=ACC[t], scalar=stat[:, c + 2:c + 3], in1=FGB,
                                                                   op0=ALU.mult, op1=ALU.mult),
                  reads=ACCK(t) + ['st%d' % (c + 2), 'fgb'], writes=ACCK(t))
            dma('sp', out_d[t * 128:(t + 1) * 128, :], ACC[t], 'out%d' % t, reads=ACCK(t), writes=['outd%d' % t])
        S.add('sp', None, reads=['outd%d' % t for t in range(8)] + ['dbgo_' + n for n in dbg_out])
        S.emit(nc, st)
    return nc


_CONST_CACHE = {}


def _consts():
    if _CONST_CACHE:
        return _CONST_CACHE
    f = np.float32
    t = np.arange(SEQ)
    row = (t // 64).astype(np.float64)
    col = (t % 64).astype(np.float64)
    inv = 10000.0 ** (-np.arange(0, 64, 2, dtype=np.float64) / 64.0)
    ar = row[:, None] * inv[None, :]
    ac = col[:, None] * inv[None, :]
    cos = np.concatenate([np.cos(ar), np.cos(ar), np.cos(ac), np.cos(ac)], axis=1).astype(f)
    sin = np.concatenate([-np.sin(ar), np.sin(ar), -np.sin(ac), np.sin(ac)], axis=1).astype(f)
    n = np.arange(SEQ, dtype=np.int64)
    ang = 2.0 * np.pi * ((n[:, None] * n[None, :]) % SEQ).astype(np.float64) / SEQ
    cn = np.cos(ang).astype(f)
    sn = np.sin(ang).astype(f)
    c = np.arange(128, dtype=np.int64)
    angc = 2.0 * np.pi * ((c[:, None] * c[None, :]) % 128).astype(np.float64) / 128
    cc = (np.cos(angc) / 512.0).astype(f)
    sc = (-np.sin(angc) / 512.0).astype(f)
    sel3 = np.zeros((3, 128), f)
    sel3[0] = 1.0
    sel3[2] = 1.0
    r3 = np.array([[1, 0], [0, 1], [1, 1]], f)
    _CONST_CACHE.update(cos=cos, sin=sin, cn=cn, sn=sn, cc=cc, sc=sc, sel3=sel3, r3=r3, ident=np.eye(128, dtype=f))
    return _CONST_CACHE


_PROG = {}


def _in_maps(x, c, ctx, c_ctx, w_ada, b_ada, norm1_g, w_in, q_norm_g, k_norm_g, w_fourier,
             w_out, norm2_g, w_mlp1, w_mlp2, final_norm_g):
    K = _consts()
    f = np.float32
    A = lambda a: np.ascontiguousarray(np.asarray(a, dtype=f))
    x, c, ctx, c_ctx = A(x), A(c), A(ctx), A(c_ctx)
    shared = dict(
        w_ada=A(w_ada[0]), b_ada=A(b_ada[0]).reshape(1, -1),
        g1col=A(np.asarray(norm1_g[0]).reshape(16, 128).T), g2col=A(np.asarray(norm2_g[0]).reshape(16, 128).T),
        w_in=A(w_in[0]), qg_bc=A(np.broadcast_to(np.asarray(q_norm_g[0])[None, :], (128, 128))),
        kg_bc=A(np.broadcast_to(np.asarray(k_norm_g[0])[None, :], (128, 128))),
        w_f=A(w_fourier[0]), w_out=A(w_out[0]), w1=A(w_mlp1[0]), w2=A(w_mlp2[0]),
        fg_bc=A(np.broadcast_to(np.asarray(final_norm_g)[None, :], (128, D))),
        ident=K['ident'], sel3=K['sel3'], r3=K['r3'], dft_cc=K['cc'], dft_sc=K['sc'],
    )
    maps = []
    for core in range(8):
        b, half = core // 2, core % 2
        own = slice(half * NOWN, (half + 1) * NOWN)
        oth = slice((1 - half) * NOWN, (2 - half) * NOWN)
        order = np.concatenate([np.arange(own.start, own.stop), np.arange(oth.start, oth.stop)])
        cpair = np.stack([c[b], c_ctx], axis=0)
        ccol = A(cpair.reshape(2, 16, 128).transpose(2, 1, 0).reshape(128, 32))
        m = dict(shared)
        m.update(
            x_own=A(x[b, own]), x_oth=A(x[b, oth]), ctx_b=A(ctx[b]), ccol=ccol,
            rope_cos=A(K['cos'][order]), rope_sin=A(K['sin'][order]),
            dft_cn=A(K['cn'][order][:, own]), dft_sn=A(K['sn'][order][:, own]),
        )
        maps.append(m)
    return maps


def kernel(**inputs):
    if 'nc' not in _PROG:
        _PROG['nc'] = build_program()
    nc = _PROG['nc']
    maps = _in_maps(**inputs)
    res = run_bass_kernel_spmd(nc, maps, core_ids=list(range(8)))
    out = np.zeros((BATCH, SEQ, D), np.float32)
    for core in range(8):
        b, half = core // 2, core % 2
        out[b, half * NOWN:(half + 1) * NOWN] = res.results[core]["out"]
    return out
```

```python
import numpy as np
from contextlib import ExitStack
import concourse.bass as bass
import concourse.mybir as mybir
from concourse.bass_utils import run_bass_kernel_spmd

F32 = mybir.dt.float32
BF16 = mybir.dt.bfloat16
ALU = mybir.AluOpType
AF = mybir.ActivationFunctionType
AX = mybir.AxisListType

D = 2048
SEQ = 2048
BATCH = 4
CTX = 256
NOWN = 1024
HD = 128
NQH = 12
NKV = 4
DFF = 8192
EPS = 1e-6
NKEY = 2304
ENGS = ('pe', 'act', 'dve', 'pool', 'sp')


class Sched:
    def __init__(self):
        self.ops = []
        self.neng = {e: 0 for e in ENGS}
        self.lastw = {}
        self.readers = {}
        self.dma_cnt = {}

    def add(self, eng, fn, reads=(), writes=(), dma=None, ndma=1):
        idx = self.neng[eng]
        self.neng[eng] += 1
        deps = set()
        for k in reads:
            if k in self.lastw:
                deps.add(self.lastw[k])
        for k in writes:
            if k in self.lastw:
                deps.add(self.lastw[k])
            for ev in self.readers.get(k, ()):
                deps.add(ev)
        if dma is not None:
            self.dma_cnt[dma] = self.dma_cnt.get(dma, 0) + 16 * ndma
            ev = ('d', dma, self.dma_cnt[dma])
        else:
            ev = ('c', eng, idx)
        deps.discard(ev)
        for k in writes:
            self.lastw[k] = ev
            self.readers[k] = []
        for k in reads:
            if k not in writes:
                self.readers.setdefault(k, []).append(ev)
        self.ops.append(dict(eng=eng, idx=idx, fn=fn, deps=deps, ev=ev, dma=dma))
        return ev

    def plan(self):
        known = {e: {} for e in ENGS}
        snap = {}
        signal = set()
        for op in self.ops:
            F = op['eng']
            kn = known[F]
            waits = []
            for d in sorted(op['deps'], key=lambda t: (t[0], str(t[1]), t[2])):
                if d[0] == 'c':
                    E, i = d[1], d[2]
                    if E == 'pe' and F == 'pe':
                        continue
                    if kn.get(('c', E), -1) >= i:
                        continue
                    waits.append(d)
                else:
                    if kn.get(('d', d[1]), -1) >= d[2]:
                        continue
                    waits.append(d)
            for d in waits:
                if d[0] == 'c':
                    signal.add(d)
                    kn[('c', d[1])] = max(kn.get(('c', d[1]), -1), d[2])
                else:
                    kn[('d', d[1])] = max(kn.get(('d', d[1]), -1), d[2])
                for k, v in snap.get(d, {}).items():
                    if kn.get(k, -1) < v:
                        kn[k] = v
            op['waits'] = waits
            snap[op['ev']] = dict(kn)
        cnt = {e: 0 for e in ENGS}
        self.sigval = {}
        for op in self.ops:
            if op['dma'] is None and op['ev'] in signal:
                cnt[op['eng']] += 1
                self.sigval[op['ev']] = cnt[op['eng']]
                op['signal'] = True
            else:
                op['signal'] = False
        return cnt

    def emit(self, nc, stack):
        self.plan()
        esem = {e: stack.enter_context(nc.semaphore('s_' + e)) for e in ENGS}
        dsem = {}
        for k in self.dma_cnt:
            dsem[k] = stack.enter_context(nc.semaphore('d_' + str(k)))
        per = {e: [o for o in self.ops if o['eng'] == e] for e in ENGS}
        sigval = self.sigval

        def run(engname, eng):
            for op in per[engname]:
                for d in op['waits']:
                    if d[0] == 'c':
                        eng.wait_ge(esem[d[1]], sigval[d])
                    else:
                        eng.wait_ge(dsem[d[1]], d[2])
                if op['fn'] is None:
                    continue
                ins = op['fn'](eng)
                if op['dma'] is not None:
                    if not isinstance(ins, (list, tuple)):
                        ins = [ins]
                    for i_ in ins:
                        i_.then_inc(dsem[op['dma']], 16)
                elif op['signal']:
                    ins.then_inc(esem[engname], 1)

        with nc.Block() as block:
            @block.tensor
            def _(e):
                run('pe', e)

            @block.scalar
            def _(e):
                run('act', e)

            @block.vector
            def _(e):
                run('dve', e)

            @block.gpsimd
            def _(e):
                run('pool', e)

            @block.sync
            def _(e):
                run('sp', e)


KB = 1024
ARENA_BYTES = 206 * KB


def build_program(dbg=None, stop=None):
    nc = bass.Bass("TRN2", target_bir_lowering=False)
    S = Sched()

    def din(name, shape, dt=F32):
        return nc.dram_tensor(name, list(shape), dt, kind="ExternalInput").ap()

    x_own = din("x_own", [NOWN, D])
    x_oth = din("x_oth", [NOWN, D])
    ctx_in = din("ctx_b", [CTX, D])
    ccol = din("ccol", [128, 32])
    w_ada = din("w_ada", [D, 6 * D])
    g1col_d = din("g1col", [128, 16])
    g2col_d = din("g2col", [128, 16])
    w_in = din("w_in", [D, 3072])
    qg_d = din("qg_bc", [128, 128])
    kg_d = din("kg_bc", [128, 128])
    wf_d = din("w_f", [4, 128, 128])
    w_out = din("w_out", [D, D])
    w1 = din("w1", [D, DFF])
    w2 = din("w2", [DFF, D])
    fg_d = din("fg_bc", [128, D])
    ident_d = din("ident", [128, 128])
    sel_d = din("sel3", [3, 128])
    r3_d = din("r3", [3, 2])
    b_ada_d = din("b_ada", [1, 6 * D])
    cos_d = din("rope_cos", [SEQ, 128])
    sin_d = din("rope_sin", [SEQ, 128])
    cn_d = din("dft_cn", [SEQ, NOWN])
    sn_d = din("dft_sn", [SEQ, NOWN])
    cc_d = din("dft_cc", [128, 128])
    sc_d = din("dft_sc", [128, 128])
    out_d = nc.dram_tensor("out", [NOWN, D], F32, kind="ExternalOutput").ap()
    dbg_out = {}
    if dbg:
        for name, shape in dbg.items():
            dbg_out[name] = nc.dram_tensor("dbg_" + name, list(shape), F32, kind="ExternalOutput").ap()

    with ExitStack() as st:
        arena = st.enter_context(nc.sbuf_tensor("arena", [128, ARENA_BYTES // 4], F32))
        psum = st.enter_context(nc.psum_tensor("psum", [128, 4096], F32))

        def AV(off, nbytes, dt=F32):
            assert off % 4 == 0 and nbytes % 4 == 0 and off + nbytes <= ARENA_BYTES
            a = arena[:, off // 4:(off + nbytes) // 4]
            if dt == BF16:
                a = a.bitcast(BF16)
            return a

        def PS(bank):
            return psum[:, bank * 512:(bank + 1) * 512]

        def PSB(bank):
            return psum[:, bank * 512:(bank + 1) * 512].bitcast(BF16)

        o = 0
        identf = AV(o, 512); o += 512
        ident = AV(o, 256, BF16); o += 256
        ones_bf = AV(o, 256, BF16); o += 256
        qg_bc = AV(o, 512); o += 512
        kg_bc = AV(o, 512); o += 512
        g1col = AV(o, 64); o += 64
        g2col = AV(o, 64); o += 64
        ccol_t = AV(o, 128); o += 128
        sT = AV(o, 64, BF16); o += 64
        modcol = AV(o, 768).rearrange("p (m j r) -> p m j r", m=6, j=16); o += 768
        gs = AV(o, 192).rearrange("p (k j) -> p k j", k=3); o += 192
        stat = AV(o, 1408); o += 1408
        assert o <= 6 * KB
        sel3 = AV(o, 512)[0:3, :]; o += 512
        r3 = AV(o, 8)[0:3, :]; o += 8
        ones_f = AV(o, 512); o += 512
        modrow = [AV(184 * KB, 2 * KB)[0:3, :], AV(186 * KB, 2 * KB)[0:3, :]]

        WB = [AV(6 * KB, 16 * KB, BF16), AV(22 * KB, 16 * KB, BF16)]
        qT = AV(38 * KB, 24 * KB, BF16).rearrange("p (h n) -> p h n", h=NQH)
        kT = AV(62 * KB, 18 * KB, BF16).rearrange("p (h n) -> p h n", h=NKV)
        Vt = AV(80 * KB, 18 * KB, BF16).rearrange("p (c n) -> p c n", c=18)
        Ut = AV(98 * KB, 16 * KB, BF16).rearrange("p (c n) -> p c n", c=16)
        ACC = [AV(38 * KB + t * 8 * KB, 8 * KB) for t in range(8)]
        hT = AV(114 * KB, 32 * KB, BF16).rearrange("p (k n) -> p k n", k=16)
        mixT = hT
        h2T = hT
        XT = [AV(146 * KB, 8 * KB), AV(154 * KB, 8 * KB)]
        XN = [AV(162 * KB, 4 * KB, BF16), AV(166 * KB, 4 * KB, BF16)]
        ROPE = [AV(170 * KB + i * KB, KB).rearrange("p (a n) -> p a n", a=2) for i in range(4)]
        SCR = [AV(174 * KB + i * 2 * KB, 2 * KB) for i in range(8)]
        GT1 = AV(190 * KB, 8 * KB)
        GT2 = AV(198 * KB, 8 * KB)
        JUNK = AV(190 * KB, 4 * KB, BF16)
        FTAB = [AV(146 * KB + i * 2 * KB, 2 * KB, BF16).rearrange("p (a n) -> p a n", a=2) for i in range(6)]
        RDEN = [AV(162 * KB, 2 * KB), AV(164 * KB, 2 * KB)]
        ABT = AV(166 * KB, 8 * KB, BF16).rearrange("p (g n) -> p g n", g=8)
        PT = [AV(174 * KB + i * KB, KB, BF16) for i in range(4)]
        W1S = WB
        A2T = [AV(146 * KB, 16 * KB, BF16).rearrange("p (j n) -> p j n", j=8),
               AV(162 * KB, 16 * KB, BF16).rearrange("p (j n) -> p j n", j=8)]
        W2S = [AV(178 * KB, 8 * KB, BF16).rearrange("p (a n) -> p a n", a=2),
               AV(186 * KB, 8 * KB, BF16).rearrange("p (a n) -> p a n", a=2),
               AV(102 * KB, 8 * KB, BF16).rearrange("p (a n) -> p a n", a=2)]
        RSC = [AV(110 * KB, 2 * KB), AV(112 * KB, 2 * KB)]
        TMPG = [AV(194 * KB, 2 * KB), AV(196 * KB, 2 * KB)]
        FGB = AV(6 * KB, 8 * KB)

        statc = {3: [0, 0, 32], 12: [0, 96, 13], 1: [0, 252, 100]}

        def newstat(n=1):
            ent = statc[n]
            c = ent[1] + (ent[0] % ent[2]) * n
            ent[0] += 1
            assert c + n <= 352
            return c

        ACCK = lambda t: ['acc%d_%d' % (t, cb) for cb in range(4)]
        fence_id = [0]

        def fence(old_keys, eng='dve'):
            fence_id[0] += 1
            name = 'fence%d' % fence_id[0]
            c = newstat()
            S.add(eng, lambda e, c=c: e.memset(stat[:, c:c + 1], 0.0), writes=list(old_keys) + [name])
            return name

        def dma(eng, out, in_, key, reads=(), writes=(), noncontig=False):
            def fn(e, out=out, in_=in_):
                if noncontig:
                    with nc.allow_non_contiguous_dma(reason="layout"):
                        return e.dma_start(out=out, in_=in_)
                return e.dma_start(out=out, in_=in_)
            return S.add(eng, fn, reads=reads, writes=writes, dma=key)

        def mm(out, lhsT, rhs, start, stop, reads, writes):
            S.add('pe', lambda e, o_=out, l=lhsT, r=rhs, a=start, b=stop: e.matmul(o_, lhsT=l, rhs=r, start=a, stop=b),
                  reads=reads, writes=writes)

        def tr(out, in_, idt, reads, writes):
            S.add('pe', lambda e, o_=out, i_=in_, d_=idt: e.transpose(out=o_, in_=i_, identity=d_), reads=reads, writes=writes)

        def dbg_dump(name, ap, reads, eng='sp'):
            if name in dbg_out:
                dma(eng, dbg_out[name], ap, 'dbg_' + name, reads=reads, writes=['dbgo_' + name])

        def finish():
            S.add('sp', None, reads=['dbgo_' + n for n in dbg_out])
            S.emit(nc, st)
            return nc

        def dump(name, ap, reads):
            if name in dbg_out:
                dma('pool', dbg_out[name], ap, 'dbg_' + name, reads=reads, writes=['dbgo_' + name])

        dma('sp', identf, ident_d[:, :], 'c_identf', writes=['identf'])
        dma('pool', ident, ident_d[:, :], 'c_ident', writes=['ident'])
        dma('sp', sel3, sel_d[:, :], 'c_sel', writes=['sel'])
        dma('sp', r3, r3_d[:, :], 'c_r3', writes=['r3'])
        dma('sp', qg_bc, qg_d[:, :], 'c_qg', writes=['qg'])
        dma('sp', kg_bc, kg_d[:, :], 'c_kg', writes=['kg'])
        dma('sp', g1col, g1col_d[:, :], 'c_g1', writes=['g1col'])
        dma('sp', g2col, g2col_d[:, :], 'c_g2', writes=['g2col'])
        dma('sp', ccol_t, ccol[:, :], 'c_cc', writes=['ccol'])
        S.add('dve', lambda e: e.memset(ones_bf, 1.0), writes=['ones'])
        S.add('dve', lambda e: e.memset(ones_f, 1.0), writes=['onesf'])
        S.add('act', lambda e: e.activation(out=sT, in_=ccol_t, func=AF.Silu), reads=['ccol'], writes=['sT'])
        sT3 = sT.rearrange("p (k r) -> p k r", r=2)

        wb_i = [0]

        def wb_next():
            s = wb_i[0] % 2
            wb_i[0] += 1
            return s

        ada_i = [0]

        def ada_gen(j, psA, psB, extra=(), padded=False):
            m, jb = j // 4, j % 4
            s = wb_next()
            r = ada_i[0] % 2
            ada_i[0] += 1
            wv = WB[s].rearrange("p (k n) -> p k n", k=16)
            dma('pool', wv, w_ada[:, j * 512:(j + 1) * 512].rearrange("(k p) n -> p k n", p=128), 'wb%d' % s, writes=['wb%d' % s])
            dma('sp', modrow[r][2:3, :], b_ada_d[:, j * 512:(j + 1) * 512], 'brow%d' % r, reads=list(extra), writes=['brow%d' % r])
            yield
            for kc in range(16):
                if padded:
                    mm(PS(psA), STPAD[:, kc, :], wv[:, kc, :], kc == 0, kc == 15,
                       reads=['stpad', 'wb%d' % s], writes=['ps%d' % psA])
                else:
                    mm(PS(psA)[0:2, :], sT3[:, kc, :], wv[:, kc, :], kc == 0, kc == 15,
                       reads=['sT', 'wb%d' % s], writes=['ps%d' % psA])
                yield
            S.add('dve', lambda e, r=r: e.tensor_copy(out=modrow[r][0:2, :], in_=PS(psA)[0:2, :]),
                  reads=['ps%d' % psA] + list(extra), writes=['modrow%d' % r])
            if m in (0, 1, 3, 4):
                for q in range(4):
                    mm(PS(psB)[:, q * 2:(q + 1) * 2], modrow[r][:, q * 128:(q + 1) * 128], r3, True, True,
                       reads=['modrow%d' % r, 'brow%d' % r, 'r3'], writes=['ps%d' % psB])
                S.add('dve', lambda e, m=m, jb=jb: e.tensor_copy(
                    out=modcol[:, m, jb * 4:(jb + 1) * 4, :], in_=PS(psB)[:, 0:8].rearrange("p (q r) -> p q r", r=2)),
                    reads=['ps%d' % psB], writes=['modcol%d' % m])
            else:
                gt = GT1 if m == 2 else GT2
                mm(PS(psB), sel3, modrow[r], True, True, reads=['modrow%d' % r, 'brow%d' % r, 'sel'], writes=['ps%d' % psB])
                S.add('dve', lambda e, gt=gt, jb=jb: e.tensor_copy(out=gt[:, jb * 512:(jb + 1) * 512], in_=PS(psB)),
                      reads=['ps%d' % psB] + list(extra), writes=['gt%d_%d' % (m, jb)])
            yield

        def ada_block(j, psA, psB, extra=()):
            for _ in ada_gen(j, psA, psB, extra):
                pass

        for j in range(8):
            ada_block(j, 0, 1)
        S.add('dve', lambda e: e.scalar_tensor_tensor(out=gs[:, 0, :], in0=modcol[:, 1, :, 0], scalar=1.0, in1=g1col,
                                                      op0=ALU.add, op1=ALU.mult),
              reads=['modcol1', 'g1col'], writes=['gs0'])
        S.add('dve', lambda e: e.scalar_tensor_tensor(out=gs[:, 1, :], in0=modcol[:, 1, :, 1], scalar=1.0, in1=g1col,
                                                      op0=ALU.add, op1=ALU.mult),
              reads=['modcol1', 'g1col'], writes=['gs1'])

        if stop == 'A':
            dump('modcol', modcol.rearrange("p m j r -> p (m j r)"), ['modcol0', 'modcol1'])
            dump('gs', gs.rearrange("p k j -> p (k j)"), ['gs0', 'gs1'])
            return finish()

        xt_i = [0]
        tb_i = [0]
        rope_i = [0]
        scr_i = [0]
        pc_i = [0]
        pq_i = [0]
        ev_i = [0]

        def scr_next():
            s = scr_i[0] % 8
            scr_i[0] += 1
            return s

        def norm_gen(src, ntiles, gsk, gskey, shm, shr, dst, dstkey, cfg, srckeys=None, extra_reads=(), evac_extra_writes=None, tiles=None):
            for t in (tiles if tiles is not None else range(ntiles)):
                if srckeys is None:
                    xs = xt_i[0] % 2
                    xt_i[0] += 1
                    xt = XT[xs]
                    xkey = 'xt%d' % xs
                    dma('sp', xt, src[t * 128:(t + 1) * 128, :], xkey, writes=[xkey])
                    xkeys = [xkey]
                else:
                    xt = src[t]
                    xkeys = list(srckeys[t])
                c = newstat(3)
                S.add('act', lambda e, xt=xt, c=c: e.activation(out=cfg['junk'], in_=xt, func=AF.Square, accum_out=stat[:, c:c + 1]),
                      reads=xkeys + list(extra_reads), writes=['junk', 'st%d' % c])
                S.add('act', lambda e, c=c: e.activation(out=stat[:, c + 1:c + 2], in_=stat[:, c:c + 1], func=AF.Ln,
                                                         scale=1.0 / D, bias=EPS),
                      reads=['st%d' % c], writes=['st%d' % (c + 1)])
                S.add('act', lambda e, c=c: e.activation(out=stat[:, c + 2:c + 3], in_=stat[:, c + 1:c + 2], func=AF.Exp, scale=-0.5),
                      reads=['st%d' % (c + 1)], writes=['st%d' % (c + 2)])
                yield
                xs2 = t % 2
                XNc = cfg['xn']
                S.add('dve', lambda e, xt=xt, c=c, xs2=xs2, XNc=XNc: e.tensor_scalar(out=XNc[xs2], in0=xt, scalar1=stat[:, c + 2:c + 3],
                                                                           scalar2=None, op0=ALU.mult),
                      reads=xkeys + ['st%d' % (c + 2)] + list(extra_reads), writes=['xn%d' % xs2])
                yield
                for half in range(2):
                    bank = cfg['banks'][tb_i[0] % len(cfg['banks'])]
                    tb_i[0] += 1
                    for k8 in range(8):
                        kc = half * 8 + k8
                        tr(PSB(bank)[:, k8 * 128:(k8 + 1) * 128], XNc[xs2][:, kc * 128:(kc + 1) * 128], ident,
                           reads=['xn%d' % xs2, 'ident'], writes=['ps%d' % bank])
                    for k8 in range(8):
                        kc = half * 8 + k8
                        eng = 'dve' if half == 0 else 'act'
                        o_ = dst[:, kc, t * 128:(t + 1) * 128]
                        i_ = PSB(bank)[:, k8 * 128:(k8 + 1) * 128]
                        sc_ = gs[:, gsk, kc:kc + 1]
                        sh_ = modcol[:, shm, kc:kc + 1, shr]
                        if eng == 'dve':
                            S.add('dve', lambda e, o_=o_, i_=i_, sc_=sc_, sh_=sh_: e.tensor_scalar(
                                out=o_, in0=i_, scalar1=sc_, scalar2=sh_, op0=ALU.mult, op1=ALU.add),
                                reads=['ps%d' % bank, gskey, 'modcol%d' % shm] + list(extra_reads), writes=['%s_%d_%d' % (dstkey, t, kc)] + (evac_extra_writes(t) if evac_extra_writes else []))
                        else:
                            S.add('act', lambda e, o_=o_, i_=i_, sc_=sc_, sh_=sh_: e.activation(
                                out=o_, in_=i_, func=AF.Identity, scale=sc_, bias=sh_),
                                reads=['ps%d' % bank, gskey, 'modcol%d' % shm] + list(extra_reads), writes=['%s_%d_%d' % (dstkey, t, kc)] + (evac_extra_writes(t) if evac_extra_writes else []))
                yield

        def norm_set(*a, **kw):
            for _ in norm_gen(*a, **kw):
                pass

        cfgB = dict(junk=JUNK, xn=XN, banks=[2, 3, 4, 5])
        CBANKS = [6, 7]
        QBANKS = [0, 1]

        def evac_copy(out, in_, reads, writes):
            eng = 'act' if ev_i[0] % 2 == 0 else 'dve'
            ev_i[0] += 1
            if eng == 'act':
                S.add('act', lambda e, o_=out, i_=in_: e.copy(out=o_, in_=i_), reads=reads, writes=writes)
            else:
                S.add('dve', lambda e, o_=out, i_=in_: e.tensor_copy(out=o_, in_=i_), reads=reads, writes=writes)

        def qk_process(bank, is_q, use_rope, rope_src, tok0, dst, dstkey):
            pk = 'ps%d' % bank
            gbc = qg_bc if is_q else kg_bc
            gk = 'qg' if is_q else 'kg'
            s_sq, s_xn, s_a, s_b = scr_next(), scr_next(), scr_next(), scr_next()
            c = newstat(12)
            S.add('act', lambda e: e.activation(out=SCR[s_sq], in_=PS(bank), func=AF.Square), reads=[pk], writes=['scr%d' % s_sq])
            S.add('dve', lambda e: e.tensor_reduce(out=stat[:, c:c + 4], in_=SCR[s_sq].rearrange("p (h d) -> p h d", h=4),
                                                   axis=AX.X, op=ALU.add),
                  reads=['scr%d' % s_sq], writes=['st%d' % c])
            S.add('act', lambda e: e.activation(out=stat[:, c + 4:c + 8], in_=stat[:, c:c + 4], func=AF.Ln, scale=1.0 / HD, bias=EPS),
                  reads=['st%d' % c], writes=['st%d' % (c + 4)])
            S.add('act', lambda e: e.activation(out=stat[:, c + 8:c + 12], in_=stat[:, c + 4:c + 8], func=AF.Exp, scale=-0.5),
                  reads=['st%d' % (c + 4)], writes=['st%d' % (c + 8)])
            x3 = lambda ap: ap.rearrange("p (h d) -> p h d", h=4)
            S.add('dve', lambda e: e.tensor_tensor(out=x3(SCR[s_xn]), in0=x3(PS(bank)),
                                                   in1=stat[:, c + 8:c + 12].unsqueeze(2).to_broadcast([128, 4, 128]), op=ALU.mult),
                  reads=[pk, 'st%d' % (c + 8)], writes=['scr%d' % s_xn])
            if use_rope:
                S.add('dve', lambda e: e.tensor_tensor(out=x3(SCR[s_a]), in0=x3(SCR[s_xn]),
                                                       in1=gbc.unsqueeze(1).to_broadcast([128, 4, 128]), op=ALU.mult),
                      reads=['scr%d' % s_xn, gk], writes=['scr%d' % s_a])
                rs = rope_i[0] % 4
                rope_i[0] += 1
                rk = 'rope%d' % rs
                dma('sp', ROPE[rs][:, 0, :], cos_d[rope_src:rope_src + 128, :], rk, writes=[rk])
                dma('sp', ROPE[rs][:, 1, :], sin_d[rope_src:rope_src + 128, :], rk + 's', writes=[rk + 's'])
                x5 = lambda ap: ap.rearrange("p (h a f d) -> p h a f d", h=4, a=2, f=2)
                sin4 = ROPE[rs][:, 1, :].rearrange("p (a f d) -> p a f d", a=2, f=2)
                for f in range(2):
                    S.add('dve', lambda e, f=f: e.tensor_tensor(
                        out=x5(SCR[s_b])[:, :, :, f, :], in0=x5(SCR[s_a])[:, :, :, 1 - f, :],
                        in1=sin4[:, :, f, :].unsqueeze(1).to_broadcast([128, 4, 2, 32]), op=ALU.mult),
                        reads=['scr%d' % s_a, rk + 's'], writes=['scr%d' % s_b] if f == 0 else ['scr%d' % s_b, 'dummy_b'])
                S.add('dve', lambda e: e.tensor_tensor(out=x3(SCR[s_xn]), in0=x3(SCR[s_a]),
                                                       in1=ROPE[rs][:, 0, :].unsqueeze(1).to_broadcast([128, 4, 128]), op=ALU.mult),
                      reads=['scr%d' % s_a, rk], writes=['scr%d' % s_xn])
                xb = SCR[s_a].bitcast(BF16)[:, 0:512]
                S.add('dve', lambda e: e.tensor_tensor(out=xb, in0=SCR[s_xn], in1=SCR[s_b], op=ALU.add),
                      reads=['scr%d' % s_xn, 'scr%d' % s_b], writes=['scr%d' % s_a])
            else:
                xb = SCR[s_a].bitcast(BF16)[:, 0:512]
                S.add('dve', lambda e: e.tensor_tensor(out=xb.rearrange("p (h d) -> p h d", h=4), in0=x3(SCR[s_xn]),
                                                       in1=gbc.unsqueeze(1).to_broadcast([128, 4, 128]), op=ALU.mult),
                      reads=['scr%d' % s_xn, gk], writes=['scr%d' % s_a])
            def back():
                qb = QBANKS[pq_i[0] % 2]
                pq_i[0] += 1
                for h in range(4):
                    tr(PSB(qb)[:, h * 128:(h + 1) * 128], xb[:, h * 128:(h + 1) * 128], ident,
                       reads=['scr%d' % s_a, 'ident'], writes=['ps%d' % qb])
                evac_copy(dst, PSB(qb)[:, 0:512].rearrange("p (h n) -> p h n", h=4), reads=['ps%d' % qb], writes=[dstkey])
            return back

        PEND_DEPTH = 2
        HTB = [AV(114 * KB, 16 * KB, BF16).rearrange("p (k n) -> p k n", k=16),
               AV(130 * KB, 16 * KB, BF16).rearrange("p (k n) -> p k n", k=16)]

        def project_gen(ntiles, hb, blocks, key_base, rope_base):
            hbuf = HTB[hb]
            hkey = 'hT%d' % hb
            pending = []
            for cb in blocks:
                s = wb_next()
                wv = WB[s].rearrange("p (k n) -> p k n", k=16)
                dma('pool', wv, w_in[:, cb * 512:(cb + 1) * 512].rearrange("(k p) n -> p k n", p=128), 'wb%d' % s, writes=['wb%d' % s])
                for t in range(ntiles):
                    bank = CBANKS[pc_i[0] % 2]
                    pc_i[0] += 1
                    for kc in range(16):
                        mm(PS(bank), hbuf[:, kc, t * 128:(t + 1) * 128], wv[:, kc, :], kc == 0, kc == 15,
                           reads=['%s_%d_%d' % (hkey, t, kc), 'wb%d' % s], writes=['ps%d' % bank])
                    kchunk = key_base // 128 + t
                    while len(pending) > PEND_DEPTH - 1:
                        pending.pop(0)()
                    if cb < 3:
                        pending.append(qk_process(bank, True, True, rope_base + t * 128, None,
                                                  qT[:, cb * 4:(cb + 1) * 4, key_base + t * 128:key_base + (t + 1) * 128], 'qT'))
                    elif cb == 3:
                        pending.append(qk_process(bank, False, rope_base is not None, (rope_base + t * 128) if rope_base is not None else 0, None,
                                                  kT[:, :, key_base + t * 128:key_base + (t + 1) * 128], 'kT'))
                    elif cb == 4:
                        evac_copy(Vt[:, kchunk, :], PS(bank), reads=['ps%d' % bank], writes=['V'])
                    else:
                        evac_copy(Ut[:, kchunk, :], PS(bank), reads=['ps%d' % bank], writes=['U'])
                    yield
            while pending:
                pending.pop(0)()

        groups = [
            (x_oth[0:512, :], 4, 0, 'gs0', 0, [3, 4, 5], 1024, 1024),
            (x_oth[512:1024, :], 4, 0, 'gs0', 0, [3, 4, 5], 1536, 1536),
            (ctx_in, 2, 1, 'gs1', 1, [3, 4], 2048, None),
            (x_own[0:512, :], 4, 0, 'gs0', 0, [3, 4, 5, 0, 1, 2], 0, 0),
            (x_own[512:1024, :], 4, 0, 'gs0', 0, [3, 4, 5, 0, 1, 2], 512, 512),
        ]

        def mk_norm(gi):
            src, nt, gsk, gskey, shr, blocks, kb, rb = groups[gi]
            return norm_gen(src, nt, gsk, gskey, 0, shr, HTB[gi % 2], 'hT%d' % (gi % 2), cfgB)

        for _ in mk_norm(0):
            pass
        for gi, g in enumerate(groups):
            src, nt, gsk, gskey, shr, blocks, kb, rb = g
            pg = project_gen(nt, gi % 2, blocks, kb, rb)
            ng = mk_norm(gi + 1) if gi + 1 < len(groups) else None
            nunits = nt * len(blocks)
            nsteps = 3 * groups[gi + 1][1] if ng is not None else 0
            stride = max(1, nunits // (nsteps + 1)) if nsteps else 0
            for ui, _ in enumerate(pg):
                if ng is not None and ui >= 1 and (ui - 1) % stride == 0:
                    try:
                        next(ng)
                    except StopIteration:
                        ng = None
            if ng is not None:
                for _ in ng:
                    pass

        if stop == 'C':
            dump('qT', qT.rearrange("p h n -> p (h n)"), ['qT'])
            dump('kT', kT.rearrange("p h n -> p (h n)"), ['kT'])
            dump('V', Vt.rearrange("p c n -> p (c n)"), ['V'])
            dump('U', Ut.rearrange("p c n -> p (c n)"), ['U'])
            return finish()

        f0 = fence(['hT%d_%d_%d' % (hb, t, kc) for hb in range(2) for t in range(4) for kc in range(16)] + ['junk', 'xn0', 'xn1', 'xt0', 'xt1']
                   + ['scr%d' % i for i in range(8)] + ['rope%d' % i for i in range(4)] + ['rope%ds' % i for i in range(4)])
        wcs = AV(178 * KB, 2 * KB, BF16).rearrange("p (g d) -> p g d", g=8)
        cc_t = SCR[3][:, 0:128]
        sc_t = SCR[3][:, 128:256]
        dma('sp', cc_t, cc_d[:, :], 'c_ccm', reads=[f0], writes=['ccm'])
        dma('sp', sc_t, sc_d[:, :], 'c_scm', reads=[f0], writes=['scm'])
        wf_t = SCR[4].rearrange("p (g d) -> p g d", g=4)
        dma('sp', wf_t, wf_d.rearrange("g c d -> c g d"), 'c_wf', reads=[f0], writes=['wfm'])
        for g in range(4):
            mm(PS(6)[:, g * 128:(g + 1) * 128], cc_t, wf_t[:, g, :], True, True, reads=['ccm', 'wfm'], writes=['ps6'])
        for g in range(4):
            mm(PS(7)[:, g * 128:(g + 1) * 128], sc_t, wf_t[:, g, :], True, True, reads=['scm', 'wfm'], writes=['ps7'])
        S.add('dve', lambda e: e.tensor_copy(out=wcs[:, 0:4, :], in_=PS(6).rearrange("p (g d) -> p g d", g=4)), reads=['ps6', f0], writes=['wcs'])
        S.add('dve', lambda e: e.tensor_copy(out=wcs[:, 4:8, :], in_=PS(7).rearrange("p (g d) -> p g d", g=4)), reads=['ps7', f0], writes=['wcsb'])

        ada_todo = list(range(8, 24))
        SCALE = float(HD) ** -0.5
        units = [(h, qg) for h in range(NQH) for qg in range(2)]
        steps = [(ui, ch) for ui in range(len(units)) for ch in range(18)]
        NPT = 4
        cur_ada = [None]
        SBANKS = [0, 1, 6]
        DACC = [AV(188 * KB, 2 * KB), AV(182 * KB, 2 * KB)]
        STPAD = AV(158 * KB, 4 * KB, BF16).rearrange("p (k m) -> p k m", k=16)
        S.add('dve', lambda e: e.memset(STPAD, 0.0), reads=[f0], writes=['stpad'])
        S.add('dve', lambda e: e.tensor_copy(out=STPAD[:, :, 0:2], in_=sT3), reads=['sT'], writes=['stpad'])

        def ada_tick():
            if cur_ada[0] is None:
                if not ada_todo:
                    return
                cur_ada[0] = ada_gen(ada_todo.pop(0), 7, 7, extra=[f0], padded=True)
            try:
                next(cur_ada[0])
            except StopIteration:
                cur_ada[0] = None

        def emit_S(k):
            ui, ch = steps[k]
            h, qg = units[ui]
            kh = h // 3
            sbank = SBANKS[k % 3]
            mm(PS(sbank), kT[:, kh, ch * 128:(ch + 1) * 128], qT[:, h, qg * 512:(qg + 1) * 512], True, True,
               reads=['kT', 'qT'], writes=['ps%d' % sbank])
            p = k % NPT
            S.add('act', lambda e, p=p, sbank=sbank: e.activation(out=PT[p], in_=PS(sbank), func=AF.Exp, scale=SCALE),
                  reads=['ps%d' % sbank, f0], writes=['pt%d' % p])

        emit_S(0)
        emit_S(1)
        emit_S(2)
        for k, (ui, ch) in enumerate(steps):
            h, qg = units[ui]
            kh = h // 3
            p = k % NPT
            ob = 2 + (ui % 2) * 2
            mm(PS(ob), Vt[:, ch, kh * 128:(kh + 1) * 128], PT[p], ch == 0, ch == 17, reads=['V', 'pt%d' % p], writes=['ps%d' % ob])
            da = ui % 2
            if ch % 2 == 0:
                mm(PS(ob + 1), ones_bf, PT[p], ch == 0, False, reads=['ones', 'pt%d' % p], writes=['ps%d' % (ob + 1)])
            elif ch == 1:
                wr_ = ['dacc%d' % da] + (['wfm', 'scm', 'ccm'] if ui < 2 else [])
                S.add('dve', lambda e, da=da, p=p: e.tensor_copy(out=DACC[da], in_=PT[p]), reads=['pt%d' % p, f0], writes=wr_)
            else:
                S.add('dve', lambda e, da=da, p=p: e.tensor_tensor(out=DACC[da], in0=DACC[da], in1=PT[p], op=ALU.add),
                      reads=['pt%d' % p, 'dacc%d' % da], writes=['dacc%d' % da])
            if ch == 17:
                mm(PS(ob + 1), ones_f, DACC[da], False, True, reads=['onesf', 'dacc%d' % da], writes=['ps%d' % (ob + 1)])
            if k + 3 < len(steps):
                emit_S(k + 3)
            ada_tick()
            if ch == 17:
                rd = ui % 2
                S.add('dve', lambda e, rd=rd, ob=ob: e.reciprocal(out=RDEN[rd], in_=PS(ob + 1)), reads=['ps%d' % (ob + 1), f0], writes=['rden%d' % rd])
                S.add('dve', lambda e, rd=rd, h=h, qg=qg, ob=ob: e.tensor_tensor(out=mixT[:, h, qg * 512:(qg + 1) * 512], in0=PS(ob), in1=RDEN[rd],
                                                                               op=ALU.mult),
                      reads=['ps%d' % ob, 'rden%d' % rd, f0], writes=['mixT_%d' % t_ for t_ in range(qg * 4, qg * 4 + 4)])
        while ada_todo or cur_ada[0] is not None:
            ada_tick()
        S.add('dve', lambda e: e.scalar_tensor_tensor(out=gs[:, 2, :], in0=modcol[:, 4, :, 0], scalar=1.0, in1=g2col,
                                                      op0=ALU.add, op1=ALU.mult),
              reads=['modcol4', 'g2col'], writes=['gs2'])

        ft_i = [0]
        for hq in range(2):
            for ch in range(16):
                fs = ft_i[0] % 6
                ft_i[0] += 1
                fk = 'ftab%d' % fs
                dma('pool', FTAB[fs][:, 0, :], cn_d[ch * 128:(ch + 1) * 128, hq * 512:(hq + 1) * 512], fk, reads=[f0], writes=[fk])
                dma('pool', FTAB[fs][:, 1, :], sn_d[ch * 128:(ch + 1) * 128, hq * 512:(hq + 1) * 512], fk + 's', reads=[f0], writes=[fk + 's'])
                for g in range(4):
                    mm(PS(g), Ut[:, ch, g * 128:(g + 1) * 128], FTAB[fs][:, 0, :], ch == 0, ch == 15,
                       reads=['U', fk], writes=['ps%d' % g])
                    mm(PS(4 + g), Ut[:, ch, g * 128:(g + 1) * 128], FTAB[fs][:, 1, :], ch == 0, ch == 15,
                       reads=['U', fk + 's'], writes=['ps%d' % (4 + g)])
            for i in range(8):
                evac_copy(ABT[:, i, :], PS(i), reads=['ps%d' % i, f0], writes=['abt%d' % i])
            for g in range(4):
                mm(PS(g), wcs[:, g, :], ABT[:, g, :], True, False, reads=['wcs', 'wcsb', 'abt%d' % g], writes=['ps%d' % g])
                mm(PS(g), wcs[:, 4 + g, :], ABT[:, 4 + g, :], False, True, reads=['wcs', 'wcsb', 'abt%d' % (4 + g)], writes=['ps%d' % g])
                evac_copy(mixT[:, 12 + g, hq * 512:(hq + 1) * 512], PS(g), reads=['ps%d' % g], writes=['mixT_%d' % t_ for t_ in range(hq * 4, hq * 4 + 4)])

        if stop == 'D':
            dump('mixT', mixT.rearrange("p k n -> p (k n)"), ['mixT_%d' % t_ for t_ in range(8)])
            dump('gt1', GT1, ['gt2_%d' % i for i in range(4)])
            return finish()

        w2_slot = {}
        W2C = [[AV(178 * KB + j * 4 * KB, 4 * KB, BF16) for j in range(4)],
               [AV(102 * KB + j * 4 * KB, 4 * KB, BF16) for j in range(3)] + [AV(162 * KB, 4 * KB, BF16)]]
        def load_w2(b, deps):
            s = b % 2
            w2_slot[b] = s
            r0 = b * 512
            if s == 0:
                o_ = AV(178 * KB, 16 * KB, BF16).rearrange("p (a n) -> p a n", a=4)
                dma('pool', o_, w2[r0:r0 + 512, :].rearrange("(a p) n -> p a n", p=128), 'w2s0', reads=list(deps), writes=['w2s0'] + ['w2g0_%d' % j for j in range(4)])
            else:
                o_ = AV(102 * KB, 12 * KB, BF16).rearrange("p (a n) -> p a n", a=3)

                def fn(e, o_=o_, r0=r0):
                    i1 = e.dma_start(out=o_, in_=w2[r0:r0 + 384, :].rearrange("(a p) n -> p a n", p=128))
                    i2 = e.dma_start(out=W2C[1][3], in_=w2[r0 + 384:r0 + 512, :])
                    return [i1, i2]
                S.add('pool', fn, reads=list(deps), writes=['w2s1'] + ['w2g1_%d' % j for j in range(4)], dma='w2s1', ndma=2)
            for j in range(4):
                S.add('pool', lambda e, s=s, j=j: e.tensor_tensor(out=W2C[s][j], in0=W2C[s][j], in1=GT2, op=ALU.mult),
                      reads=['w2s%d' % s] + ['gt5_%d' % i for i in range(4)], writes=['w2g%d_%d' % (s, j)])

        f1 = fence(['qT', 'kT', 'V', 'U', 'wcs', 'wcsb', 'ccm', 'scm', 'wfm', 'rden0', 'rden1', 'modrow0', 'modrow1', 'brow0', 'brow1']
                   + ['pt%d' % i for i in range(4)] + ['abt%d' % i for i in range(8)]
                   + ['ftab%d' % i for i in range(6)] + ['ftab%ds' % i for i in range(6)] + ['dacc0', 'dacc1', 'stpad'])
        for t in range(8):
            dma('sp', ACC[t], x_own[t * 128:(t + 1) * 128, :], 'acc%d' % t, reads=[f1], writes=ACCK(t))
        pe_i = [0]
        tm_i = [0]
        ng2 = {}
        TMPE = [AV(102 * KB + i * 2 * KB, 2 * KB) for i in range(4)]
        cfgF = dict(junk=AV(170 * KB, 4 * KB, BF16), xn=[AV(162 * KB, 4 * KB, BF16), AV(166 * KB, 4 * KB, BF16)], banks=[4, 5, 6, 7])
        wo_slot = {}

        def load_wo(cb):
            s_ = wb_next()
            wo_slot[cb] = s_
            dma('pool', WB[s_].rearrange("p (k n) -> p k n", k=16), w_out[:, cb * 512:(cb + 1) * 512].rearrange("(k p) n -> p k n", p=128),
                'wb%d' % s_, writes=['wb%d' % s_])

        load_wo(0)
        load_wo(1)
        for cb in range(4):
            s = wo_slot[cb]
            wv = WB[s].rearrange("p (k n) -> p k n", k=16)
            for t in range(8):
                bank = pe_i[0] % 3
                pe_i[0] += 1
                for mc in range(16):
                    mm(PS(bank), mixT[:, mc, t * 128:(t + 1) * 128], wv[:, mc, :], mc == 0, mc == 15,
                       reads=['mixT_%d' % t, 'wb%d' % s], writes=['ps%d' % bank])
                tm = tm_i[0] % 4
                tm_i[0] += 1
                S.add('dve', lambda e, tm=tm, bank=bank, cb=cb: e.tensor_tensor(out=TMPE[tm], in0=PS(bank), in1=GT1[:, cb * 512:(cb + 1) * 512],
                                                                                 op=ALU.mult),
                      reads=['ps%d' % bank, 'gt2_%d' % cb, f1], writes=['tmpE%d' % tm])
                S.add('dve', lambda e, tm=tm, t=t, cb=cb: e.tensor_tensor(out=ACC[t][:, cb * 512:(cb + 1) * 512],
                                                                          in0=ACC[t][:, cb * 512:(cb + 1) * 512], in1=TMPE[tm], op=ALU.add),
                      reads=['tmpE%d' % tm, 'acc%d_%d' % (t, cb)], writes=['acc%d_%d' % (t, cb)])
                if cb == 3:
                    ng2[t] = norm_gen(ACC, 8, 2, 'gs2', 3, 0, h2T, 'hT', cfgF, srckeys=[ACCK(t_) for t_ in range(8)],
                                      extra_reads=[f1], evac_extra_writes=lambda t_: ['mixT_%d' % t_], tiles=[t])
                    next(ng2[t])
                    if t >= 2:
                        next(ng2[t - 2])
                    next(ng2[t])
            if cb + 2 < 4:
                load_wo(cb + 2)
            if cb == 1:
                fmid = fence(['gt2_0', 'gt2_1'])
                load_w2(0, [f1, fmid])
        for t in (6, 7):
            next(ng2[t])

        if stop == 'E':
            for t in range(8):
                dump('acc%d' % t, ACC[t], ACCK(t))
            return finish()

        f2 = fence(['mixT_%d' % t_ for t_ in range(8)] + ['tmpE%d' % i for i in range(4)] + ['gt2_%d' % i for i in range(4)])
        if stop == 'F':
            dump('h2T', h2T.rearrange("p k n -> p (k n)"), ['hT_%d_%d' % (t, kc) for t in range(8) for kc in range(16)])
            return finish()

        f3 = fence(['junk', 'xn0', 'xn1'])
        NB = 16
        w1_slot = {}
        A2T = [AV(146 * KB, 8 * KB, BF16).rearrange("p (j n) -> p j n", j=4),
               AV(154 * KB, 8 * KB, BF16).rearrange("p (j n) -> p j n", j=4)]
        RSC = [AV(170 * KB, 2 * KB), AV(172 * KB, 2 * KB), AV(174 * KB, 2 * KB), AV(176 * KB, 2 * KB)]
        TMPG = [AV(194 * KB, 2 * KB), AV(196 * KB, 2 * KB)]

        def load_w1(b):
            s = wb_next()
            w1_slot[b] = s
            wv = WB[s].rearrange("p (k n) -> p k n", k=16)
            dma('pool', wv, w1[:, b * 512:(b + 1) * 512].rearrange("(k p) n -> p k n", p=128), 'wb%d' % s, writes=['wb%d' % s])

        am_i = [0]
        rs_i = [0]

        def stageA(b, j):
            s = w1_slot[b]
            wv = WB[s].rearrange("p (k n) -> p k n", k=16)
            bb = b % 2
            banks = [(am_i[0] % 2) * 2, (am_i[0] % 2) * 2 + 1]
            am_i[0] += 1
            for kc in range(16):
                for tg in range(2):
                    mm(PS(banks[tg]), wv[:, kc, j * 128:(j + 1) * 128], h2T[:, kc, tg * 512:(tg + 1) * 512], kc == 0, kc == 15,
                       reads=['wb%d' % s] + ['hT_%d_%d' % (tg * 4 + i, kc) for i in range(4)], writes=['ps%d' % banks[tg]])
            for tg in range(2):
                r = rs_i[0] % 4
                rs_i[0] += 1
                S.add('act', lambda e, r=r, bk=banks[tg]: e.activation(out=RSC[r], in_=PS(bk), func=AF.Relu),
                      reads=['ps%d' % banks[tg], f3], writes=['rsc%d' % r])
                S.add('act', lambda e, r=r, bb=bb, j=j, tg=tg: e.activation(out=A2T[bb][:, j, tg * 512:(tg + 1) * 512], in_=RSC[r], func=AF.Square),
                      reads=['rsc%d' % r, f3], writes=['a2t%d_%d' % (bb, j)])

        bn_i = [0]
        tg_i = [0]

        def stageB(b, t, cp):
            s = w2_slot[b]
            bb = b % 2
            banks = [4 + (bn_i[0] % 2) * 2, 5 + (bn_i[0] % 2) * 2]
            bn_i[0] += 1
            for j in range(4):
                for ci in range(2):
                    cb = cp * 2 + ci
                    mm(PS(banks[ci]), A2T[bb][:, j, t * 128:(t + 1) * 128], W2C[s][j][:, cb * 512:(cb + 1) * 512], j == 0, j == 3,
                       reads=['a2t%d_%d' % (bb, j), 'w2g%d_%d' % (s, j)], writes=['ps%d' % banks[ci]])
            for ci in range(2):
                cb = cp * 2 + ci
                S.add('dve', lambda e, bk=banks[ci], t=t, cb=cb: e.tensor_tensor(out=ACC[t][:, cb * 512:(cb + 1) * 512], in0=PS(bk),
                                                                                in1=ACC[t][:, cb * 512:(cb + 1) * 512], op=ALU.add),
                      reads=['ps%d' % banks[ci], 'acc%d_%d' % (t, cb), f2], writes=['acc%d_%d' % (t, cb)])

        ADD_ENG = 'pool'
        load_w1(0)
        load_w1(1)
        load_w2(1, [f1, f2, f3])
        for j in range(4):
            stageA(0, j)
        for b in range(NB):
            bunits = [(t, cp) for t in range(8) for cp in range(2)]
            aunits = list(range(4)) if b + 1 < NB else []
            for i, (t, cp) in enumerate(bunits):
                if i % 4 == 0 and aunits:
                    stageA(b + 1, aunits.pop(0))
                stageB(b, t, cp)
            if b + 2 < NB:
                load_w1(b + 2)
                load_w2(b + 2, [f1, f2, f3])

        if stop == 'G':
            for t in range(8):
                dump('acc%d' % t, ACC[t], ACCK(t))
            return finish()

        f4 = fence(['hT_%d_%d' % (t, kc) for t in range(8) for kc in range(16)] + ['wb0', 'wb1'])
        dma('sp', FGB, fg_d[:, :], 'c_fg', reads=[f4], writes=['fgb'])
        JH = AV(114 * KB, 4 * KB, BF16)
        for t in range(8):
            c = newstat(3)
            S.add('act', lambda e, t=t, c=c: e.activation(out=JH, in_=ACC[t], func=AF.Square, accum_out=stat[:, c:c + 1]),
                  reads=ACCK(t) + [f4], writes=['junkH', 'st%d' % c])
            S.add('act', lambda e, c=c: e.activation(out=stat[:, c + 1:c + 2], in_=stat[:, c:c + 1], func=AF.Ln, scale=1.0 / D, bias=EPS),
                  reads=['st%d' % c], writes=['st%d' % (c + 1)])
            S.add('act', lambda e, c=c: e.activation(out=stat[:, c + 2:c + 3], in_=stat[:, c + 1:c + 2], func=AF.Exp, scale=-0.5),
                  reads=['st%d' % (c + 1)], writes=['st%d' % (c + 2)])
            S.add('dve', lambda e, t=t, c=c: e.scalar_tensor_tensor(out=ACC[t], in0=ACC[t], scalar=stat[:, c + 2:c + 3], in1=FGB,
                                                                   op0=ALU.mult, op1=ALU.mult),
                  reads=ACCK(t) + ['st%d' % (c + 2), 'fgb'], writes=ACCK(t))
            dma('sp', out_d[t * 128:(t + 1) * 128, :], ACC[t], 'out%d' % t, reads=ACCK(t), writes=['outd%d' % t])
        S.add('sp', None, reads=['outd%d' % t for t in range(8)] + ['dbgo_' + n for n in dbg_out])
        S.emit(nc, st)
    return nc


_CONST_CACHE = {}


def _consts():
    if _CONST_CACHE:
        return _CONST_CACHE
    f = np.float32
    t = np.arange(SEQ)
    row = (t // 64).astype(np.float64)
    col = (t % 64).astype(np.float64)
    inv = 10000.0 ** (-np.arange(0, 64, 2, dtype=np.float64) / 64.0)
    ar = row[:, None] * inv[None, :]
    ac = col[:, None] * inv[None, :]
    cos = np.concatenate([np.cos(ar), np.cos(ar), np.cos(ac), np.cos(ac)], axis=1).astype(f)
    sin = np.concatenate([-np.sin(ar), np.sin(ar), -np.sin(ac), np.sin(ac)], axis=1).astype(f)
    n = np.arange(SEQ, dtype=np.int64)
    ang = 2.0 * np.pi * ((n[:, None] * n[None, :]) % SEQ).astype(np.float64) / SEQ
    cn = np.cos(ang).astype(f)
    sn = np.sin(ang).astype(f)
    c = np.arange(128, dtype=np.int64)
    angc = 2.0 * np.pi * ((c[:, None] * c[None, :]) % 128).astype(np.float64) / 128
    cc = (np.cos(angc) / 512.0).astype(f)
    sc = (-np.sin(angc) / 512.0).astype(f)
    sel3 = np.zeros((3, 128), f)
    sel3[0] = 1.0
    sel3[2] = 1.0
    r3 = np.array([[1, 0], [0, 1], [1, 1]], f)
    _CONST_CACHE.update(cos=cos, sin=sin, cn=cn, sn=sn, cc=cc, sc=sc, sel3=sel3, r3=r3, ident=np.eye(128, dtype=f))
    return _CONST_CACHE


_PROG = {}


def _in_maps(x, c, ctx, c_ctx, w_ada, b_ada, norm1_g, w_in, q_norm_g, k_norm_g, w_fourier,
             w_out, norm2_g, w_mlp1, w_mlp2, final_norm_g):
    K = _consts()
    f = np.float32
    A = lambda a: np.ascontiguousarray(np.asarray(a, dtype=f))
    x, c, ctx, c_ctx = A(x), A(c), A(ctx), A(c_ctx)
    shared = dict(
        w_ada=A(w_ada[0]), b_ada=A(b_ada[0]).reshape(1, -1),
        g1col=A(np.asarray(norm1_g[0]).reshape(16, 128).T), g2col=A(np.asarray(norm2_g[0]).reshape(16, 128).T),
        w_in=A(w_in[0]), qg_bc=A(np.broadcast_to(np.asarray(q_norm_g[0])[None, :], (128, 128))),
        kg_bc=A(np.broadcast_to(np.asarray(k_norm_g[0])[None, :], (128, 128))),
        w_f=A(w_fourier[0]), w_out=A(w_out[0]), w1=A(w_mlp1[0]), w2=A(w_mlp2[0]),
        fg_bc=A(np.broadcast_to(np.asarray(final_norm_g)[None, :], (128, D))),
        ident=K['ident'], sel3=K['sel3'], r3=K['r3'], dft_cc=K['cc'], dft_sc=K['sc'],
    )
    maps = []
    for core in range(8):
        b, half = core // 2, core % 2
        own = slice(half * NOWN, (half + 1) * NOWN)
        oth = slice((1 - half) * NOWN, (2 - half) * NOWN)
        order = np.concatenate([np.arange(own.start, own.stop), np.arange(oth.start, oth.stop)])
        cpair = np.stack([c[b], c_ctx], axis=0)
        ccol = A(cpair.reshape(2, 16, 128).transpose(2, 1, 0).reshape(128, 32))
        m = dict(shared)
        m.update(
            x_own=A(x[b, own]), x_oth=A(x[b, oth]), ctx_b=A(ctx[b]), ccol=ccol,
            rope_cos=A(K['cos'][order]), rope_sin=A(K['sin'][order]),
            dft_cn=A(K['cn'][order][:, own]), dft_sn=A(K['sn'][order][:, own]),
        )
        maps.append(m)
    return maps


def kernel(**inputs):
    if 'nc' not in _PROG:
        _PROG['nc'] = build_program()
    nc = _PROG['nc']
    maps = _in_maps(**inputs)
    res = run_bass_kernel_spmd(nc, maps, core_ids=list(range(8)))
    out = np.zeros((BATCH, SEQ, D), np.float32)
    for core in range(8):
        b, half = core // 2, core % 2
        out[b, half * NOWN:(half + 1) * NOWN] = res.results[core]["out"]
    return out
```

```python
import numpy as np
from contextlib import ExitStack
import concourse.bass as bass
import concourse.mybir as mybir
from concourse.bass_utils import run_bass_kernel_spmd

F32 = mybir.dt.float32
BF16 = mybir.dt.bfloat16
ALU = mybir.AluOpType
AF = mybir.ActivationFunctionType
AX = mybir.AxisListType

D = 2048
SEQ = 2048
BATCH = 4
CTX = 256
NOWN = 1024
HD = 128
NQH = 12
NKV = 4
DFF = 8192
EPS = 1e-6
NKEY = 2304
ENGS = ('pe', 'act', 'dve', 'pool', 'sp')


class Sched:
    def __init__(self):
        self.ops = []
        self.neng = {e: 0 for e in ENGS}
        self.lastw = {}
        self.readers = {}
        self.dma_cnt = {}

    def add(self, eng, fn, reads=(), writes=(), dma=None, ndma=1):
        idx = self.neng[eng]
        self.neng[eng] += 1
        deps = set()
        for k in reads:
            if k in self.lastw:
                deps.add(self.lastw[k])
        for k in writes:
            if k in self.lastw:
                deps.add(self.lastw[k])
            for ev in self.readers.get(k, ()):
                deps.add(ev)
        if dma is not None:
            self.dma_cnt[dma] = self.dma_cnt.get(dma, 0) + 16 * ndma
            ev = ('d', dma, self.dma_cnt[dma])
        else:
            ev = ('c', eng, idx)
        deps.discard(ev)
        for k in writes:
            self.lastw[k] = ev
            self.readers[k] = []
        for k in reads:
            if k not in writes:
                self.readers.setdefault(k, []).append(ev)
        self.ops.append(dict(eng=eng, idx=idx, fn=fn, deps=deps, ev=ev, dma=dma))
        return ev

    def plan(self):
        known = {e: {} for e in ENGS}
        snap = {}
        signal = set()
        for op in self.ops:
            F = op['eng']
            kn = known[F]
            waits = []
            for d in sorted(op['deps'], key=lambda t: (t[0], str(t[1]), t[2])):
                if d[0] == 'c':
                    E, i = d[1], d[2]
                    if E == 'pe' and F == 'pe':
                        continue
                    if kn.get(('c', E), -1) >= i:
                        continue
                    waits.append(d)
                else:
                    if kn.get(('d', d[1]), -1) >= d[2]:
                        continue
                    waits.append(d)
            for d in waits:
                if d[0] == 'c':
                    signal.add(d)
                    kn[('c', d[1])] = max(kn.get(('c', d[1]), -1), d[2])
                else:
                    kn[('d', d[1])] = max(kn.get(('d', d[1]), -1), d[2])
                for k, v in snap.get(d, {}).items():
                    if kn.get(k, -1) < v:
                        kn[k] = v
            op['waits'] = waits
            snap[op['ev']] = dict(kn)
        cnt = {e: 0 for e in ENGS}
        self.sigval = {}
        for op in self.ops:
            if op['dma'] is None and op['ev'] in signal:
                cnt[op['eng']] += 1
                self.sigval[op['ev']] = cnt[op['eng']]
                op['signal'] = True
            else:
                op['signal'] = False
        return cnt

    def emit(self, nc, stack):
        self.plan()
        esem = {e: stack.enter_context(nc.semaphore('s_' + e)) for e in ENGS}
        dsem = {}
        for k in self.dma_cnt:
            dsem[k] = stack.enter_context(nc.semaphore('d_' + str(k)))
        per = {e: [o for o in self.ops if o['eng'] == e] for e in ENGS}
        sigval = self.sigval

        def run(engname, eng):
            for op in per[engname]:
                for d in op['waits']:
                    if d[0] == 'c':
                        eng.wait_ge(esem[d[1]], sigval[d])
                    else:
                        eng.wait_ge(dsem[d[1]], d[2])
                if op['fn'] is None:
                    continue
                ins = op['fn'](eng)
                if op['dma'] is not None:
                    if not isinstance(ins, (list, tuple)):
                        ins = [ins]
                    for i_ in ins:
                        i_.then_inc(dsem[op['dma']], 16)
                elif op['signal']:
                    ins.then_inc(esem[engname], 1)

        with nc.Block() as block:
            @block.tensor
            def _(e):
                run('pe', e)

            @block.scalar
            def _(e):
                run('act', e)

            @block.vector
            def _(e):
                run('dve', e)

            @block.gpsimd
            def _(e):
                run('pool', e)

            @block.sync
            def _(e):
                run('sp', e)


KB = 1024
ARENA_BYTES = 206 * KB


def build_program(dbg=None, stop=None):
    nc = bass.Bass("TRN2", target_bir_lowering=False)
    S = Sched()

    def din(name, shape, dt=F32):
        return nc.dram_tensor(name, list(shape), dt, kind="ExternalInput").ap()

    x_own = din("x_own", [NOWN, D])
    x_oth = din("x_oth", [NOWN, D])
    ctx_in = din("ctx_b", [CTX, D])
    ccol = din("ccol", [128, 32])
    w_ada = din("w_ada", [D, 6 * D])
    g1col_d = din("g1col", [128, 16])
    g2col_d = din("g2col", [128, 16])
    w_in = din("w_in", [D, 3072])
    qg_d = din("qg_bc", [128, 128])
    kg_d = din("kg_bc", [128, 128])
    wf_d = din("w_f", [4, 128, 128])
    w_out = din("w_out", [D, D])
    w1 = din("w1", [D, DFF])
    w2 = din("w2", [DFF, D])
    fg_d = din("fg_bc", [128, D])
    ident_d = din("ident", [128, 128])
    sel_d = din("sel3", [3, 128])
    r3_d = din("r3", [3, 2])
    b_ada_d = din("b_ada", [1, 6 * D])
    cos_d = din("rope_cos", [SEQ, 128])
    sin_d = din("rope_sin", [SEQ, 128])
    cn_d = din("dft_cn", [SEQ, NOWN])
    sn_d = din("dft_sn", [SEQ, NOWN])
    cc_d = din("dft_cc", [128, 128])
    sc_d = din("dft_sc", [128, 128])
    out_d = nc.dram_tensor("out", [NOWN, D], F32, kind="ExternalOutput").ap()
    dbg_out = {}
    if dbg:
        for name, shape in dbg.items():
            dbg_out[name] = nc.dram_tensor("dbg_" + name, list(shape), F32, kind="ExternalOutput").ap()

    with ExitStack() as st:
        arena = st.enter_context(nc.sbuf_tensor("arena", [128, ARENA_BYTES // 4], F32))
        psum = st.enter_context(nc.psum_tensor("psum", [128, 4096], F32))

        def AV(off, nbytes, dt=F32):
            assert off % 4 == 0 and nbytes % 4 == 0 and off + nbytes <= ARENA_BYTES
            a = arena[:, off // 4:(off + nbytes) // 4]
            if dt == BF16:
                a = a.bitcast(BF16)
            return a

        def PS(bank):
            return psum[:, bank * 512:(bank + 1) * 512]

        def PSB(bank):
            return psum[:, bank * 512:(bank + 1) * 512].bitcast(BF16)

        o = 0
        identf = AV(o, 512); o += 512
        ident = AV(o, 256, BF16); o += 256
        ones_bf = AV(o, 256, BF16); o += 256
        qg_bc = AV(o, 512); o += 512
        kg_bc = AV(o, 512); o += 512
        g1col = AV(o, 64); o += 64
        g2col = AV(o, 64); o += 64
        ccol_t = AV(o, 128); o += 128
        sT = AV(o, 64, BF16); o += 64
        modcol = AV(o, 768).rearrange("p (m j r) -> p m j r", m=6, j=16); o += 768
        gs = AV(o, 192).rearrange("p (k j) -> p k j", k=3); o += 192
        stat = AV(o, 1408); o += 1408
        assert o <= 6 * KB
        sel3 = AV(o, 512)[0:3, :]; o += 512
        r3 = AV(o, 8)[0:3, :]; o += 8
        ones_f = AV(o, 512); o += 512
        modrow = [AV(184 * KB, 2 * KB)[0:3, :], AV(186 * KB, 2 * KB)[0:3, :]]

        WB = [AV(6 * KB, 16 * KB, BF16), AV(22 * KB, 16 * KB, BF16)]
        qT = AV(38 * KB, 24 * KB, BF16).rearrange("p (h n) -> p h n", h=NQH)
        kT = AV(62 * KB, 18 * KB, BF16).rearrange("p (h n) -> p h n", h=NKV)
        Vt = AV(80 * KB, 18 * KB, BF16).rearrange("p (c n) -> p c n", c=18)
        Ut = AV(98 * KB, 16 * KB, BF16).rearrange("p (c n) -> p c n", c=16)
        ACC = [AV(38 * KB + t * 8 * KB, 8 * KB) for t in range(8)]
        hT = AV(114 * KB, 32 * KB, BF16).rearrange("p (k n) -> p k n", k=16)
        mixT = hT
        h2T = hT
        XT = [AV(146 * KB, 8 * KB), AV(154 * KB, 8 * KB)]
        XN = [AV(162 * KB, 4 * KB, BF16), AV(166 * KB, 4 * KB, BF16)]
        ROPE = [AV(170 * KB + i * KB, KB).rearrange("p (a n) -> p a n", a=2) for i in range(4)]
        SCR = [AV(174 * KB + i * 2 * KB, 2 * KB) for i in range(8)]
        GT1 = AV(190 * KB, 8 * KB)
        GT2 = AV(198 * KB, 8 * KB)
        JUNK = AV(190 * KB, 4 * KB, BF16)
        FTAB = [AV(146 * KB + i * 2 * KB, 2 * KB, BF16).rearrange("p (a n) -> p a n", a=2) for i in range(6)]
        RDEN = [AV(162 * KB, 2 * KB), AV(164 * KB, 2 * KB)]
        ABT = AV(166 * KB, 8 * KB, BF16).rearrange("p (g n) -> p g n", g=8)
        PT = [AV(174 * KB + i * KB, KB, BF16) for i in range(4)]
        W1S = WB
        A2T = [AV(146 * KB, 16 * KB, BF16).rearrange("p (j n) -> p j n", j=8),
               AV(162 * KB, 16 * KB, BF16).rearrange("p (j n) -> p j n", j=8)]
        W2S = [AV(178 * KB, 8 * KB, BF16).rearrange("p (a n) -> p a n", a=2),
               AV(186 * KB, 8 * KB, BF16).rearrange("p (a n) -> p a n", a=2),
               AV(102 * KB, 8 * KB, BF16).rearrange("p (a n) -> p a n", a=2)]
        RSC = [AV(110 * KB, 2 * KB), AV(112 * KB, 2 * KB)]
        TMPG = [AV(194 * KB, 2 * KB), AV(196 * KB, 2 * KB)]
        FGB = AV(6 * KB, 8 * KB)

        statc = {3: [0, 0, 32], 12: [0, 96, 13], 1: [0, 252, 100]}

        def newstat(n=1):
            ent = statc[n]
            c = ent[1] + (ent[0] % ent[2]) * n
            ent[0] += 1
            assert c + n <= 352
            return c

        ACCK = lambda t: ['acc%d_%d' % (t, cb) for cb in range(4)]
        fence_id = [0]

        def fence(old_keys, eng='dve'):
            fence_id[0] += 1
            name = 'fence%d' % fence_id[0]
            c = newstat()
            S.add(eng, lambda e, c=c: e.memset(stat[:, c:c + 1], 0.0), writes=list(old_keys) + [name])
            return name

        def dma(eng, out, in_, key, reads=(), writes=(), noncontig=False):
            def fn(e, out=out, in_=in_):
                if noncontig:
                    with nc.allow_non_contiguous_dma(reason="layout"):
                        return e.dma_start(out=out, in_=in_)
                return e.dma_start(out=out, in_=in_)
            return S.add(eng, fn, reads=reads, writes=writes, dma=key)

        def mm(out, lhsT, rhs, start, stop, reads, writes):
            S.add('pe', lambda e, o_=out, l=lhsT, r=rhs, a=start, b=stop: e.matmul(o_, lhsT=l, rhs=r, start=a, stop=b),
                  reads=reads, writes=writes)

        def tr(out, in_, idt, reads, writes):
            S.add('pe', lambda e, o_=out, i_=in_, d_=idt: e.transpose(out=o_, in_=i_, identity=d_), reads=reads, writes=writes)

        def dbg_dump(name, ap, reads, eng='sp'):
            if name in dbg_out:
                dma(eng, dbg_out[name], ap, 'dbg_' + name, reads=reads, writes=['dbgo_' + name])

        def finish():
            S.add('sp', None, reads=['dbgo_' + n for n in dbg_out])
            S.emit(nc, st)
            return nc

        def dump(name, ap, reads):
            if name in dbg_out:
                dma('pool', dbg_out[name], ap, 'dbg_' + name, reads=reads, writes=['dbgo_' + name])

        dma('sp', identf, ident_d[:, :], 'c_identf', writes=['identf'])
        dma('pool', ident, ident_d[:, :], 'c_ident', writes=['ident'])
        dma('sp', sel3, sel_d[:, :], 'c_sel', writes=['sel'])
        dma('sp', r3, r3_d[:, :], 'c_r3', writes=['r3'])
        dma('sp', qg_bc, qg_d[:, :], 'c_qg', writes=['qg'])
        dma('sp', kg_bc, kg_d[:, :], 'c_kg', writes=['kg'])
        dma('sp', g1col, g1col_d[:, :], 'c_g1', writes=['g1col'])
        dma('sp', g2col, g2col_d[:, :], 'c_g2', writes=['g2col'])
        dma('sp', ccol_t, ccol[:, :], 'c_cc', writes=['ccol'])
        S.add('dve', lambda e: e.memset(ones_bf, 1.0), writes=['ones'])
        S.add('dve', lambda e: e.memset(ones_f, 1.0), writes=['onesf'])
        S.add('act', lambda e: e.activation(out=sT, in_=ccol_t, func=AF.Silu), reads=['ccol'], writes=['sT'])
        sT3 = sT.rearrange("p (k r) -> p k r", r=2)

        wb_i = [0]

        def wb_next():
            s = wb_i[0] % 2
            wb_i[0] += 1
            return s

        ada_i = [0]

        def ada_gen(j, psA, psB, extra=(), padded=False):
            m, jb = j // 4, j % 4
            s = wb_next()
            r = ada_i[0] % 2
            ada_i[0] += 1
            wv = WB[s].rearrange("p (k n) -> p k n", k=16)
            dma('pool', wv, w_ada[:, j * 512:(j + 1) * 512].rearrange("(k p) n -> p k n", p=128), 'wb%d' % s, writes=['wb%d' % s])
            dma('sp', modrow[r][2:3, :], b_ada_d[:, j * 512:(j + 1) * 512], 'brow%d' % r, reads=list(extra), writes=['brow%d' % r])
            yield
            for kc in range(16):
                if padded:
                    mm(PS(psA), STPAD[:, kc, :], wv[:, kc, :], kc == 0, kc == 15,
                       reads=['stpad', 'wb%d' % s], writes=['ps%d' % psA])
                else:
                    mm(PS(psA)[0:2, :], sT3[:, kc, :], wv[:, kc, :], kc == 0, kc == 15,
                       reads=['sT', 'wb%d' % s], writes=['ps%d' % psA])
                yield
            S.add('dve', lambda e, r=r: e.tensor_copy(out=modrow[r][0:2, :], in_=PS(psA)[0:2, :]),
                  reads=['ps%d' % psA] + list(extra), writes=['modrow%d' % r])
            if m in (0, 1, 3, 4):
                for q in range(4):
                    mm(PS(psB)[:, q * 2:(q + 1) * 2], modrow[r][:, q * 128:(q + 1) * 128], r3, True, True,
                       reads=['modrow%d' % r, 'brow%d' % r, 'r3'], writes=['ps%d' % psB])
                S.add('dve', lambda e, m=m, jb=jb: e.tensor_copy(
                    out=modcol[:, m, jb * 4:(jb + 1) * 4, :], in_=PS(psB)[:, 0:8].rearrange("p (q r) -> p q r", r=2)),
                    reads=['ps%d' % psB], writes=['modcol%d' % m])
            else:
                gt = GT1 if m == 2 else GT2
                mm(PS(psB), sel3, modrow[r], True, True, reads=['modrow%d' % r, 'brow%d' % r, 'sel'], writes=['ps%d' % psB])
                S.add('dve', lambda e, gt=gt, jb=jb: e.tensor_copy(out=gt[:, jb * 512:(jb + 1) * 512], in_=PS(psB)),
                      reads=['ps%d' % psB] + list(extra), writes=['gt%d_%d' % (m, jb)])
            yield

        def ada_block(j, psA, psB, extra=()):
            for _ in ada_gen(j, psA, psB, extra):
                pass

        for j in range(8):
            ada_block(j, 0, 1)
        S.add('dve', lambda e: e.scalar_tensor_tensor(out=gs[:, 0, :], in0=modcol[:, 1, :, 0], scalar=1.0, in1=g1col,
                                                      op0=ALU.add, op1=ALU.mult),
              reads=['modcol1', 'g1col'], writes=['gs0'])
        S.add('dve', lambda e: e.scalar_tensor_tensor(out=gs[:, 1, :], in0=modcol[:, 1, :, 1], scalar=1.0, in1=g1col,
                                                      op0=ALU.add, op1=ALU.mult),
              reads=['modcol1', 'g1col'], writes=['gs1'])

        if stop == 'A':
            dump('modcol', modcol.rearrange("p m j r -> p (m j r)"), ['modcol0', 'modcol1'])
            dump('gs', gs.rearrange("p k j -> p (k j)"), ['gs0', 'gs1'])
            return finish()

        xt_i = [0]
        tb_i = [0]
        rope_i = [0]
        scr_i = [0]
        pc_i = [0]
        pq_i = [0]
        ev_i = [0]

        def scr_next():
            s = scr_i[0] % 8
            scr_i[0] += 1
            return s

        def norm_gen(src, ntiles, gsk, gskey, shm, shr, dst, dstkey, cfg, srckeys=None, extra_reads=(), evac_extra_writes=None, tiles=None):
            for t in (tiles if tiles is not None else range(ntiles)):
                if srckeys is None:
                    xs = xt_i[0] % 2
                    xt_i[0] += 1
                    xt = XT[xs]
                    xkey = 'xt%d' % xs
                    dma('sp', xt, src[t * 128:(t + 1) * 128, :], xkey, writes=[xkey])
                    xkeys = [xkey]
                else:
                    xt = src[t]
                    xkeys = list(srckeys[t])
                c = newstat(3)
                S.add('act', lambda e, xt=xt, c=c: e.activation(out=cfg['junk'], in_=xt, func=AF.Square, accum_out=stat[:, c:c + 1]),
                      reads=xkeys + list(extra_reads), writes=['junk', 'st%d' % c])
                S.add('act', lambda e, c=c: e.activation(out=stat[:, c + 1:c + 2], in_=stat[:, c:c + 1], func=AF.Ln,
                                                         scale=1.0 / D, bias=EPS),
                      reads=['st%d' % c], writes=['st%d' % (c + 1)])
                S.add('act', lambda e, c=c: e.activation(out=stat[:, c + 2:c + 3], in_=stat[:, c + 1:c + 2], func=AF.Exp, scale=-0.5),
                      reads=['st%d' % (c + 1)], writes=['st%d' % (c + 2)])
                yield
                xs2 = t % 2
                XNc = cfg['xn']
                S.add('dve', lambda e, xt=xt, c=c, xs2=xs2, XNc=XNc: e.tensor_scalar(out=XNc[xs2], in0=xt, scalar1=stat[:, c + 2:c + 3],
                                                                           scalar2=None, op0=ALU.mult),
                      reads=xkeys + ['st%d' % (c + 2)] + list(extra_reads), writes=['xn%d' % xs2])
                yield
                for half in range(2):
                    bank = cfg['banks'][tb_i[0] % len(cfg['banks'])]
                    tb_i[0] += 1
                    for k8 in range(8):
                        kc = half * 8 + k8
                        tr(PSB(bank)[:, k8 * 128:(k8 + 1) * 128], XNc[xs2][:, kc * 128:(kc + 1) * 128], ident,
                           reads=['xn%d' % xs2, 'ident'], writes=['ps%d' % bank])
                    for k8 in range(8):
                        kc = half * 8 + k8
                        eng = 'dve' if half == 0 else 'act'
                        o_ = dst[:, kc, t * 128:(t + 1) * 128]
                        i_ = PSB(bank)[:, k8 * 128:(k8 + 1) * 128]
                        sc_ = gs[:, gsk, kc:kc + 1]
                        sh_ = modcol[:, shm, kc:kc + 1, shr]
                        if eng == 'dve':
                            S.add('dve', lambda e, o_=o_, i_=i_, sc_=sc_, sh_=sh_: e.tensor_scalar(
                                out=o_, in0=i_, scalar1=sc_, scalar2=sh_, op0=ALU.mult, op1=ALU.add),
                                reads=['ps%d' % bank, gskey, 'modcol%d' % shm] + list(extra_reads), writes=['%s_%d_%d' % (dstkey, t, kc)] + (evac_extra_writes(t) if evac_extra_writes else []))
                        else:
                            S.add('act', lambda e, o_=o_, i_=i_, sc_=sc_, sh_=sh_: e.activation(
                                out=o_, in_=i_, func=AF.Identity, scale=sc_, bias=sh_),
                                reads=['ps%d' % bank, gskey, 'modcol%d' % shm] + list(extra_reads), writes=['%s_%d_%d' % (dstkey, t, kc)] + (evac_extra_writes(t) if evac_extra_writes else []))
                yield

        def norm_set(*a, **kw):
            for _ in norm_gen(*a, **kw):
                pass

        cfgB = dict(junk=JUNK, xn=XN, banks=[2, 3, 4, 5])
        CBANKS = [6, 7]
        QBANKS = [0, 1]

        def evac_copy(out, in_, reads, writes):
            eng = 'act' if ev_i[0] % 2 == 0 else 'dve'
            ev_i[0] += 1
            if eng == 'act':
                S.add('act', lambda e, o_=out, i_=in_: e.copy(out=o_, in_=i_), reads=reads, writes=writes)
            else:
                S.add('dve', lambda e, o_=out, i_=in_: e.tensor_copy(out=o_, in_=i_), reads=reads, writes=writes)

        def qk_process(bank, is_q, use_rope, rope_src, tok0, dst, dstkey):
            pk = 'ps%d' % bank
            gbc = qg_bc if is_q else kg_bc
            gk = 'qg' if is_q else 'kg'
            s_sq, s_xn, s_a, s_b = scr_next(), scr_next(), scr_next(), scr_next()
            c = newstat(12)
            S.add('act', lambda e: e.activation(out=SCR[s_sq], in_=PS(bank), func=AF.Square), reads=[pk], writes=['scr%d' % s_sq])
            S.add('dve', lambda e: e.tensor_reduce(out=stat[:, c:c + 4], in_=SCR[s_sq].rearrange("p (h d) -> p h d", h=4),
                                                   axis=AX.X, op=ALU.add),
                  reads=['scr%d' % s_sq], writes=['st%d' % c])
            S.add('act', lambda e: e.activation(out=stat[:, c + 4:c + 8], in_=stat[:, c:c + 4], func=AF.Ln, scale=1.0 / HD, bias=EPS),
                  reads=['st%d' % c], writes=['st%d' % (c + 4)])
            S.add('act', lambda e: e.activation(out=stat[:, c + 8:c + 12], in_=stat[:, c + 4:c + 8], func=AF.Exp, scale=-0.5),
                  reads=['st%d' % (c + 4)], writes=['st%d' % (c + 8)])
            x3 = lambda ap: ap.rearrange("p (h d) -> p h d", h=4)
            S.add('dve', lambda e: e.tensor_tensor(out=x3(SCR[s_xn]), in0=x3(PS(bank)),
                                                   in1=stat[:, c + 8:c + 12].unsqueeze(2).to_broadcast([128, 4, 128]), op=ALU.mult),
                  reads=[pk, 'st%d' % (c + 8)], writes=['scr%d' % s_xn])
            if use_rope:
                S.add('dve', lambda e: e.tensor_tensor(out=x3(SCR[s_a]), in0=x3(SCR[s_xn]),
                                                       in1=gbc.unsqueeze(1).to_broadcast([128, 4, 128]), op=ALU.mult),
                      reads=['scr%d' % s_xn, gk], writes=['scr%d' % s_a])
                rs = rope_i[0] % 4
                rope_i[0] += 1
                rk = 'rope%d' % rs
                dma('sp', ROPE[rs][:, 0, :], cos_d[rope_src:rope_src + 128, :], rk, writes=[rk])
                dma('sp', ROPE[rs][:, 1, :], sin_d[rope_src:rope_src + 128, :], rk + 's', writes=[rk + 's'])
                x5 = lambda ap: ap.rearrange("p (h a f d) -> p h a f d", h=4, a=2, f=2)
                sin4 = ROPE[rs][:, 1, :].rearrange("p (a f d) -> p a f d", a=2, f=2)
                for f in range(2):
                    S.add('dve', lambda e, f=f: e.tensor_tensor(
                        out=x5(SCR[s_b])[:, :, :, f, :], in0=x5(SCR[s_a])[:, :, :, 1 - f, :],
                        in1=sin4[:, :, f, :].unsqueeze(1).to_broadcast([128, 4, 2, 32]), op=ALU.mult),
                        reads=['scr%d' % s_a, rk + 's'], writes=['scr%d' % s_b] if f == 0 else ['scr%d' % s_b, 'dummy_b'])
                S.add('dve', lambda e: e.tensor_tensor(out=x3(SCR[s_xn]), in0=x3(SCR[s_a]),
                                                       in1=ROPE[rs][:, 0, :].unsqueeze(1).to_broadcast([128, 4, 128]), op=ALU.mult),
                      reads=['scr%d' % s_a, rk], writes=['scr%d' % s_xn])
                xb = SCR[s_a].bitcast(BF16)[:, 0:512]
                S.add('dve', lambda e: e.tensor_tensor(out=xb, in0=SCR[s_xn], in1=SCR[s_b], op=ALU.add),
                      reads=['scr%d' % s_xn, 'scr%d' % s_b], writes=['scr%d' % s_a])
            else:
                xb = SCR[s_a].bitcast(BF16)[:, 0:512]
                S.add('dve', lambda e: e.tensor_tensor(out=xb.rearrange("p (h d) -> p h d", h=4), in0=x3(SCR[s_xn]),
                                                       in1=gbc.unsqueeze(1).to_broadcast([128, 4, 128]), op=ALU.mult),
                      reads=['scr%d' % s_xn, gk], writes=['scr%d' % s_a])
            def back():
                qb = QBANKS[pq_i[0] % 2]
                pq_i[0] += 1
                for h in range(4):
                    tr(PSB(qb)[:, h * 128:(h + 1) * 128], xb[:, h * 128:(h + 1) * 128], ident,
                       reads=['scr%d' % s_a, 'ident'], writes=['ps%d' % qb])
                evac_copy(dst, PSB(qb)[:, 0:512].rearrange("p (h n) -> p h n", h=4), reads=['ps%d' % qb], writes=[dstkey])
            return back

        PEND_DEPTH = 2
        HTB = [AV(114 * KB, 16 * KB, BF16).rearrange("p (k n) -> p k n", k=16),
               AV(130 * KB, 16 * KB, BF16).rearrange("p (k n) -> p k n", k=16)]

        def project_gen(ntiles, hb, blocks, key_base, rope_base):
            hbuf = HTB[hb]
            hkey = 'hT%d' % hb
            pending = []
            for cb in blocks:
                s = wb_next()
                wv = WB[s].rearrange("p (k n) -> p k n", k=16)
                dma('pool', wv, w_in[:, cb * 512:(cb + 1) * 512].rearrange("(k p) n -> p k n", p=128), 'wb%d' % s, writes=['wb%d' % s])
                for t in range(ntiles):
                    bank = CBANKS[pc_i[0] % 2]
                    pc_i[0] += 1
                    for kc in range(16):
                        mm(PS(bank), hbuf[:, kc, t * 128:(t + 1) * 128], wv[:, kc, :], kc == 0, kc == 15,
                           reads=['%s_%d_%d' % (hkey, t, kc), 'wb%d' % s], writes=['ps%d' % bank])
                    kchunk = key_base // 128 + t
                    while len(pending) > PEND_DEPTH - 1:
                        pending.pop(0)()
                    if cb < 3:
                        pending.append(qk_process(bank, True, True, rope_base + t * 128, None,
                                                  qT[:, cb * 4:(cb + 1) * 4, key_base + t * 128:key_base + (t + 1) * 128], 'qT'))
                    elif cb == 3:
                        pending.append(qk_process(bank, False, rope_base is not None, (rope_base + t * 128) if rope_base is not None else 0, None,
                                                  kT[:, :, key_base + t * 128:key_base + (t + 1) * 128], 'kT'))
                    elif cb == 4:
                        evac_copy(Vt[:, kchunk, :], PS(bank), reads=['ps%d' % bank], writes=['V'])
                    else:
                        evac_copy(Ut[:, kchunk, :], PS(bank), reads=['ps%d' % bank], writes=['U'])
                    yield
            while pending:
                pending.pop(0)()

        groups = [
            (x_oth[0:512, :], 4, 0, 'gs0', 0, [3, 4, 5], 1024, 1024),
            (x_oth[512:1024, :], 4, 0, 'gs0', 0, [3, 4, 5], 1536, 1536),
            (ctx_in, 2, 1, 'gs1', 1, [3, 4], 2048, None),
            (x_own[0:512, :], 4, 0, 'gs0', 0, [3, 4, 5, 0, 1, 2], 0, 0),
            (x_own[512:1024, :], 4, 0, 'gs0', 0, [3, 4, 5, 0, 1, 2], 512, 512),
        ]

        def mk_norm(gi):
            src, nt, gsk, gskey, shr, blocks, kb, rb = groups[gi]
            return norm_gen(src, nt, gsk, gskey, 0, shr, HTB[gi % 2], 'hT%d' % (gi % 2), cfgB)

        for _ in mk_norm(0):
            pass
        for gi, g in enumerate(groups):
            src, nt, gsk, gskey, shr, blocks, kb, rb = g
            pg = project_gen(nt, gi % 2, blocks, kb, rb)
            ng = mk_norm(gi + 1) if gi + 1 < len(groups) else None
            nunits = nt * len(blocks)
            nsteps = 3 * groups[gi + 1][1] if ng is not None else 0
            stride = max(1, nunits // (nsteps + 1)) if nsteps else 0
            for ui, _ in enumerate(pg):
                if ng is not None and ui >= 1 and (ui - 1) % stride == 0:
                    try:
                        next(ng)
                    except StopIteration:
                        ng = None
            if ng is not None:
                for _ in ng:
                    pass

        if stop == 'C':
            dump('qT', qT.rearrange("p h n -> p (h n)"), ['qT'])
            dump('kT', kT.rearrange("p h n -> p (h n)"), ['kT'])
            dump('V', Vt.rearrange("p c n -> p (c n)"), ['V'])
            dump('U', Ut.rearrange("p c n -> p (c n)"), ['U'])
            return finish()

        f0 = fence(['hT%d_%d_%d' % (hb, t, kc) for hb in range(2) for t in range(4) for kc in range(16)] + ['junk', 'xn0', 'xn1', 'xt0', 'xt1']
                   + ['scr%d' % i for i in range(8)] + ['rope%d' % i for i in range(4)] + ['rope%ds' % i for i in range(4)])
        wcs = AV(178 * KB, 2 * KB, BF16).rearrange("p (g d) -> p g d", g=8)
        cc_t = SCR[3][:, 0:128]
        sc_t = SCR[3][:, 128:256]
        dma('sp', cc_t, cc_d[:, :], 'c_ccm', reads=[f0], writes=['ccm'])
        dma('sp', sc_t, sc_d[:, :], 'c_scm', reads=[f0], writes=['scm'])
        wf_t = SCR[4].rearrange("p (g d) -> p g d", g=4)
        dma('sp', wf_t, wf_d.rearrange("g c d -> c g d"), 'c_wf', reads=[f0], writes=['wfm'])
        for g in range(4):
            mm(PS(6)[:, g * 128:(g + 1) * 128], cc_t, wf_t[:, g, :], True, True, reads=['ccm', 'wfm'], writes=['ps6'])
        for g in range(4):
            mm(PS(7)[:, g * 128:(g + 1) * 128], sc_t, wf_t[:, g, :], True, True, reads=['scm', 'wfm'], writes=['ps7'])
        S.add('dve', lambda e: e.tensor_copy(out=wcs[:, 0:4, :], in_=PS(6).rearrange("p (g d) -> p g d", g=4)), reads=['ps6', f0], writes=['wcs'])
        S.add('dve', lambda e: e.tensor_copy(out=wcs[:, 4:8, :], in_=PS(7).rearrange("p (g d) -> p g d", g=4)), reads=['ps7', f0], writes=['wcsb'])

        ada_todo = list(range(8, 24))
        SCALE = float(HD) ** -0.5
        units = [(h, qg) for h in range(NQH) for qg in range(2)]
        steps = [(ui, ch) for ui in range(len(units)) for ch in range(18)]
        NPT = 4
        cur_ada = [None]
        SBANKS = [0, 1, 6]
        DACC = [AV(188 * KB, 2 * KB), AV(182 * KB, 2 * KB)]
        STPAD = AV(158 * KB, 4 * KB, BF16).rearrange("p (k m) -> p k m", k=16)
        S.add('dve', lambda e: e.memset(STPAD, 0.0), reads=[f0], writes=['stpad'])
        S.add('dve', lambda e: e.tensor_copy(out=STPAD[:, :, 0:2], in_=sT3), reads=['sT'], writes=['stpad'])

        def ada_tick():
            if cur_ada[0] is None:
                if not ada_todo:
                    return
                cur_ada[0] = ada_gen(ada_todo.pop(0), 7, 7, extra=[f0], padded=True)
            try:
                next(cur_ada[0])
            except StopIteration:
                cur_ada[0] = None

        def emit_S(k):
            ui, ch = steps[k]
            h, qg = units[ui]
            kh = h // 3
            sbank = SBANKS[k % 3]
            mm(PS(sbank), kT[:, kh, ch * 128:(ch + 1) * 128], qT[:, h, qg * 512:(qg + 1) * 512], True, True,
               reads=['kT', 'qT'], writes=['ps%d' % sbank])
            p = k % NPT
            S.add('act', lambda e, p=p, sbank=sbank: e.activation(out=PT[p], in_=PS(sbank), func=AF.Exp, scale=SCALE),
                  reads=['ps%d' % sbank, f0], writes=['pt%d' % p])

        emit_S(0)
        emit_S(1)
        emit_S(2)
        for k, (ui, ch) in enumerate(steps):
            h, qg = units[ui]
            kh = h // 3
            p = k % NPT
            ob = 2 + (ui % 2) * 2
            mm(PS(ob), Vt[:, ch, kh * 128:(kh + 1) * 128], PT[p], ch == 0, ch == 17, reads=['V', 'pt%d' % p], writes=['ps%d' % ob])
            da = ui % 2
            if ch % 2 == 1:
                mm(PS(ob + 1), ones_bf, PT[p], ch == 1, False, reads=['ones', 'pt%d' % p], writes=['ps%d' % (ob + 1)])
            elif ch == 0:
                wr_ = ['dacc%d' % da] + (['wfm', 'scm', 'ccm'] if ui < 2 else [])
                S.add('dve', lambda e, da=da, p=p: e.tensor_copy(out=DACC[da], in_=PT[p]), reads=['pt%d' % p, f0], writes=wr_)
            else:
                S.add('dve', lambda e, da=da, p=p: e.tensor_tensor(out=DACC[da], in0=DACC[da], in1=PT[p], op=ALU.add),
                      reads=['pt%d' % p, 'dacc%d' % da], writes=['dacc%d' % da])
            if ch == 17:
                mm(PS(ob + 1), ones_f, DACC[da], False, True, reads=['onesf', 'dacc%d' % da], writes=['ps%d' % (ob + 1)])
            if k + 3 < len(steps):
                emit_S(k + 3)
            ada_tick()
            if ch == 17:
                rd = ui % 2
                S.add('dve', lambda e, rd=rd, ob=ob: e.reciprocal(out=RDEN[rd], in_=PS(ob + 1)), reads=['ps%d' % (ob + 1), f0], writes=['rden%d' % rd])
                S.add('dve', lambda e, rd=rd, h=h, qg=qg, ob=ob: e.tensor_tensor(out=mixT[:, h, qg * 512:(qg + 1) * 512], in0=PS(ob), in1=RDEN[rd],
                                                                               op=ALU.mult),
                      reads=['ps%d' % ob, 'rden%d' % rd, f0], writes=['mixT_%d' % t_ for t_ in range(qg * 4, qg * 4 + 4)])
        while ada_todo or cur_ada[0] is not None:
            ada_tick()
        S.add('dve', lambda e: e.scalar_tensor_tensor(out=gs[:, 2, :], in0=modcol[:, 4, :, 0], scalar=1.0, in1=g2col,
                                                      op0=ALU.add, op1=ALU.mult),
              reads=['modcol4', 'g2col'], writes=['gs2'])

        ft_i = [0]
        for hq in range(2):
            for ch in range(16):
                fs = ft_i[0] % 6
                ft_i[0] += 1
                fk = 'ftab%d' % fs
                dma('pool', FTAB[fs][:, 0, :], cn_d[ch * 128:(ch + 1) * 128, hq * 512:(hq + 1) * 512], fk, reads=[f0], writes=[fk])
                dma('pool', FTAB[fs][:, 1, :], sn_d[ch * 128:(ch + 1) * 128, hq * 512:(hq + 1) * 512], fk + 's', reads=[f0], writes=[fk + 's'])
                for g in range(4):
                    mm(PS(g), Ut[:, ch, g * 128:(g + 1) * 128], FTAB[fs][:, 0, :], ch == 0, ch == 15,
                       reads=['U', fk], writes=['ps%d' % g])
                    mm(PS(4 + g), Ut[:, ch, g * 128:(g + 1) * 128], FTAB[fs][:, 1, :], ch == 0, ch == 15,
                       reads=['U', fk + 's'], writes=['ps%d' % (4 + g)])
            for i in range(8):
                evac_copy(ABT[:, i, :], PS(i), reads=['ps%d' % i, f0], writes=['abt%d' % i])
            for g in range(4):
                mm(PS(g), wcs[:, g, :], ABT[:, g, :], True, False, reads=['wcs', 'wcsb', 'abt%d' % g], writes=['ps%d' % g])
                mm(PS(g), wcs[:, 4 + g, :], ABT[:, 4 + g, :], False, True, reads=['wcs', 'wcsb', 'abt%d' % (4 + g)], writes=['ps%d' % g])
                evac_copy(mixT[:, 12 + g, hq * 512:(hq + 1) * 512], PS(g), reads=['ps%d' % g], writes=['mixT_%d' % t_ for t_ in range(hq * 4, hq * 4 + 4)])

        if stop == 'D':
            dump('mixT', mixT.rearrange("p k n -> p (k n)"), ['mixT_%d' % t_ for t_ in range(8)])
            dump('gt1', GT1, ['gt2_%d' % i for i in range(4)])
            return finish()

        w2_slot = {}
        W2C = [[AV(178 * KB + j * 4 * KB, 4 * KB, BF16) for j in range(4)],
               [AV(102 * KB + j * 4 * KB, 4 * KB, BF16) for j in range(3)] + [AV(162 * KB, 4 * KB, BF16)]]
        def load_w2(b, deps):
            s = b % 2
            w2_slot[b] = s
            r0 = b * 512
            if s == 0:
                o_ = AV(178 * KB, 16 * KB, BF16).rearrange("p (a n) -> p a n", a=4)
                dma('pool', o_, w2[r0:r0 + 512, :].rearrange("(a p) n -> p a n", p=128), 'w2s0', reads=list(deps), writes=['w2s0'] + ['w2g0_%d' % j for j in range(4)])
            else:
                o_ = AV(102 * KB, 12 * KB, BF16).rearrange("p (a n) -> p a n", a=3)

                def fn(e, o_=o_, r0=r0):
                    i1 = e.dma_start(out=o_, in_=w2[r0:r0 + 384, :].rearrange("(a p) n -> p a n", p=128))
                    i2 = e.dma_start(out=W2C[1][3], in_=w2[r0 + 384:r0 + 512, :])
                    return [i1, i2]
                S.add('pool', fn, reads=list(deps), writes=['w2s1'] + ['w2g1_%d' % j for j in range(4)], dma='w2s1', ndma=2)
            for j in range(4):
                S.add('pool', lambda e, s=s, j=j: e.tensor_tensor(out=W2C[s][j], in0=W2C[s][j], in1=GT2, op=ALU.mult),
                      reads=['w2s%d' % s] + ['gt5_%d' % i for i in range(4)], writes=['w2g%d_%d' % (s, j)])

        f1 = fence(['qT', 'kT', 'V', 'U', 'wcs', 'wcsb', 'ccm', 'scm', 'wfm', 'rden0', 'rden1', 'modrow0', 'modrow1', 'brow0', 'brow1']
                   + ['pt%d' % i for i in range(4)] + ['abt%d' % i for i in range(8)]
                   + ['ftab%d' % i for i in range(6)] + ['ftab%ds' % i for i in range(6)] + ['dacc0', 'dacc1', 'stpad'])
        for t in range(8):
            dma('sp', ACC[t], x_own[t * 128:(t + 1) * 128, :], 'acc%d' % t, reads=[f1], writes=ACCK(t))
        pe_i = [0]
        tm_i = [0]
        ng2 = {}
        TMPE = [AV(102 * KB + i * 2 * KB, 2 * KB) for i in range(4)]
        cfgF = dict(junk=AV(170 * KB, 4 * KB, BF16), xn=[AV(162 * KB, 4 * KB, BF16), AV(166 * KB, 4 * KB, BF16)], banks=[4, 5, 6, 7])
        wo_slot = {}

        def load_wo(cb):
            s_ = wb_next()
            wo_slot[cb] = s_
            dma('pool', WB[s_].rearrange("p (k n) -> p k n", k=16), w_out[:, cb * 512:(cb + 1) * 512].rearrange("(k p) n -> p k n", p=128),
                'wb%d' % s_, writes=['wb%d' % s_])

        load_wo(0)
        load_wo(1)
        for cb in range(4):
            s = wo_slot[cb]
            wv = WB[s].rearrange("p (k n) -> p k n", k=16)
            for t in range(8):
                bank = pe_i[0] % 3
                pe_i[0] += 1
                for mc in range(16):
                    mm(PS(bank), mixT[:, mc, t * 128:(t + 1) * 128], wv[:, mc, :], mc == 0, mc == 15,
                       reads=['mixT_%d' % t, 'wb%d' % s], writes=['ps%d' % bank])
                tm = tm_i[0] % 4
                tm_i[0] += 1
                S.add('dve', lambda e, tm=tm, bank=bank, cb=cb: e.tensor_tensor(out=TMPE[tm], in0=PS(bank), in1=GT1[:, cb * 512:(cb + 1) * 512],
                                                                                 op=ALU.mult),
                      reads=['ps%d' % bank, 'gt2_%d' % cb, f1], writes=['tmpE%d' % tm])
                S.add('dve', lambda e, tm=tm, t=t, cb=cb: e.tensor_tensor(out=ACC[t][:, cb * 512:(cb + 1) * 512],
                                                                          in0=ACC[t][:, cb * 512:(cb + 1) * 512], in1=TMPE[tm], op=ALU.add),
                      reads=['tmpE%d' % tm, 'acc%d_%d' % (t, cb)], writes=['acc%d_%d' % (t, cb)])
                if cb == 3:
                    ng2[t] = norm_gen(ACC, 8, 2, 'gs2', 3, 0, h2T, 'hT', cfgF, srckeys=[ACCK(t_) for t_ in range(8)],
                                      extra_reads=[f1], evac_extra_writes=lambda t_: ['mixT_%d' % t_], tiles=[t])
                    next(ng2[t])
                    if t >= 2:
                        next(ng2[t - 2])
                    next(ng2[t])
            if cb + 2 < 4:
                load_wo(cb + 2)
            if cb == 1:
                fmid = fence(['gt2_0', 'gt2_1'])
                load_w2(0, [f1, fmid])
        for t in (6, 7):
            next(ng2[t])

        if stop == 'E':
            for t in range(8):
                dump('acc%d' % t, ACC[t], ACCK(t))
            return finish()

        f2 = fence(['mixT_%d' % t_ for t_ in range(8)] + ['tmpE%d' % i for i in range(4)] + ['gt2_%d' % i for i in range(4)])
        if stop == 'F':
            dump('h2T', h2T.rearrange("p k n -> p (k n)"), ['hT_%d_%d' % (t, kc) for t in range(8) for kc in range(16)])
            return finish()

        f3 = fence(['junk', 'xn0', 'xn1'])
        NB = 16
        w1_slot = {}
        A2T = [AV(146 * KB, 8 * KB, BF16).rearrange("p (j n) -> p j n", j=4),
               AV(154 * KB, 8 * KB, BF16).rearrange("p (j n) -> p j n", j=4)]
        RSC = [AV(170 * KB, 2 * KB), AV(172 * KB, 2 * KB), AV(174 * KB, 2 * KB), AV(176 * KB, 2 * KB)]
        TMPG = [AV(194 * KB, 2 * KB), AV(196 * KB, 2 * KB)]

        def load_w1(b):
            s = wb_next()
            w1_slot[b] = s
            wv = WB[s].rearrange("p (k n) -> p k n", k=16)
            dma('pool', wv, w1[:, b * 512:(b + 1) * 512].rearrange("(k p) n -> p k n", p=128), 'wb%d' % s, writes=['wb%d' % s])

        am_i = [0]
        rs_i = [0]

        def stageA(b, j):
            s = w1_slot[b]
            wv = WB[s].rearrange("p (k n) -> p k n", k=16)
            bb = b % 2
            banks = [(am_i[0] % 2) * 2, (am_i[0] % 2) * 2 + 1]
            am_i[0] += 1
            for kc in range(16):
                for tg in range(2):
                    mm(PS(banks[tg]), wv[:, kc, j * 128:(j + 1) * 128], h2T[:, kc, tg * 512:(tg + 1) * 512], kc == 0, kc == 15,
                       reads=['wb%d' % s] + ['hT_%d_%d' % (tg * 4 + i, kc) for i in range(4)], writes=['ps%d' % banks[tg]])
            for tg in range(2):
                r = rs_i[0] % 4
                rs_i[0] += 1
                S.add('act', lambda e, r=r, bk=banks[tg]: e.activation(out=RSC[r], in_=PS(bk), func=AF.Relu),
                      reads=['ps%d' % banks[tg], f3], writes=['rsc%d' % r])
                S.add('act', lambda e, r=r, bb=bb, j=j, tg=tg: e.activation(out=A2T[bb][:, j, tg * 512:(tg + 1) * 512], in_=RSC[r], func=AF.Square),
                      reads=['rsc%d' % r, f3], writes=['a2t%d_%d' % (bb, j)])

        bn_i = [0]
        tg_i = [0]

        def stageB(b, t, cp):
            s = w2_slot[b]
            bb = b % 2
            banks = [4 + (bn_i[0] % 2) * 2, 5 + (bn_i[0] % 2) * 2]
            bn_i[0] += 1
            for j in range(4):
                for ci in range(2):
                    cb = cp * 2 + ci
                    mm(PS(banks[ci]), A2T[bb][:, j, t * 128:(t + 1) * 128], W2C[s][j][:, cb * 512:(cb + 1) * 512], j == 0, j == 3,
                       reads=['a2t%d_%d' % (bb, j), 'w2g%d_%d' % (s, j)], writes=['ps%d' % banks[ci]])
            for ci in range(2):
                cb = cp * 2 + ci
                S.add('dve', lambda e, bk=banks[ci], t=t, cb=cb: e.tensor_tensor(out=ACC[t][:, cb * 512:(cb + 1) * 512], in0=PS(bk),
                                                                                in1=ACC[t][:, cb * 512:(cb + 1) * 512], op=ALU.add),
                      reads=['ps%d' % banks[ci], 'acc%d_%d' % (t, cb), f2], writes=['acc%d_%d' % (t, cb)])

        ADD_ENG = 'pool'
        load_w1(0)
        load_w1(1)
        load_w2(1, [f1, f2, f3])
        for j in range(4):
            stageA(0, j)
        for b in range(NB):
            bunits = [(t, cp) for t in range(8) for cp in range(2)]
            aunits = list(range(4)) if b + 1 < NB else []
            for i, (t, cp) in enumerate(bunits):
                if i % 4 == 0 and aunits:
                    stageA(b + 1, aunits.pop(0))
                stageB(b, t, cp)
            if b + 2 < NB:
                load_w1(b + 2)
                load_w2(b + 2, [f1, f2, f3])

        if stop == 'G':
            for t in range(8):
                dump('acc%d' % t, ACC[t], ACCK(t))
            return finish()

        f4 = fence(['hT_%d_%d' % (t, kc) for t in range(8) for kc in range(16)] + ['wb0', 'wb1'])
        dma('sp', FGB, fg_d[:, :], 'c_fg', reads=[f4], writes=['fgb'])
        JH = AV(114 * KB, 4 * KB, BF16)
        for t in range(8):
            c = newstat(3)
            S.add('act', lambda e, t=t, c=c: e.activation(out=JH, in_=ACC[t], func=AF.Square, accum_out=stat[:, c:c + 1]),
                  reads=ACCK(t) + [f4], writes=['junkH', 'st%d' % c])
            S.add('act', lambda e, c=c: e.activation(out=stat[:, c + 1:c + 2], in_=stat[:, c:c + 1], func=AF.Ln, scale=1.0 / D, bias=EPS),
                  reads=['st%d' % c], writes=['st%d' % (c + 1)])
            S.add('act', lambda e, c=c: e.activation(out=stat[:, c + 2:c + 3], in_=stat[:, c + 1:c + 2], func=AF.Exp, scale=-0.5),
                  reads=['st%d' % (c + 1)], writes=['st%d' % (c + 2)])
            S.add('dve', lambda e, t=t, c=c: e.scalar_tensor_tensor(out=ACC[t], in0=ACC[t], scalar=stat[:, c + 2:c + 3], in1=FGB,
                                                                   op0=ALU.mult, op1=ALU.mult),
                  reads=ACCK(t) + ['st%d' % (c + 2), 'fgb'], writes=ACCK(t))
            dma('sp', out_d[t * 128:(t + 1) * 128, :], ACC[t], 'out%d' % t, reads=ACCK(t), writes=['outd%d' % t])
        S.add('sp', None, reads=['outd%d' % t for t in range(8)] + ['dbgo_' + n for n in dbg_out])
        S.emit(nc, st)
    return nc


_CONST_CACHE = {}


def _consts():
    if _CONST_CACHE:
        return _CONST_CACHE
    f = np.float32
    t = np.arange(SEQ)
    row = (t // 64).astype(np.float64)
    col = (t % 64).astype(np.float64)
    inv = 10000.0 ** (-np.arange(0, 64, 2, dtype=np.float64) / 64.0)
    ar = row[:, None] * inv[None, :]
    ac = col[:, None] * inv[None, :]
    cos = np.concatenate([np.cos(ar), np.cos(ar), np.cos(ac), np.cos(ac)], axis=1).astype(f)
    sin = np.concatenate([-np.sin(ar), np.sin(ar), -np.sin(ac), np.sin(ac)], axis=1).astype(f)
    n = np.arange(SEQ, dtype=np.int64)
    ang = 2.0 * np.pi * ((n[:, None] * n[None, :]) % SEQ).astype(np.float64) / SEQ
    cn = np.cos(ang).astype(f)
    sn = np.sin(ang).astype(f)
    c = np.arange(128, dtype=np.int64)
    angc = 2.0 * np.pi * ((c[:, None] * c[None, :]) % 128).astype(np.float64) / 128
    cc = (np.cos(angc) / 512.0).astype(f)
    sc = (-np.sin(angc) / 512.0).astype(f)
    sel3 = np.zeros((3, 128), f)
    sel3[0] = 1.0
    sel3[2] = 1.0
    r3 = np.array([[1, 0], [0, 1], [1, 1]], f)
    _CONST_CACHE.update(cos=cos, sin=sin, cn=cn, sn=sn, cc=cc, sc=sc, sel3=sel3, r3=r3, ident=np.eye(128, dtype=f))
    return _CONST_CACHE


_PROG = {}


def _in_maps(x, c, ctx, c_ctx, w_ada, b_ada, norm1_g, w_in, q_norm_g, k_norm_g, w_fourier,
             w_out, norm2_g, w_mlp1, w_mlp2, final_norm_g):
    K = _consts()
    f = np.float32
    A = lambda a: np.ascontiguousarray(np.asarray(a, dtype=f))
    x, c, ctx, c_ctx = A(x), A(c), A(ctx), A(c_ctx)
    shared = dict(
        w_ada=A(w_ada[0]), b_ada=A(b_ada[0]).reshape(1, -1),
        g1col=A(np.asarray(norm1_g[0]).reshape(16, 128).T), g2col=A(np.asarray(norm2_g[0]).reshape(16, 128).T),
        w_in=A(w_in[0]), qg_bc=A(np.broadcast_to(np.asarray(q_norm_g[0])[None, :], (128, 128))),
        kg_bc=A(np.broadcast_to(np.asarray(k_norm_g[0])[None, :], (128, 128))),
        w_f=A(w_fourier[0]), w_out=A(w_out[0]), w1=A(w_mlp1[0]), w2=A(w_mlp2[0]),
        fg_bc=A(np.broadcast_to(np.asarray(final_norm_g)[None, :], (128, D))),
        ident=K['ident'], sel3=K['sel3'], r3=K['r3'], dft_cc=K['cc'], dft_sc=K['sc'],
    )
    maps = []
    for core in range(8):
        b, half = core // 2, core % 2
        own = slice(half * NOWN, (half + 1) * NOWN)
        oth = slice((1 - half) * NOWN, (2 - half) * NOWN)
        order = np.concatenate([np.arange(own.start, own.stop), np.arange(oth.start, oth.stop)])
        cpair = np.stack([c[b], c_ctx], axis=0)
        ccol = A(cpair.reshape(2, 16, 128).transpose(2, 1, 0).reshape(128, 32))
        m = dict(shared)
        m.update(
            x_own=A(x[b, own]), x_oth=A(x[b, oth]), ctx_b=A(ctx[b]), ccol=ccol,
            rope_cos=A(K['cos'][order]), rope_sin=A(K['sin'][order]),
            dft_cn=A(K['cn'][order][:, own]), dft_sn=A(K['sn'][order][:, own]),
        )
        maps.append(m)
    return maps


def kernel(**inputs):
    if 'nc' not in _PROG:
        _PROG['nc'] = build_program()
    nc = _PROG['nc']
    maps = _in_maps(**inputs)
    res = run_bass_kernel_spmd(nc, maps, core_ids=list(range(8)))
    out = np.zeros((BATCH, SEQ, D), np.float32)
    for core in range(8):
        b, half = core // 2, core % 2
        out[b, half * NOWN:(half + 1) * NOWN] = res.results[core]["out"]
    return out
```

```python
import numpy as np
from contextlib import ExitStack
import concourse.bass as bass
import concourse.mybir as mybir
from concourse.bass_utils import run_bass_kernel_spmd

F32 = mybir.dt.float32
BF16 = mybir.dt.bfloat16
ALU = mybir.AluOpType
AF = mybir.ActivationFunctionType
AX = mybir.AxisListType

D = 2048
SEQ = 2048
BATCH = 4
CTX = 256
NOWN = 1024
HD = 128
NQH = 12
NKV = 4
DFF = 8192
EPS = 1e-6
NKEY = 2304
ENGS = ('pe', 'act', 'dve', 'pool', 'sp')


class Sched:
    def __init__(self):
        self.ops = []
        self.neng = {e: 0 for e in ENGS}
        self.lastw = {}
        self.readers = {}
        self.dma_cnt = {}

    def add(self, eng, fn, reads=(), writes=(), dma=None, ndma=1):
        idx = self.neng[eng]
        self.neng[eng] += 1
        deps = set()
        for k in reads:
            if k in self.lastw:
                deps.add(self.lastw[k])
        for k in writes:
            if k in self.lastw:
                deps.add(self.lastw[k])
            for ev in self.readers.get(k, ()):
                deps.add(ev)
        if dma is not None:
            self.dma_cnt[dma] = self.dma_cnt.get(dma, 0) + 16 * ndma
            ev = ('d', dma, self.dma_cnt[dma])
        else:
            ev = ('c', eng, idx)
        deps.discard(ev)
        for k in writes:
            self.lastw[k] = ev
            self.readers[k] = []
        for k in reads:
            if k not in writes:
                self.readers.setdefault(k, []).append(ev)
        self.ops.append(dict(eng=eng, idx=idx, fn=fn, deps=deps, ev=ev, dma=dma))
        return ev

    def plan(self):
        known = {e: {} for e in ENGS}
        snap = {}
        signal = set()
        for op in self.ops:
            F = op['eng']
            kn = known[F]
            waits = []
            for d in sorted(op['deps'], key=lambda t: (t[0], str(t[1]), t[2])):
                if d[0] == 'c':
                    E, i = d[1], d[2]
                    if E == 'pe' and F == 'pe':
                        continue
                    if kn.get(('c', E), -1) >= i:
                        continue
                    waits.append(d)
                else:
                    if kn.get(('d', d[1]), -1) >= d[2]:
                        continue
                    waits.append(d)
            for d in waits:
                if d[0] == 'c':
                    signal.add(d)
                    kn[('c', d[1])] = max(kn.get(('c', d[1]), -1), d[2])
                else:
                    kn[('d', d[1])] = max(kn.get(('d', d[1]), -1), d[2])
                for k, v in snap.get(d, {}).items():
                    if kn.get(k, -1) < v:
                        kn[k] = v
            op['waits'] = waits
            snap[op['ev']] = dict(kn)
        cnt = {e: 0 for e in ENGS}
        self.sigval = {}
        for op in self.ops:
            if op['dma'] is None and op['ev'] in signal:
                cnt[op['eng']] += 1
                self.sigval[op['ev']] = cnt[op['eng']]
                op['signal'] = True
            else:
                op['signal'] = False
        return cnt

    def emit(self, nc, stack):
        self.plan()
        esem = {e: stack.enter_context(nc.semaphore('s_' + e)) for e in ENGS}
        dsem = {}
        for k in self.dma_cnt:
            dsem[k] = stack.enter_context(nc.semaphore('d_' + str(k)))
        per = {e: [o for o in self.ops if o['eng'] == e] for e in ENGS}
        sigval = self.sigval

        def run(engname, eng):
            for op in per[engname]:
                for d in op['waits']:
                    if d[0] == 'c':
                        eng.wait_ge(esem[d[1]], sigval[d])
                    else:
                        eng.wait_ge(dsem[d[1]], d[2])
                if op['fn'] is None:
                    continue
                ins = op['fn'](eng)
                if op['dma'] is not None:
                    if not isinstance(ins, (list, tuple)):
                        ins = [ins]
                    for i_ in ins:
                        i_.then_inc(dsem[op['dma']], 16)
                elif op['signal']:
                    ins.then_inc(esem[engname], 1)

        with nc.Block() as block:
            @block.tensor
            def _(e):
                run('pe', e)

            @block.scalar
            def _(e):
                run('act', e)

            @block.vector
            def _(e):
                run('dve', e)

            @block.gpsimd
            def _(e):
                run('pool', e)

            @block.sync
            def _(e):
                run('sp', e)


KB = 1024
ARENA_BYTES = 206 * KB


def build_program(dbg=None, stop=None):
    nc = bass.Bass("TRN2", target_bir_lowering=False)
    S = Sched()

    def din(name, shape, dt=F32):
        return nc.dram_tensor(name, list(shape), dt, kind="ExternalInput").ap()

    x_own = din("x_own", [NOWN, D])
    x_oth = din("x_oth", [NOWN, D])
    ctx_in = din("ctx_b", [CTX, D])
    ccol = din("ccol", [128, 32])
    w_ada = din("w_ada", [D, 6 * D])
    g1col_d = din("g1col", [128, 16])
    g2col_d = din("g2col", [128, 16])
    w_in = din("w_in", [D, 3072])
    qg_d = din("qg_bc", [128, 128])
    kg_d = din("kg_bc", [128, 128])
    wf_d = din("w_f", [4, 128, 128])
    w_out = din("w_out", [D, D])
    w1 = din("w1", [D, DFF])
    w2 = din("w2", [DFF, D])
    fg_d = din("fg_bc", [128, D])
    ident_d = din("ident", [128, 128])
    sel_d = din("sel3", [3, 128])
    r3_d = din("r3", [3, 2])
    b_ada_d = din("b_ada", [1, 6 * D])
    cos_d = din("rope_cos", [SEQ, 128])
    sin_d = din("rope_sin", [SEQ, 128])
    cn_d = din("dft_cn", [SEQ, NOWN])
    sn_d = din("dft_sn", [SEQ, NOWN])
    cc_d = din("dft_cc", [128, 128])
    sc_d = din("dft_sc", [128, 128])
    out_d = nc.dram_tensor("out", [NOWN, D], F32, kind="ExternalOutput").ap()
    dbg_out = {}
    if dbg:
        for name, shape in dbg.items():
            dbg_out[name] = nc.dram_tensor("dbg_" + name, list(shape), F32, kind="ExternalOutput").ap()

    with ExitStack() as st:
        arena = st.enter_context(nc.sbuf_tensor("arena", [128, ARENA_BYTES // 4], F32))
        psum = st.enter_context(nc.psum_tensor("psum", [128, 4096], F32))

        def AV(off, nbytes, dt=F32):
            assert off % 4 == 0 and nbytes % 4 == 0 and off + nbytes <= ARENA_BYTES
            a = arena[:, off // 4:(off + nbytes) // 4]
            if dt == BF16:
                a = a.bitcast(BF16)
            return a

        def PS(bank):
            return psum[:, bank * 512:(bank + 1) * 512]

        def PSB(bank):
            return psum[:, bank * 512:(bank + 1) * 512].bitcast(BF16)

        o = 0
        identf = AV(o, 512); o += 512
        ident = AV(o, 256, BF16); o += 256
        ones_bf = AV(o, 256, BF16); o += 256
        qg_bc = AV(o, 512); o += 512
        kg_bc = AV(o, 512); o += 512
        g1col = AV(o, 64); o += 64
        g2col = AV(o, 64); o += 64
        ccol_t = AV(o, 128); o += 128
        sT = AV(o, 64, BF16); o += 64
        modcol = AV(o, 768).rearrange("p (m j r) -> p m j r", m=6, j=16); o += 768
        gs = AV(o, 192).rearrange("p (k j) -> p k j", k=3); o += 192
        stat = AV(o, 1408); o += 1408
        assert o <= 6 * KB
        sel3 = AV(o, 512)[0:3, :]; o += 512
        r3 = AV(o, 8)[0:3, :]; o += 8
        ones_f = AV(o, 512); o += 512
        modrow = [AV(184 * KB, 2 * KB)[0:3, :], AV(186 * KB, 2 * KB)[0:3, :]]

        WB = [AV(6 * KB, 16 * KB, BF16), AV(22 * KB, 16 * KB, BF16)]
        qT = AV(38 * KB, 24 * KB, BF16).rearrange("p (h n) -> p h n", h=NQH)
        kT = AV(62 * KB, 18 * KB, BF16).rearrange("p (h n) -> p h n", h=NKV)
        Vt = AV(80 * KB, 18 * KB, BF16).rearrange("p (c n) -> p c n", c=18)
        Ut = AV(98 * KB, 16 * KB, BF16).rearrange("p (c n) -> p c n", c=16)
        ACC = [AV(38 * KB + t * 8 * KB, 8 * KB) for t in range(8)]
        hT = AV(114 * KB, 32 * KB, BF16).rearrange("p (k n) -> p k n", k=16)
        mixT = hT
        h2T = hT
        XT = [AV(146 * KB, 8 * KB), AV(154 * KB, 8 * KB)]
        XN = [AV(162 * KB, 4 * KB, BF16), AV(166 * KB, 4 * KB, BF16)]
        ROPE = [AV(170 * KB + i * KB, KB).rearrange("p (a n) -> p a n", a=2) for i in range(4)]
        SCR = [AV(174 * KB + i * 2 * KB, 2 * KB) for i in range(8)]
        GT1 = AV(190 * KB, 8 * KB)
        GT2 = AV(198 * KB, 8 * KB)
        JUNK = AV(190 * KB, 4 * KB, BF16)
        FTAB = [AV(146 * KB + i * 2 * KB, 2 * KB, BF16).rearrange("p (a n) -> p a n", a=2) for i in range(6)]
        RDEN = [AV(162 * KB, 2 * KB), AV(164 * KB, 2 * KB)]
        ABT = AV(166 * KB, 8 * KB, BF16).rearrange("p (g n) -> p g n", g=8)
        PT = [AV(174 * KB + i * KB, KB, BF16) for i in range(4)]
        W1S = WB
        A2T = [AV(146 * KB, 16 * KB, BF16).rearrange("p (j n) -> p j n", j=8),
               AV(162 * KB, 16 * KB, BF16).rearrange("p (j n) -> p j n", j=8)]
        W2S = [AV(178 * KB, 8 * KB, BF16).rearrange("p (a n) -> p a n", a=2),
               AV(186 * KB, 8 * KB, BF16).rearrange("p (a n) -> p a n", a=2),
               AV(102 * KB, 8 * KB, BF16).rearrange("p (a n) -> p a n", a=2)]
        RSC = [AV(110 * KB, 2 * KB), AV(112 * KB, 2 * KB)]
        TMPG = [AV(194 * KB, 2 * KB), AV(196 * KB, 2 * KB)]
        FGB = AV(6 * KB, 8 * KB)

        statc = {3: [0, 0, 32], 12: [0, 96, 13], 1: [0, 252, 100]}

        def newstat(n=1):
            ent = statc[n]
            c = ent[1] + (ent[0] % ent[2]) * n
            ent[0] += 1
            assert c + n <= 352
            return c

        ACCK = lambda t: ['acc%d_%d' % (t, cb) for cb in range(4)]
        fence_id = [0]

        def fence(old_keys, eng='dve'):
            fence_id[0] += 1
            name = 'fence%d' % fence_id[0]
            c = newstat()
            S.add(eng, lambda e, c=c: e.memset(stat[:, c:c + 1], 0.0), writes=list(old_keys) + [name])
            return name

        def dma(eng, out, in_, key, reads=(), writes=(), noncontig=False):
            def fn(e, out=out, in_=in_):
                if noncontig:
                    with nc.allow_non_contiguous_dma(reason="layout"):
                        return e.dma_start(out=out, in_=in_)
                return e.dma_start(out=out, in_=in_)
            return S.add(eng, fn, reads=reads, writes=writes, dma=key)

        def mm(out, lhsT, rhs, start, stop, reads, writes):
            S.add('pe', lambda e, o_=out, l=lhsT, r=rhs, a=start, b=stop: e.matmul(o_, lhsT=l, rhs=r, start=a, stop=b),
                  reads=reads, writes=writes)

        def tr(out, in_, idt, reads, writes):
            S.add('pe', lambda e, o_=out, i_=in_, d_=idt: e.transpose(out=o_, in_=i_, identity=d_), reads=reads, writes=writes)

        def dbg_dump(name, ap, reads, eng='sp'):
            if name in dbg_out:
                dma(eng, dbg_out[name], ap, 'dbg_' + name, reads=reads, writes=['dbgo_' + name])

        def finish():
            S.add('sp', None, reads=['dbgo_' + n for n in dbg_out])
            S.emit(nc, st)
            return nc

        def dump(name, ap, reads):
            if name in dbg_out:
                dma('pool', dbg_out[name], ap, 'dbg_' + name, reads=reads, writes=['dbgo_' + name])

        dma('sp', identf, ident_d[:, :], 'c_identf', writes=['identf'])
        dma('pool', ident, ident_d[:, :], 'c_ident', writes=['ident'])
        dma('sp', sel3, sel_d[:, :], 'c_sel', writes=['sel'])
        dma('sp', r3, r3_d[:, :], 'c_r3', writes=['r3'])
        dma('sp', qg_bc, qg_d[:, :], 'c_qg', writes=['qg'])
        dma('sp', kg_bc, kg_d[:, :], 'c_kg', writes=['kg'])
        dma('sp', g1col, g1col_d[:, :], 'c_g1', writes=['g1col'])
        dma('sp', g2col, g2col_d[:, :], 'c_g2', writes=['g2col'])
        dma('sp', ccol_t, ccol[:, :], 'c_cc', writes=['ccol'])
        S.add('dve', lambda e: e.memset(ones_bf, 1.0), writes=['ones'])
        S.add('dve', lambda e: e.memset(ones_f, 1.0), writes=['onesf'])
        S.add('act', lambda e: e.activation(out=sT, in_=ccol_t, func=AF.Silu), reads=['ccol'], writes=['sT'])
        sT3 = sT.rearrange("p (k r) -> p k r", r=2)

        wb_i = [0]

        def wb_next():
            s = wb_i[0] % 2
            wb_i[0] += 1
            return s

        ada_i = [0]

        def ada_gen(j, psA, psB, extra=(), padded=False):
            m, jb = j // 4, j % 4
            s = wb_next()
            r = ada_i[0] % 2
            ada_i[0] += 1
            wv = WB[s].rearrange("p (k n) -> p k n", k=16)
            dma('pool', wv, w_ada[:, j * 512:(j + 1) * 512].rearrange("(k p) n -> p k n", p=128), 'wb%d' % s, writes=['wb%d' % s])
            dma('sp', modrow[r][2:3, :], b_ada_d[:, j * 512:(j + 1) * 512], 'brow%d' % r, reads=list(extra), writes=['brow%d' % r])
            yield
            for kc in range(16):
                if padded:
                    mm(PS(psA), STPAD[:, kc, :], wv[:, kc, :], kc == 0, kc == 15,
                       reads=['stpad', 'wb%d' % s], writes=['ps%d' % psA])
                else:
                    mm(PS(psA)[0:2, :], sT3[:, kc, :], wv[:, kc, :], kc == 0, kc == 15,
                       reads=['sT', 'wb%d' % s], writes=['ps%d' % psA])
                yield
            S.add('dve', lambda e, r=r: e.tensor_copy(out=modrow[r][0:2, :], in_=PS(psA)[0:2, :]),
                  reads=['ps%d' % psA] + list(extra), writes=['modrow%d' % r])
            if m in (0, 1, 3, 4):
                for q in range(4):
                    mm(PS(psB)[:, q * 2:(q + 1) * 2], modrow[r][:, q * 128:(q + 1) * 128], r3, True, True,
                       reads=['modrow%d' % r, 'brow%d' % r, 'r3'], writes=['ps%d' % psB])
                S.add('dve', lambda e, m=m, jb=jb: e.tensor_copy(
                    out=modcol[:, m, jb * 4:(jb + 1) * 4, :], in_=PS(psB)[:, 0:8].rearrange("p (q r) -> p q r", r=2)),
                    reads=['ps%d' % psB], writes=['modcol%d' % m])
            else:
                gt = GT1 if m == 2 else GT2
                mm(PS(psB), sel3, modrow[r], True, True, reads=['modrow%d' % r, 'brow%d' % r, 'sel'], writes=['ps%d' % psB])
                S.add('dve', lambda e, gt=gt, jb=jb: e.tensor_copy(out=gt[:, jb * 512:(jb + 1) * 512], in_=PS(psB)),
                      reads=['ps%d' % psB] + list(extra), writes=['gt%d_%d' % (m, jb)])
            yield

        def ada_block(j, psA, psB, extra=()):
            for _ in ada_gen(j, psA, psB, extra):
                pass

        for j in range(8):
            ada_block(j, 0, 1)
        S.add('dve', lambda e: e.scalar_tensor_tensor(out=gs[:, 0, :], in0=modcol[:, 1, :, 0], scalar=1.0, in1=g1col,
                                                      op0=ALU.add, op1=ALU.mult),
              reads=['modcol1', 'g1col'], writes=['gs0'])
        S.add('dve', lambda e: e.scalar_tensor_tensor(out=gs[:, 1, :], in0=modcol[:, 1, :, 1], scalar=1.0, in1=g1col,
                                                      op0=ALU.add, op1=ALU.mult),
              reads=['modcol1', 'g1col'], writes=['gs1'])

        if stop == 'A':
            dump('modcol', modcol.rearrange("p m j r -> p (m j r)"), ['modcol0', 'modcol1'])
            dump('gs', gs.rearrange("p k j -> p (k j)"), ['gs0', 'gs1'])
            return finish()

        xt_i = [0]
        tb_i = [0]
        rope_i = [0]
        scr_i = [0]
        pc_i = [0]
        pq_i = [0]
        ev_i = [0]

        def scr_next():
            s = scr_i[0] % 8
            scr_i[0] += 1
            return s

        def norm_gen(src, ntiles, gsk, gskey, shm, shr, dst, dstkey, cfg, srckeys=None, extra_reads=(), evac_extra_writes=None, tiles=None):
            for t in (tiles if tiles is not None else range(ntiles)):
                if srckeys is None:
                    xs = xt_i[0] % 2
                    xt_i[0] += 1
                    xt = XT[xs]
                    xkey = 'xt%d' % xs
                    dma('sp', xt, src[t * 128:(t + 1) * 128, :], xkey, writes=[xkey])
                    xkeys = [xkey]
                else:
                    xt = src[t]
                    xkeys = list(srckeys[t])
                c = newstat(3)
                S.add('act', lambda e, xt=xt, c=c: e.activation(out=cfg['junk'], in_=xt, func=AF.Square, accum_out=stat[:, c:c + 1]),
                      reads=xkeys + list(extra_reads), writes=['junk', 'st%d' % c])
                S.add('act', lambda e, c=c: e.activation(out=stat[:, c + 1:c + 2], in_=stat[:, c:c + 1], func=AF.Ln,
                                                         scale=1.0 / D, bias=EPS),
                      reads=['st%d' % c], writes=['st%d' % (c + 1)])
                S.add('act', lambda e, c=c: e.activation(out=stat[:, c + 2:c + 3], in_=stat[:, c + 1:c + 2], func=AF.Exp, scale=-0.5),
                      reads=['st%d' % (c + 1)], writes=['st%d' % (c + 2)])
                yield
                xs2 = t % 2
                XNc = cfg['xn']
                S.add('dve', lambda e, xt=xt, c=c, xs2=xs2, XNc=XNc: e.tensor_scalar(out=XNc[xs2], in0=xt, scalar1=stat[:, c + 2:c + 3],
                                                                           scalar2=None, op0=ALU.mult),
                      reads=xkeys + ['st%d' % (c + 2)] + list(extra_reads), writes=['xn%d' % xs2])
                yield
                for half in range(2):
                    bank = cfg['banks'][tb_i[0] % len(cfg['banks'])]
                    tb_i[0] += 1
                    for k8 in range(8):
                        kc = half * 8 + k8
                        tr(PSB(bank)[:, k8 * 128:(k8 + 1) * 128], XNc[xs2][:, kc * 128:(kc + 1) * 128], ident,
                           reads=['xn%d' % xs2, 'ident'], writes=['ps%d' % bank])
                    for k8 in range(8):
                        kc = half * 8 + k8
                        eng = 'dve' if half == 0 else 'act'
                        o_ = dst[:, kc, t * 128:(t + 1) * 128]
                        i_ = PSB(bank)[:, k8 * 128:(k8 + 1) * 128]
                        sc_ = gs[:, gsk, kc:kc + 1]
                        sh_ = modcol[:, shm, kc:kc + 1, shr]
                        if eng == 'dve':
                            S.add('dve', lambda e, o_=o_, i_=i_, sc_=sc_, sh_=sh_: e.tensor_scalar(
                                out=o_, in0=i_, scalar1=sc_, scalar2=sh_, op0=ALU.mult, op1=ALU.add),
                                reads=['ps%d' % bank, gskey, 'modcol%d' % shm] + list(extra_reads), writes=['%s_%d_%d' % (dstkey, t, kc)] + (evac_extra_writes(t) if evac_extra_writes else []))
                        else:
                            S.add('act', lambda e, o_=o_, i_=i_, sc_=sc_, sh_=sh_: e.activation(
                                out=o_, in_=i_, func=AF.Identity, scale=sc_, bias=sh_),
                                reads=['ps%d' % bank, gskey, 'modcol%d' % shm] + list(extra_reads), writes=['%s_%d_%d' % (dstkey, t, kc)] + (evac_extra_writes(t) if evac_extra_writes else []))
                yield

        def norm_set(*a, **kw):
            for _ in norm_gen(*a, **kw):
                pass

        cfgB = dict(junk=JUNK, xn=XN, banks=[2, 3, 4, 5])
        CBANKS = [6, 7]
        QBANKS = [0, 1]

        def evac_copy(out, in_, reads, writes):
            eng = 'act' if ev_i[0] % 2 == 0 else 'dve'
            ev_i[0] += 1
            if eng == 'act':
                S.add('act', lambda e, o_=out, i_=in_: e.copy(out=o_, in_=i_), reads=reads, writes=writes)
            else:
                S.add('dve', lambda e, o_=out, i_=in_: e.tensor_copy(out=o_, in_=i_), reads=reads, writes=writes)

        def qk_process(bank, is_q, use_rope, rope_src, tok0, dst, dstkey):
            pk = 'ps%d' % bank
            gbc = qg_bc if is_q else kg_bc
            gk = 'qg' if is_q else 'kg'
            s_sq, s_xn, s_a, s_b = scr_next(), scr_next(), scr_next(), scr_next()
            c = newstat(12)
            S.add('act', lambda e: e.activation(out=SCR[s_sq], in_=PS(bank), func=AF.Square), reads=[pk], writes=['scr%d' % s_sq])
            S.add('dve', lambda e: e.tensor_reduce(out=stat[:, c:c + 4], in_=SCR[s_sq].rearrange("p (h d) -> p h d", h=4),
                                                   axis=AX.X, op=ALU.add),
                  reads=['scr%d' % s_sq], writes=['st%d' % c])
            S.add('act', lambda e: e.activation(out=stat[:, c + 4:c + 8], in_=stat[:, c:c + 4], func=AF.Ln, scale=1.0 / HD, bias=EPS),
                  reads=['st%d' % c], writes=['st%d' % (c + 4)])
            S.add('act', lambda e: e.activation(out=stat[:, c + 8:c + 12], in_=stat[:, c + 4:c + 8], func=AF.Exp, scale=-0.5),
                  reads=['st%d' % (c + 4)], writes=['st%d' % (c + 8)])
            x3 = lambda ap: ap.rearrange("p (h d) -> p h d", h=4)
            S.add('dve', lambda e: e.tensor_tensor(out=x3(SCR[s_xn]), in0=x3(PS(bank)),
                                                   in1=stat[:, c + 8:c + 12].unsqueeze(2).to_broadcast([128, 4, 128]), op=ALU.mult),
                  reads=[pk, 'st%d' % (c + 8)], writes=['scr%d' % s_xn])
            if use_rope:
                S.add('dve', lambda e: e.tensor_tensor(out=x3(SCR[s_a]), in0=x3(SCR[s_xn]),
                                                       in1=gbc.unsqueeze(1).to_broadcast([128, 4, 128]), op=ALU.mult),
                      reads=['scr%d' % s_xn, gk], writes=['scr%d' % s_a])
                rs = rope_i[0] % 4
                rope_i[0] += 1
                rk = 'rope%d' % rs
                dma('sp', ROPE[rs][:, 0, :], cos_d[rope_src:rope_src + 128, :], rk, writes=[rk])
                dma('sp', ROPE[rs][:, 1, :], sin_d[rope_src:rope_src + 128, :], rk + 's', writes=[rk + 's'])
                x5 = lambda ap: ap.rearrange("p (h a f d) -> p h a f d", h=4, a=2, f=2)
                sin4 = ROPE[rs][:, 1, :].rearrange("p (a f d) -> p a f d", a=2, f=2)
                for f in range(2):
                    S.add('dve', lambda e, f=f: e.tensor_tensor(
                        out=x5(SCR[s_b])[:, :, :, f, :], in0=x5(SCR[s_a])[:, :, :, 1 - f, :],
                        in1=sin4[:, :, f, :].unsqueeze(1).to_broadcast([128, 4, 2, 32]), op=ALU.mult),
                        reads=['scr%d' % s_a, rk + 's'], writes=['scr%d' % s_b] if f == 0 else ['scr%d' % s_b, 'dummy_b'])
                S.add('dve', lambda e: e.tensor_tensor(out=x3(SCR[s_xn]), in0=x3(SCR[s_a]),
                                                       in1=ROPE[rs][:, 0, :].unsqueeze(1).to_broadcast([128, 4, 128]), op=ALU.mult),
                      reads=['scr%d' % s_a, rk], writes=['scr%d' % s_xn])
                xb = SCR[s_a].bitcast(BF16)[:, 0:512]
                S.add('dve', lambda e: e.tensor_tensor(out=xb, in0=SCR[s_xn], in1=SCR[s_b], op=ALU.add),
                      reads=['scr%d' % s_xn, 'scr%d' % s_b], writes=['scr%d' % s_a])
            else:
                xb = SCR[s_a].bitcast(BF16)[:, 0:512]
                S.add('dve', lambda e: e.tensor_tensor(out=xb.rearrange("p (h d) -> p h d", h=4), in0=x3(SCR[s_xn]),
                                                       in1=gbc.unsqueeze(1).to_broadcast([128, 4, 128]), op=ALU.mult),
                      reads=['scr%d' % s_xn, gk], writes=['scr%d' % s_a])
            def back():
                qb = QBANKS[pq_i[0] % 2]
                pq_i[0] += 1
                for h in range(4):
                    tr(PSB(qb)[:, h * 128:(h + 1) * 128], xb[:, h * 128:(h + 1) * 128], ident,
                       reads=['scr%d' % s_a, 'ident'], writes=['ps%d' % qb])
                evac_copy(dst, PSB(qb)[:, 0:512].rearrange("p (h n) -> p h n", h=4), reads=['ps%d' % qb], writes=[dstkey])
            return back

        PEND_DEPTH = 2
        PENDING = []
        HTB = [AV(114 * KB, 16 * KB, BF16).rearrange("p (k n) -> p k n", k=16),
               AV(130 * KB, 16 * KB, BF16).rearrange("p (k n) -> p k n", k=16)]

        def project_gen(ntiles, hb, blocks, key_base, rope_base):
            hbuf = HTB[hb]
            hkey = 'hT%d' % hb
            pending = PENDING
            for cb in blocks:
                s = wb_next()
                wv = WB[s].rearrange("p (k n) -> p k n", k=16)
                dma('pool', wv, w_in[:, cb * 512:(cb + 1) * 512].rearrange("(k p) n -> p k n", p=128), 'wb%d' % s, writes=['wb%d' % s])
                for t in range(ntiles):
                    bank = CBANKS[pc_i[0] % 2]
                    pc_i[0] += 1
                    for kc in range(16):
                        mm(PS(bank), hbuf[:, kc, t * 128:(t + 1) * 128], wv[:, kc, :], kc == 0, kc == 15,
                           reads=['%s_%d_%d' % (hkey, t, kc), 'wb%d' % s], writes=['ps%d' % bank])
                    kchunk = key_base // 128 + t
                    while len(pending) > PEND_DEPTH - 1:
                        pending.pop(0)()
                    if cb < 3:
                        pending.append(qk_process(bank, True, True, rope_base + t * 128, None,
                                                  qT[:, cb * 4:(cb + 1) * 4, key_base + t * 128:key_base + (t + 1) * 128], 'qT'))
                    elif cb == 3:
                        pending.append(qk_process(bank, False, rope_base is not None, (rope_base + t * 128) if rope_base is not None else 0, None,
                                                  kT[:, :, key_base + t * 128:key_base + (t + 1) * 128], 'kT'))
                    elif cb == 4:
                        evac_copy(Vt[:, kchunk, :], PS(bank), reads=['ps%d' % bank], writes=['V'])
                    else:
                        evac_copy(Ut[:, kchunk, :], PS(bank), reads=['ps%d' % bank], writes=['U'])
                    yield

        groups = [
            (x_oth[0:512, :], 4, 0, 'gs0', 0, [3, 4, 5], 1024, 1024),
            (x_oth[512:1024, :], 4, 0, 'gs0', 0, [3, 4, 5], 1536, 1536),
            (ctx_in, 2, 1, 'gs1', 1, [3, 4], 2048, None),
            (x_own[0:512, :], 4, 0, 'gs0', 0, [3, 4, 5, 0, 1, 2], 0, 0),
            (x_own[512:1024, :], 4, 0, 'gs0', 0, [3, 4, 5, 0, 1, 2], 512, 512),
        ]

        def mk_norm(gi):
            src, nt, gsk, gskey, shr, blocks, kb, rb = groups[gi]
            return norm_gen(src, nt, gsk, gskey, 0, shr, HTB[gi % 2], 'hT%d' % (gi % 2), cfgB)

        def mk_norm_tile(gi, t):
            src, nt, gsk, gskey, shr, blocks, kb, rb = groups[gi]
            return norm_gen(src, nt, gsk, gskey, 0, shr, HTB[gi % 2], 'hT%d' % (gi % 2), cfgB, tiles=[t])

        n0 = groups[0][1]
        tg0 = [mk_norm_tile(0, t) for t in range(n0)]
        for i in range(n0 + 2):
            if i < n0:
                next(tg0[i])
            if 0 <= i - 1 < n0:
                next(tg0[i - 1])
            if 0 <= i - 2 < n0:
                next(tg0[i - 2])
        for gi, g in enumerate(groups):
            src, nt, gsk, gskey, shr, blocks, kb, rb = g
            pg = project_gen(nt, gi % 2, blocks, kb, rb)
            ng = mk_norm(gi + 1) if gi + 1 < len(groups) else None
            nunits = nt * len(blocks)
            nsteps = 3 * groups[gi + 1][1] if ng is not None else 0
            stride = max(1, nunits // (nsteps + 1)) if nsteps else 0
            for ui, _ in enumerate(pg):
                if ng is not None and ui >= 1 and (ui - 1) % stride == 0:
                    try:
                        next(ng)
                    except StopIteration:
                        ng = None
            if ng is not None:
                for _ in ng:
                    pass
        while PENDING:
            PENDING.pop(0)()

        if stop == 'C':
            dump('qT', qT.rearrange("p h n -> p (h n)"), ['qT'])
            dump('kT', kT.rearrange("p h n -> p (h n)"), ['kT'])
            dump('V', Vt.rearrange("p c n -> p (c n)"), ['V'])
            dump('U', Ut.rearrange("p c n -> p (c n)"), ['U'])
            return finish()

        f0 = fence(['hT%d_%d_%d' % (hb, t, kc) for hb in range(2) for t in range(4) for kc in range(16)] + ['junk', 'xn0', 'xn1', 'xt0', 'xt1']
                   + ['scr%d' % i for i in range(8)] + ['rope%d' % i for i in range(4)] + ['rope%ds' % i for i in range(4)])
        wcs = AV(178 * KB, 2 * KB, BF16).rearrange("p (g d) -> p g d", g=8)
        cc_t = SCR[3][:, 0:128]
        sc_t = SCR[3][:, 128:256]
        dma('sp', cc_t, cc_d[:, :], 'c_ccm', reads=[f0], writes=['ccm'])
        dma('sp', sc_t, sc_d[:, :], 'c_scm', reads=[f0], writes=['scm'])
        wf_t = SCR[4].rearrange("p (g d) -> p g d", g=4)
        dma('sp', wf_t, wf_d.rearrange("g c d -> c g d"), 'c_wf', reads=[f0], writes=['wfm'])
        for g in range(4):
            mm(PS(6)[:, g * 128:(g + 1) * 128], cc_t, wf_t[:, g, :], True, True, reads=['ccm', 'wfm'], writes=['ps6'])
        for g in range(4):
            mm(PS(7)[:, g * 128:(g + 1) * 128], sc_t, wf_t[:, g, :], True, True, reads=['scm', 'wfm'], writes=['ps7'])
        S.add('dve', lambda e: e.tensor_copy(out=wcs[:, 0:4, :], in_=PS(6).rearrange("p (g d) -> p g d", g=4)), reads=['ps6', f0], writes=['wcs'])
        S.add('dve', lambda e: e.tensor_copy(out=wcs[:, 4:8, :], in_=PS(7).rearrange("p (g d) -> p g d", g=4)), reads=['ps7', f0], writes=['wcsb'])

        ada_todo = list(range(8, 24))
        SCALE = float(HD) ** -0.5
        units = [(h, qg) for h in range(NQH) for qg in range(2)]
        steps = [(ui, ch) for ui in range(len(units)) for ch in range(18)]
        NPT = 4
        cur_ada = [None]
        SBANKS = [0, 1, 6]
        DACC = [AV(188 * KB, 2 * KB), AV(182 * KB, 2 * KB)]
        STPAD = AV(158 * KB, 4 * KB, BF16).rearrange("p (k m) -> p k m", k=16)
        S.add('dve', lambda e: e.memset(STPAD, 0.0), reads=[f0], writes=['stpad'])
        S.add('dve', lambda e: e.tensor_copy(out=STPAD[:, :, 0:2], in_=sT3), reads=['sT'], writes=['stpad'])

        def ada_tick():
            if cur_ada[0] is None:
                if not ada_todo:
                    return
                cur_ada[0] = ada_gen(ada_todo.pop(0), 7, 7, extra=[f0], padded=True)
            try:
                next(cur_ada[0])
            except StopIteration:
                cur_ada[0] = None

        def emit_S(k):
            ui, ch = steps[k]
            h, qg = units[ui]
            kh = h // 3
            sbank = SBANKS[k % 3]
            mm(PS(sbank), kT[:, kh, ch * 128:(ch + 1) * 128], qT[:, h, qg * 512:(qg + 1) * 512], True, True,
               reads=['kT', 'qT'], writes=['ps%d' % sbank])
            p = k % NPT
            S.add('act', lambda e, p=p, sbank=sbank: e.activation(out=PT[p], in_=PS(sbank), func=AF.Exp, scale=SCALE),
                  reads=['ps%d' % sbank, f0], writes=['pt%d' % p])

        emit_S(0)
        emit_S(1)
        emit_S(2)
        for k, (ui, ch) in enumerate(steps):
            h, qg = units[ui]
            kh = h // 3
            p = k % NPT
            ob = 2 + (ui % 2) * 2
            mm(PS(ob), Vt[:, ch, kh * 128:(kh + 1) * 128], PT[p], ch == 0, ch == 17, reads=['V', 'pt%d' % p], writes=['ps%d' % ob])
            da = ui % 2
            if ch % 2 == 1:
                mm(PS(ob + 1), ones_bf, PT[p], ch == 1, False, reads=['ones', 'pt%d' % p], writes=['ps%d' % (ob + 1)])
            elif ch == 0:
                wr_ = ['dacc%d' % da] + (['wfm', 'scm', 'ccm'] if ui < 2 else [])
                S.add('dve', lambda e, da=da, p=p: e.tensor_copy(out=DACC[da], in_=PT[p]), reads=['pt%d' % p, f0], writes=wr_)
            else:
                S.add('dve', lambda e, da=da, p=p: e.tensor_tensor(out=DACC[da], in0=DACC[da], in1=PT[p], op=ALU.add),
                      reads=['pt%d' % p, 'dacc%d' % da], writes=['dacc%d' % da])
            if ch == 17:
                mm(PS(ob + 1), ones_f, DACC[da], False, True, reads=['onesf', 'dacc%d' % da], writes=['ps%d' % (ob + 1)])
            if k + 3 < len(steps):
                emit_S(k + 3)
            ada_tick()
            if ch == 17:
                rd = ui % 2
                S.add('dve', lambda e, rd=rd, ob=ob: e.reciprocal(out=RDEN[rd], in_=PS(ob + 1)), reads=['ps%d' % (ob + 1), f0], writes=['rden%d' % rd])
                S.add('dve', lambda e, rd=rd, h=h, qg=qg, ob=ob: e.tensor_tensor(out=mixT[:, h, qg * 512:(qg + 1) * 512], in0=PS(ob), in1=RDEN[rd],
                                                                               op=ALU.mult),
                      reads=['ps%d' % ob, 'rden%d' % rd, f0], writes=['mixT_%d' % t_ for t_ in range(qg * 4, qg * 4 + 4)])
        while ada_todo or cur_ada[0] is not None:
            ada_tick()
        S.add('dve', lambda e: e.scalar_tensor_tensor(out=gs[:, 2, :], in0=modcol[:, 4, :, 0], scalar=1.0, in1=g2col,
                                                      op0=ALU.add, op1=ALU.mult),
              reads=['modcol4', 'g2col'], writes=['gs2'])

        ft_i = [0]
        for hq in range(2):
            for ch in range(16):
                fs = ft_i[0] % 6
                ft_i[0] += 1
                fk = 'ftab%d' % fs
                dma('pool', FTAB[fs][:, 0, :], cn_d[ch * 128:(ch + 1) * 128, hq * 512:(hq + 1) * 512], fk, reads=[f0], writes=[fk])
                dma('pool', FTAB[fs][:, 1, :], sn_d[ch * 128:(ch + 1) * 128, hq * 512:(hq + 1) * 512], fk + 's', reads=[f0], writes=[fk + 's'])
                for g in range(4):
                    mm(PS(g), Ut[:, ch, g * 128:(g + 1) * 128], FTAB[fs][:, 0, :], ch == 0, ch == 15,
                       reads=['U', fk], writes=['ps%d' % g])
                    mm(PS(4 + g), Ut[:, ch, g * 128:(g + 1) * 128], FTAB[fs][:, 1, :], ch == 0, ch == 15,
                       reads=['U', fk + 's'], writes=['ps%d' % (4 + g)])
            for i in range(8):
                evac_copy(ABT[:, i, :], PS(i), reads=['ps%d' % i, f0], writes=['abt%d' % i])
            for g in range(4):
                mm(PS(g), wcs[:, g, :], ABT[:, g, :], True, False, reads=['wcs', 'wcsb', 'abt%d' % g], writes=['ps%d' % g])
                mm(PS(g), wcs[:, 4 + g, :], ABT[:, 4 + g, :], False, True, reads=['wcs', 'wcsb', 'abt%d' % (4 + g)], writes=['ps%d' % g])
                evac_copy(mixT[:, 12 + g, hq * 512:(hq + 1) * 512], PS(g), reads=['ps%d' % g], writes=['mixT_%d' % t_ for t_ in range(hq * 4, hq * 4 + 4)])

        if stop == 'D':
            dump('mixT', mixT.rearrange("p k n -> p (k n)"), ['mixT_%d' % t_ for t_ in range(8)])
            dump('gt1', GT1, ['gt2_%d' % i for i in range(4)])
            return finish()

        w2_slot = {}
        W2C = [[AV(178 * KB + j * 4 * KB, 4 * KB, BF16) for j in range(4)],
               [AV(102 * KB + j * 4 * KB, 4 * KB, BF16) for j in range(3)] + [AV(162 * KB, 4 * KB, BF16)]]
        def load_w2(b, deps):
            s = b % 2
            w2_slot[b] = s
            r0 = b * 512
            if s == 0:
                o_ = AV(178 * KB, 16 * KB, BF16).rearrange("p (a n) -> p a n", a=4)
                dma('pool', o_, w2[r0:r0 + 512, :].rearrange("(a p) n -> p a n", p=128), 'w2s0', reads=list(deps), writes=['w2s0'] + ['w2g0_%d' % j for j in range(4)])
            else:
                o_ = AV(102 * KB, 12 * KB, BF16).rearrange("p (a n) -> p a n", a=3)

                def fn(e, o_=o_, r0=r0):
                    i1 = e.dma_start(out=o_, in_=w2[r0:r0 + 384, :].rearrange("(a p) n -> p a n", p=128))
                    i2 = e.dma_start(out=W2C[1][3], in_=w2[r0 + 384:r0 + 512, :])
                    return [i1, i2]
                S.add('pool', fn, reads=list(deps), writes=['w2s1'] + ['w2g1_%d' % j for j in range(4)], dma='w2s1', ndma=2)
            for j in range(4):
                S.add('pool', lambda e, s=s, j=j: e.tensor_tensor(out=W2C[s][j], in0=W2C[s][j], in1=GT2, op=ALU.mult),
                      reads=['w2s%d' % s] + ['gt5_%d' % i for i in range(4)], writes=['w2g%d_%d' % (s, j)])

        f1 = fence(['qT', 'kT', 'V', 'U', 'wcs', 'wcsb', 'ccm', 'scm', 'wfm', 'rden0', 'rden1', 'modrow0', 'modrow1', 'brow0', 'brow1']
                   + ['pt%d' % i for i in range(4)] + ['abt%d' % i for i in range(8)]
                   + ['ftab%d' % i for i in range(6)] + ['ftab%ds' % i for i in range(6)] + ['dacc0', 'dacc1', 'stpad'])
        for t in range(8):
            dma('sp', ACC[t], x_own[t * 128:(t + 1) * 128, :], 'acc%d' % t, reads=[f1], writes=ACCK(t))
        pe_i = [0]
        tm_i = [0]
        ng2 = {}
        TMPE = [AV(102 * KB + i * 2 * KB, 2 * KB) for i in range(4)]
        cfgF = dict(junk=AV(170 * KB, 4 * KB, BF16), xn=[AV(162 * KB, 4 * KB, BF16), AV(166 * KB, 4 * KB, BF16)], banks=[4, 5, 6, 7])
        wo_slot = {}

        def load_wo(cb):
            s_ = wb_next()
            wo_slot[cb] = s_
            dma('pool', WB[s_].rearrange("p (k n) -> p k n", k=16), w_out[:, cb * 512:(cb + 1) * 512].rearrange("(k p) n -> p k n", p=128),
                'wb%d' % s_, writes=['wb%d' % s_])

        load_wo(0)
        load_wo(1)
        for cb in range(4):
            s = wo_slot[cb]
            wv = WB[s].rearrange("p (k n) -> p k n", k=16)
            for t in range(8):
                bank = pe_i[0] % 3
                pe_i[0] += 1
                for mc in range(16):
                    mm(PS(bank), mixT[:, mc, t * 128:(t + 1) * 128], wv[:, mc, :], mc == 0, mc == 15,
                       reads=['mixT_%d' % t, 'wb%d' % s], writes=['ps%d' % bank])
                tm = tm_i[0] % 4
                tm_i[0] += 1
                S.add('dve', lambda e, tm=tm, bank=bank, cb=cb: e.tensor_tensor(out=TMPE[tm], in0=PS(bank), in1=GT1[:, cb * 512:(cb + 1) * 512],
                                                                                 op=ALU.mult),
                      reads=['ps%d' % bank, 'gt2_%d' % cb, f1], writes=['tmpE%d' % tm])
                S.add('dve', lambda e, tm=tm, t=t, cb=cb: e.tensor_tensor(out=ACC[t][:, cb * 512:(cb + 1) * 512],
                                                                          in0=ACC[t][:, cb * 512:(cb + 1) * 512], in1=TMPE[tm], op=ALU.add),
                      reads=['tmpE%d' % tm, 'acc%d_%d' % (t, cb)], writes=['acc%d_%d' % (t, cb)])
                if cb == 3:
                    ng2[t] = norm_gen(ACC, 8, 2, 'gs2', 3, 0, h2T, 'hT', cfgF, srckeys=[ACCK(t_) for t_ in range(8)],
                                      extra_reads=[f1], evac_extra_writes=lambda t_: ['mixT_%d' % t_], tiles=[t])
                    next(ng2[t])
                    if t >= 2:
                        next(ng2[t - 2])
                    next(ng2[t])
            if cb + 2 < 4:
                load_wo(cb + 2)
            if cb == 1:
                fmid = fence(['gt2_0', 'gt2_1'])
                load_w2(0, [f1, fmid])
        for t in (6, 7):
            next(ng2[t])

        if stop == 'E':
            for t in range(8):
                dump('acc%d' % t, ACC[t], ACCK(t))
            return finish()

        f2 = fence(['mixT_%d' % t_ for t_ in range(8)] + ['tmpE%d' % i for i in range(4)] + ['gt2_%d' % i for i in range(4)])
        if stop == 'F':
            dump('h2T', h2T.rearrange("p k n -> p (k n)"), ['hT_%d_%d' % (t, kc) for t in range(8) for kc in range(16)])
            return finish()

        f3 = fence(['junk', 'xn0', 'xn1'])
        NB = 16
        w1_slot = {}
        A2T = [AV(146 * KB, 8 * KB, BF16).rearrange("p (j n) -> p j n", j=4),
               AV(154 * KB, 8 * KB, BF16).rearrange("p (j n) -> p j n", j=4)]
        RSC = [AV(170 * KB, 2 * KB), AV(172 * KB, 2 * KB), AV(174 * KB, 2 * KB), AV(176 * KB, 2 * KB)]
        TMPG = [AV(194 * KB, 2 * KB), AV(196 * KB, 2 * KB)]

        def load_w1(b):
            s = wb_next()
            w1_slot[b] = s
            wv = WB[s].rearrange("p (k n) -> p k n", k=16)
            dma('pool', wv, w1[:, b * 512:(b + 1) * 512].rearrange("(k p) n -> p k n", p=128), 'wb%d' % s, writes=['wb%d' % s])

        am_i = [0]
        rs_i = [0]

        def stageA(b, j):
            s = w1_slot[b]
            wv = WB[s].rearrange("p (k n) -> p k n", k=16)
            bb = b % 2
            banks = [(am_i[0] % 2) * 2, (am_i[0] % 2) * 2 + 1]
            am_i[0] += 1
            for kc in range(16):
                for tg in range(2):
                    mm(PS(banks[tg]), wv[:, kc, j * 128:(j + 1) * 128], h2T[:, kc, tg * 512:(tg + 1) * 512], kc == 0, kc == 15,
                       reads=['wb%d' % s] + ['hT_%d_%d' % (tg * 4 + i, kc) for i in range(4)], writes=['ps%d' % banks[tg]])
            for tg in range(2):
                r = rs_i[0] % 4
                rs_i[0] += 1
                S.add('act', lambda e, r=r, bk=banks[tg]: e.activation(out=RSC[r], in_=PS(bk), func=AF.Relu),
                      reads=['ps%d' % banks[tg], f3], writes=['rsc%d' % r])
                S.add('act', lambda e, r=r, bb=bb, j=j, tg=tg: e.activation(out=A2T[bb][:, j, tg * 512:(tg + 1) * 512], in_=RSC[r], func=AF.Square),
                      reads=['rsc%d' % r, f3], writes=['a2t%d_%d' % (bb, j)])

        bn_i = [0]
        tg_i = [0]

        def stageB(b, t, cp):
            s = w2_slot[b]
            bb = b % 2
            banks = [4 + (bn_i[0] % 2) * 2, 5 + (bn_i[0] % 2) * 2]
            bn_i[0] += 1
            for j in range(4):
                for ci in range(2):
                    cb = cp * 2 + ci
                    mm(PS(banks[ci]), A2T[bb][:, j, t * 128:(t + 1) * 128], W2C[s][j][:, cb * 512:(cb + 1) * 512], j == 0, j == 3,
                       reads=['a2t%d_%d' % (bb, j), 'w2g%d_%d' % (s, j)], writes=['ps%d' % banks[ci]])
            for ci in range(2):
                cb = cp * 2 + ci
                S.add('dve', lambda e, bk=banks[ci], t=t, cb=cb: e.tensor_tensor(out=ACC[t][:, cb * 512:(cb + 1) * 512], in0=PS(bk),
                                                                                in1=ACC[t][:, cb * 512:(cb + 1) * 512], op=ALU.add),
                      reads=['ps%d' % banks[ci], 'acc%d_%d' % (t, cb), f2], writes=['acc%d_%d' % (t, cb)])

        ADD_ENG = 'pool'
        load_w1(0)
        load_w1(1)
        load_w2(1, [f1, f2, f3])
        for j in range(4):
            stageA(0, j)
        for b in range(NB):
            bunits = [(t, cp) for t in range(8) for cp in range(2)]
            aunits = list(range(4)) if b + 1 < NB else []
            for i, (t, cp) in enumerate(bunits):
                if i % 4 == 0 and aunits:
                    stageA(b + 1, aunits.pop(0))
                stageB(b, t, cp)
            if b + 2 < NB:
                load_w1(b + 2)
                load_w2(b + 2, [f1, f2, f3])

        if stop == 'G':
            for t in range(8):
                dump('acc%d' % t, ACC[t], ACCK(t))
            return finish()

        f4 = fence(['hT_%d_%d' % (t, kc) for t in range(8) for kc in range(16)] + ['wb0', 'wb1'])
        dma('sp', FGB, fg_d[:, :], 'c_fg', reads=[f4], writes=['fgb'])
        JH = AV(114 * KB, 4 * KB, BF16)
        for t in range(8):
            c = newstat(3)
            S.add('act', lambda e, t=t, c=c: e.activation(out=JH, in_=ACC[t], func=AF.Square, accum_out=stat[:, c:c + 1]),
                  reads=ACCK(t) + [f4], writes=['junkH', 'st%d' % c])
            S.add('act', lambda e, c=c: e.activation(out=stat[:, c + 1:c + 2], in_=stat[:, c:c + 1], func=AF.Ln, scale=1.0 / D, bias=EPS),
                  reads=['st%d' % c], writes=['st%d' % (c + 1)])
            S.add('act', lambda e, c=c: e.activation(out=stat[:, c + 2:c + 3], in_=stat[:, c + 1:c + 2], func=AF.Exp, scale=-0.5),
                  reads=['st%d' % (c + 1)], writes=['st%d' % (c + 2)])
            S.add('dve', lambda e, t=t, c=c: e.scalar_tensor_tensor(out=ACC[t], in0=ACC[t], scalar=stat[:, c + 2:c + 3], in1=FGB,
                                                                   op0=ALU.mult, op1=ALU.mult),
                  reads=ACCK(t) + ['st%d' % (c + 2), 'fgb'], writes=ACCK(t))
            dma('sp', out_d[t * 128:(t + 1) * 128, :], ACC[t], 'out%d' % t, reads=ACCK(t), writes=['outd%d' % t])
        S.add('sp', None, reads=['outd%d' % t for t in range(8)] + ['dbgo_' + n for n in dbg_out])
        S.emit(nc, st)
    return nc


_CONST_CACHE = {}


def _consts():
    if _CONST_CACHE:
        return _CONST_CACHE
    f = np.float32
    t = np.arange(SEQ)
    row = (t // 64).astype(np.float64)
    col = (t % 64).astype(np.float64)
    inv = 10000.0 ** (-np.arange(0, 64, 2, dtype=np.float64) / 64.0)
    ar = row[:, None] * inv[None, :]
    ac = col[:, None] * inv[None, :]
    cos = np.concatenate([np.cos(ar), np.cos(ar), np.cos(ac), np.cos(ac)], axis=1).astype(f)
    sin = np.concatenate([-np.sin(ar), np.sin(ar), -np.sin(ac), np.sin(ac)], axis=1).astype(f)
    n = np.arange(SEQ, dtype=np.int64)
    ang = 2.0 * np.pi * ((n[:, None] * n[None, :]) % SEQ).astype(np.float64) / SEQ
    cn = np.cos(ang).astype(f)
    sn = np.sin(ang).astype(f)
    c = np.arange(128, dtype=np.int64)
    angc = 2.0 * np.pi * ((c[:, None] * c[None, :]) % 128).astype(np.float64) / 128
    cc = (np.cos(angc) / 512.0).astype(f)
    sc = (-np.sin(angc) / 512.0).astype(f)
    sel3 = np.zeros((3, 128), f)
    sel3[0] = 1.0
    sel3[2] = 1.0
    r3 = np.array([[1, 0], [0, 1], [1, 1]], f)
    _CONST_CACHE.update(cos=cos, sin=sin, cn=cn, sn=sn, cc=cc, sc=sc, sel3=sel3, r3=r3, ident=np.eye(128, dtype=f))
    return _CONST_CACHE


_PROG = {}


def _in_maps(x, c, ctx, c_ctx, w_ada, b_ada, norm1_g, w_in, q_norm_g, k_norm_g, w_fourier,
             w_out, norm2_g, w_mlp1, w_mlp2, final_norm_g):
    K = _consts()
    f = np.float32
    A = lambda a: np.ascontiguousarray(np.asarray(a, dtype=f))
    x, c, ctx, c_ctx = A(x), A(c), A(ctx), A(c_ctx)
    shared = dict(
        w_ada=A(w_ada[0]), b_ada=A(b_ada[0]).reshape(1, -1),
        g1col=A(np.asarray(norm1_g[0]).reshape(16, 128).T), g2col=A(np.asarray(norm2_g[0]).reshape(16, 128).T),
        w_in=A(w_in[0]), qg_bc=A(np.broadcast_to(np.asarray(q_norm_g[0])[None, :], (128, 128))),
        kg_bc=A(np.broadcast_to(np.asarray(k_norm_g[0])[None, :], (128, 128))),
        w_f=A(w_fourier[0]), w_out=A(w_out[0]), w1=A(w_mlp1[0]), w2=A(w_mlp2[0]),
        fg_bc=A(np.broadcast_to(np.asarray(final_norm_g)[None, :], (128, D))),
        ident=K['ident'], sel3=K['sel3'], r3=K['r3'], dft_cc=K['cc'], dft_sc=K['sc'],
    )
    maps = []
    for core in range(8):
        b, half = core // 2, core % 2
        own = slice(half * NOWN, (half + 1) * NOWN)
        oth = slice((1 - half) * NOWN, (2 - half) * NOWN)
        order = np.concatenate([np.arange(own.start, own.stop), np.arange(oth.start, oth.stop)])
        cpair = np.stack([c[b], c_ctx], axis=0)
        ccol = A(cpair.reshape(2, 16, 128).transpose(2, 1, 0).reshape(128, 32))
        m = dict(shared)
        m.update(
            x_own=A(x[b, own]), x_oth=A(x[b, oth]), ctx_b=A(ctx[b]), ccol=ccol,
            rope_cos=A(K['cos'][order]), rope_sin=A(K['sin'][order]),
            dft_cn=A(K['cn'][order][:, own]), dft_sn=A(K['sn'][order][:, own]),
        )
        maps.append(m)
    return maps


def kernel(**inputs):
    if 'nc' not in _PROG:
        _PROG['nc'] = build_program()
    nc = _PROG['nc']
    maps = _in_maps(**inputs)
    res = run_bass_kernel_spmd(nc, maps, core_ids=list(range(8)))
    out = np.zeros((BATCH, SEQ, D), np.float32)
    for core in range(8):
        b, half = core // 2, core % 2
        out[b, half * NOWN:(half + 1) * NOWN] = res.results[core]["out"]
    return out
```
